# Optimizing a Trainium2 kernel written in Bass

```python
import jax, jax.numpy as jnp
from jax import lax
import numpy as np

D_MODEL = 2048
BATCH = 8
SEQ = 2048
DEPTH = 1

N_META = 16
HEAD_DIM = 64
MIX_WIDTH = D_MODEL
RWKV_WIDTH = MIX_WIDTH // 2
SB_WIDTH = MIX_WIDTH - RWKV_WIDTH
RWKV_HEADS = RWKV_WIDTH // HEAD_DIM
SB_HEADS = SB_WIDTH // HEAD_DIM
DECAY_LORA = 64
ICLR_LORA = 64
GATE_LORA = 160
RWKV_COLS = 3 * RWKV_WIDTH + DECAY_LORA + ICLR_LORA + GATE_LORA
SB_COLS = 3 * SB_WIDTH
IN_COLS = RWKV_COLS + SB_COLS
D_FF = ((8 * D_MODEL // 3 + 255) // 256) * 256
CONV_WIDTH = 3
SB_BLOCK = 128
DEEPNORM_ALPHA = (2.0 * DEPTH) ** 0.25
DEEPNORM_BETA = (8.0 * DEPTH) ** -0.25
LN_EPS = 1e-5
GN_EPS = 64e-5
RMS_EPS = 1e-6

kernel_name = "hymba_rwkv7_stickbreak_deepnorm_convffn"


def layer_norm(x, g, b):
    xf = x.astype(jnp.float32)
    mu = jnp.mean(xf, axis=-1, keepdims=True)
    var = jnp.mean(jnp.square(xf - mu), axis=-1, keepdims=True)
    return ((xf - mu) * lax.rsqrt(var + LN_EPS) * g + b).astype(x.dtype)


def rwkv7_mixer(p, mu, w0, w2, a0, a2, g2, k_k, k_a, r_k, gn_g, gn_b):
    B, T, _ = p.shape
    C = RWKV_WIDTH
    f32 = jnp.float32
    p_prev = jnp.pad(p, ((0, 0), (1, 0), (0, 0)))[:, :-1]
    p = p + (p_prev - p) * mu
    r, k, v, wl, al, gl = jnp.split(
        p, [C, 2 * C, 3 * C, 3 * C + DECAY_LORA, 3 * C + DECAY_LORA + ICLR_LORA], axis=-1)
    w_log = -jax.nn.softplus(-(w0 + jnp.tanh(wl) @ w2)) - 0.5
    a = jax.nn.sigmoid(a0 + al @ a2)
    g = jax.nn.sigmoid(gl) @ g2
    heads = lambda t: t.reshape(B, T, RWKV_HEADS, HEAD_DIM).astype(f32)
    kk = heads(k * k_k)
    kk = kk / jnp.maximum(jnp.linalg.norm(kk, axis=-1, keepdims=True), 1e-12)
    k = k * (1.0 + (a - 1.0) * k_a)
    r_h, k_h, v_h, a_h = heads(r), heads(k), heads(v), heads(a)
    decay = jnp.exp(-jnp.exp(heads(w_log)))

    def step(S, inp):
        r_t, dec_t, k_t, v_t, kk_t, a_t = inp
        sa = jnp.einsum('bhvk,bhk->bhv', S, -kk_t)
        S = (S * dec_t[:, :, None, :] + sa[..., None] * (kk_t * a_t)[:, :, None, :]
             + v_t[..., None] * k_t[:, :, None, :])
        return S, jnp.einsum('bhvk,bhk->bhv', S, r_t)

    xs = tuple(jnp.moveaxis(t, 1, 0) for t in (r_h, decay, k_h, v_h, kk, a_h))
    S0 = jnp.zeros((B, RWKV_HEADS, HEAD_DIM, HEAD_DIM), f32)
    _, y = lax.scan(step, S0, xs)
    y = jnp.moveaxis(y, 0, 1)
    mean = jnp.mean(y, axis=-1, keepdims=True)
    var = jnp.mean(jnp.square(y - mean), axis=-1, keepdims=True)
    y = ((y - mean) * lax.rsqrt(var + GN_EPS)).reshape(B, T, C) * gn_g + gn_b
    bonus = (jnp.sum(r_h * k_h * r_k, axis=-1, keepdims=True) * v_h).reshape(B, T, C)
    return ((y + bonus) * g).astype(p.dtype)


def stick_breaking_mixer(p, norm_g):
    B, T, _ = p.shape
    f32 = jnp.float32
    to_heads = lambda t: t.reshape(B, T, SB_HEADS, HEAD_DIM).transpose(0, 2, 1, 3)
    q, k, v = (to_heads(t) for t in jnp.split(p, 3, axis=-1))
    scale = HEAD_DIM ** -0.5
    bounds = [0, N_META] + [N_META + SB_BLOCK * (i + 1) for i in range((T - N_META) // SB_BLOCK)]
    outs = []
    for qs, qe in zip(bounds[:-1], bounds[1:]):
        z = jnp.einsum('bhqd,bhkd->bhqk', q[:, :, qs:qe], k[:, :, :qe]).astype(f32) * scale
        strict = jnp.arange(qe)[None, :] < jnp.arange(qs, qe)[:, None]
        log_beta = jax.nn.log_sigmoid(z)
        log_keep = jnp.where(strict, jax.nn.log_sigmoid(-z), 0.0)
        after = jnp.sum(log_keep, axis=-1, keepdims=True) - jnp.cumsum(log_keep, axis=-1)
        att = jnp.where(strict, jnp.exp(log_beta + after), 0.0)
        outs.append(jnp.einsum('bhqk,bhkd->bhqd', att, v[:, :, :qe].astype(f32)))
    o = jnp.concatenate(outs, axis=2).transpose(0, 2, 1, 3)
    o = o * lax.rsqrt(jnp.mean(jnp.square(o), axis=-1, keepdims=True) + RMS_EPS)
    return (o.reshape(B, T, SB_WIDTH) * norm_g).astype(p.dtype)


def conv_ffn(h, w_up, conv_w, conv_b, w_down):
    gate, val = jnp.split(h @ w_up, 2, axis=-1)
    gate = lax.conv_general_dilated(
        gate, conv_w[:, None, :], window_strides=(1,), padding=[(CONV_WIDTH - 1, 0)],
        dimension_numbers=('NWC', 'WIO', 'NWC'), feature_group_count=D_FF) + conv_b
    return (jax.nn.silu(gate) * val) @ w_down


def setup_inputs(seed: int = 0) -> dict:
    key = jax.random.key(seed)
    ks = jax.random.split(key, 32)
    f32 = jnp.float32
    nrm = lambda k, shape, s: jax.random.normal(k, shape, f32) * s
    L = DEPTH
    return {
        "x": nrm(ks[0], (BATCH, SEQ, D_MODEL), 1.0),
        "meta_tokens": nrm(ks[1], (N_META, D_MODEL), 1.0),
        "emb_ln_g": 1.0 + nrm(ks[2], (D_MODEL,), 0.02),
        "emb_ln_b": nrm(ks[3], (D_MODEL,), 0.02),
        "w_in": nrm(ks[4], (L, D_MODEL, IN_COLS), D_MODEL ** -0.5),
        "rwkv_mu": jax.random.uniform(ks[5], (L, RWKV_COLS), f32, 0.0, 1.0),
        "rwkv_w0": jax.random.uniform(ks[6], (L, RWKV_WIDTH), f32, -6.0, -1.0),
        "rwkv_w2": nrm(ks[7], (L, DECAY_LORA, RWKV_WIDTH), 0.1 * DECAY_LORA ** -0.5),
        "rwkv_a0": nrm(ks[8], (L, RWKV_WIDTH), 0.5),
        "rwkv_a2": nrm(ks[9], (L, ICLR_LORA, RWKV_WIDTH), ICLR_LORA ** -0.5),
        "rwkv_g2": nrm(ks[10], (L, GATE_LORA, RWKV_WIDTH), GATE_LORA ** -0.5),
        "rwkv_k_k": 0.85 + nrm(ks[11], (L, RWKV_WIDTH), 0.02),
        "rwkv_k_a": 1.0 + nrm(ks[12], (L, RWKV_WIDTH), 0.02),
        "rwkv_r_k": nrm(ks[13], (L, RWKV_HEADS, HEAD_DIM), 0.1),
        "rwkv_gn_g": 1.0 + nrm(ks[14], (L, RWKV_WIDTH), 0.02),
        "rwkv_gn_b": nrm(ks[15], (L, RWKV_WIDTH), 0.02),
        "sb_norm_g": 1.0 + nrm(ks[16], (L, SB_WIDTH), 0.02),
        "w_out": nrm(ks[17], (L, MIX_WIDTH, D_MODEL), MIX_WIDTH ** -0.5 * DEEPNORM_BETA),
        "ln1_g": 1.0 + nrm(ks[18], (L, D_MODEL), 0.02),
        "ln1_b": nrm(ks[19], (L, D_MODEL), 0.02),
        "ffn_w_up": nrm(ks[20], (L, D_MODEL, 2 * D_FF), D_MODEL ** -0.5),
        "ffn_conv_w": nrm(ks[21], (L, CONV_WIDTH, D_FF), CONV_WIDTH ** -0.5),
        "ffn_conv_b": nrm(ks[22], (L, D_FF), 0.02),
        "ffn_w_down": nrm(ks[23], (L, D_FF, D_MODEL), D_FF ** -0.5 * DEEPNORM_BETA),
        "ln2_g": 1.0 + nrm(ks[24], (L, D_MODEL), 0.02),
        "ln2_b": nrm(ks[25], (L, D_MODEL), 0.02),
    }


def reference(x, meta_tokens, emb_ln_g, emb_ln_b, w_in, rwkv_mu, rwkv_w0, rwkv_w2, rwkv_a0,
              rwkv_a2, rwkv_g2, rwkv_k_k, rwkv_k_a, rwkv_r_k, rwkv_gn_g, rwkv_gn_b, sb_norm_g,
              w_out, ln1_g, ln1_b, ffn_w_up, ffn_conv_w, ffn_conv_b, ffn_w_down, ln2_g, ln2_b):
    B = x.shape[0]
    meta = jnp.broadcast_to(meta_tokens[None].astype(x.dtype), (B, N_META, x.shape[-1]))
    h = layer_norm(jnp.concatenate([meta, x], axis=1), emb_ln_g, emb_ln_b)
    for l in range(DEPTH):
        p = h @ w_in[l]
        y_rwkv = rwkv7_mixer(p[..., :RWKV_COLS], rwkv_mu[l], rwkv_w0[l], rwkv_w2[l], rwkv_a0[l],
                             rwkv_a2[l], rwkv_g2[l], rwkv_k_k[l], rwkv_k_a[l], rwkv_r_k[l],
                             rwkv_gn_g[l], rwkv_gn_b[l])
        y_sb = stick_breaking_mixer(p[..., RWKV_COLS:], sb_norm_g[l])
        mix = jnp.concatenate([y_rwkv, y_sb], axis=-1) @ w_out[l]
        h = layer_norm(DEEPNORM_ALPHA * h + mix, ln1_g[l], ln1_b[l])
        ffn = conv_ffn(h, ffn_w_up[l], ffn_conv_w[l], ffn_conv_b[l], ffn_w_down[l])
        h = layer_norm(DEEPNORM_ALPHA * h + ffn, ln2_g[l], ln2_b[l])
    return h[:, N_META:]
```

```python
import numpy as np
from contextlib import ExitStack
import concourse.bass as bass
import concourse.mybir as mybir
from concourse.bass_utils import run_bass_kernel_spmd

F32 = mybir.dt.float32
BF16 = mybir.dt.bfloat16
AF = mybir.ActivationFunctionType
ALU = mybir.AluOpType
AX = mybir.AxisListType

D = 2048
T = 2064
NMETA = 16
DFF = 5632
NCORES = 8
ALPHA = 2.0 ** 0.25
TT = [(0, 16)] + [(16 + 128 * i, 128) for i in range(16)]
TG = [(0, 16)] + [(16 + 512 * i, 512) for i in range(4)]
CH = [(128 * i, 128) for i in range(24)] + [(3072, 128), (3200, 128), (3328, 32)] + \
     [(3360 + 128 * i, 128) for i in range(24)]
NCH = len(CH)

ENGS = ["pe", "act", "dve", "pool", "sp"]


class Prog:
    def __init__(self, nc):
        self.nc = nc
        self.ops = {e: [] for e in ENGS}
        self.lastw = {}
        self.readers = {}
        self.known = {e: {} for e in ENGS}
        self.chan_cnt = {}
        self.chan_order = []
        self.pending = {e: [] for e in ENGS}

    def _deps(self, eng, reads, writes):
        idx = len(self.ops[eng])
        deps = {}

        def add(tok, raw):
            src, i = tok
            if src == eng:
                if eng in ("pe", "sp"):
                    return
            if self.known[eng].get(src, -1) >= i:
                return
            if deps.get(src, -1) < i:
                deps[src] = i

        for tok in self.pending[eng]:
            if tok[0] != eng:
                add(tok, True)
        self.pending[eng] = []
        for r in reads:
            w = self.lastw.get(r)
            if w is not None:
                add(w, True)
        for r in writes:
            w = self.lastw.get(r)
            if w is not None:
                add(w, False)
            for tok in self.readers.get(r, {}).items():
                add(tok, False)
        for src, i in deps.items():
            self.known[eng][src] = i
            if not src.startswith("dma:"):
                self.ops[src][i]["signal"] = True
        return idx, list(deps.items())

    def op(self, eng, fn, reads=(), writes=()):
        idx, deps = self._deps(eng, reads, writes)
        self.ops[eng].append(dict(fn=fn, deps=deps, signal=False, chan=None))
        tok = (eng, idx)
        for r in reads:
            self.readers.setdefault(r, {})[eng] = idx
        for r in writes:
            self.lastw[r] = tok
            self.readers[r] = {}
        return tok

    def dma(self, queue, chan, fn, reads=(), writes=()):
        idx, deps = self._deps(queue, reads, writes)
        if chan not in self.chan_cnt:
            self.chan_cnt[chan] = 0
            self.chan_order.append(chan)
        self.chan_cnt[chan] += 1
        cnt = self.chan_cnt[chan]
        self.ops[queue].append(dict(fn=fn, deps=deps, signal=False, chan=chan))
        tok = ("dma:" + chan, cnt)
        for r in reads:
            self.readers.setdefault(r, {})["dma:" + chan] = cnt
        for r in writes:
            self.lastw[r] = tok
            self.readers[r] = {}
        return tok

    def barrier(self, exclude=()):
        toks = []
        for e in ["pe", "act", "dve", "pool"]:
            for i in range(len(self.ops[e]) - 1, -1, -1):
                if self.ops[e][i]["chan"] is None:
                    self.ops[e][i]["signal"] = True
                    toks.append((e, i))
                    break
        for c in self.chan_order:
            if c not in exclude:
                toks.append(("dma:" + c, self.chan_cnt[c]))
        for e in ENGS:
            self.pending[e] = list(toks)

    def emit(self, final_wait_chans=()):
        nc = self.nc
        with ExitStack() as es:
            sems = {}
            for e in ["pe", "act", "dve", "pool"]:
                sems[e] = es.enter_context(nc.semaphore("s_" + e))
            for c in self.chan_order:
                sems["dma:" + c] = es.enter_context(nc.semaphore("d_" + c))
            cnts = {}
            for e in ["pe", "act", "dve", "pool"]:
                c = 0
                arr = []
                for o in self.ops[e]:
                    if o["signal"] and o["chan"] is None:
                        c += 1
                    arr.append(c)
                cnts[e] = arr

            def run(ename, eng):
                for o in self.ops[ename]:
                    for src, i in o["deps"]:
                        if src.startswith("dma:"):
                            eng.wait_ge(sems[src], 16 * i)
                        else:
                            eng.wait_ge(sems[src], cnts[src][i])
                    ins = o["fn"](eng)
                    if o["chan"] is not None:
                        ins.then_inc(sems["dma:" + o["chan"]], 16)
                    elif o["signal"]:
                        ins.then_inc(sems[ename], 1)
                if ename == "sp":
                    for c in final_wait_chans:
                        eng.wait_ge(sems["dma:" + c], 16 * self.chan_cnt[c])

            block = es.enter_context(nc.Block())

            @block.tensor
            def _(e):
                run("pe", e)

            @block.scalar
            def _(e):
                run("act", e)

            @block.vector
            def _(e):
                run("dve", e)

            @block.gpsimd
            def _(e):
                run("pool", e)

            @block.sync
            def _(e):
                run("sp", e)


def _cst_layout():
    off = {}
    c = 0
    for name, n in [("emb_g", 16), ("emb_b", 16), ("ln1_g", 16), ("ln1_b", 16), ("ln2_g", 16), ("ln2_b", 16),
                    ("mu", 27), ("w0", 8), ("a0", 8), ("k_k", 8), ("k_a", 8), ("r_k", 8), ("gn_g", 8),
                    ("gn_b", 8), ("sb_g", 8), ("cw0", 44), ("cw1", 44), ("cw2", 44), ("cb", 44)]:
        off[name] = c
        c += n
    return off, c


CST, NCST = _cst_layout()


def build_nc(dbg=None):
    dbg = dbg or {}
    phases = dbg.get("phases", "0ABCDE")
    taps = dbg.get("taps", [])
    inject = dbg.get("inject", [])
    nc = bass.Bass("TRN2", target_bir_lowering=False)
    P = Prog(nc)

    used_inputs = []

    def din(name, shape, dt=F32, need="0ABCDE"):
        if not any(p in phases for p in need):
            return None
        used_inputs.append(name)
        return nc.dram_tensor(name, list(shape), dt, kind="ExternalInput").ap()

    def dscratch(name, shape, dt):
        if name in inject:
            used_inputs.append(name)
            return nc.dram_tensor(name, list(shape), dt, kind="ExternalInput").ap()
        if name in taps:
            return nc.dram_tensor(name, list(shape), dt, kind="ExternalOutput").ap()
        return nc.dram_tensor(name, list(shape), dt, kind="Internal").ap()

    xcat = din("xcat", [T, D], need="0")
    cst_d = din("cst", [128, NCST])
    bc_d = din("bc", [6, 128, D])
    w_in = din("w_in", [D, 6432], need="A")
    w2a2_d = din("w2a2", [128, 1024], need="B")
    g2_d = din("g2", [160, 1024], need="B")
    w_out = din("w_out", [D, D], need="D")
    w_up = din("w_up", [D, 2 * DFF], need="E")
    w_dn = din("w_dn", [DFF, D], need="E")
    out_d = nc.dram_tensor("out", [T - NMETA, D], F32, kind="ExternalOutput").ap()

    H0 = dscratch("H0", [T, D], F32)
    PT = dscratch("PT", [NCH * 128, T], F32)
    YTD = dscratch("YTD", [D, T], BF16)
    H1 = dscratch("H1", [T, D], F32)
    H1T = dscratch("H1T", [D, T], BF16)
    WUB = dscratch("WUB", [11, 128, 16, 1024], BF16)
    WDB = dscratch("WDB", [11, 128, 4, D], BF16)

    es = ExitStack()
    with es:
        def sb(name, shape, dt, st=es):
            return st.enter_context(nc.sbuf_tensor(name, list(shape), dt))

        ps = [es.enter_context(nc.psum_tensor("ps%d" % i, [128, 512], F32)) for i in range(8)]
        PSK = ["ps%d" % i for i in range(8)]

        cst = sb("cst_t", [128, NCST], F32)
        ident = sb("ident", [128, 128], F32)
        P.dma("sp", "cst", lambda e: e.dma_start(out=cst[:], in_=cst_d), writes=["cst"])
        P.op("pool", lambda e: e.memset(ident[:], 0.0), writes=["ident"])
        P.op("pool", lambda e: e.affine_select(out=ident[:], in_=ident[:], pattern=[[-1, 128]],
                                               compare_op=ALU.not_equal, fill=1.0, base=0, channel_multiplier=1),
             reads=["ident"], writes=["ident"])
        epsb = sb("epsb", [128, 4], F32)
        for j, v in enumerate([1e-5, 64e-5, 1e-6, 1.0]):
            P.op("pool", lambda e, j=j, v=v: e.memset(epsb[:, j:j + 1], v), writes=["epsb"])

        if "E" in phases:
            for fb in range(11):
                for half, c0 in enumerate([fb * 512, DFF + fb * 512]):
                    src = w_up[:, c0:c0 + 512].rearrange("(k p) n -> p k n", p=128)
                    P.dma("pool", "cv", lambda e, fb=fb, half=half, src=src: e.dma_start(out=WUB[fb, :, :, half * 512:(half + 1) * 512], in_=src),
                          writes=["WUB%d" % fb])
                srcd = w_dn[fb * 512:(fb + 1) * 512, :].rearrange("(k p) n -> p k n", p=128)
                P.dma("pool", "cv", lambda e, fb=fb, srcd=srcd: e.dma_start(out=WDB[fb], in_=srcd), writes=["WDB%d" % fb])

        def cc(name, j=0, rows=128):
            o = CST[name] + j
            return cst[0:rows, o:o + 1]

        rr = {"n": 0}

        def evq():
            rr["n"] += 1
            if dbg.get("evq"):
                return dbg["evq"]
            return "act" if rr["n"] % 2 else "dve"

        def copy_ps(eng, out, in_, reads, writes):
            if eng == "act":
                P.op("act", lambda e: e.activation(out=out, in_=in_, func=AF.Copy), reads=reads, writes=writes)
            else:
                P.op(eng, lambda e: e.tensor_copy(out=out, in_=in_), reads=reads, writes=writes)

        def wload(queue_chan, dst, src_rows_ap, nk, ncols, keys):
            for k0 in range(0, nk, 4):
                k1 = min(nk, k0 + 4)
                src = src_rows_ap[k0 * 128:k1 * 128, :].rearrange("(k p) n -> p k n", p=128)
                P.dma("pool", queue_chan, lambda e, k0=k0, k1=k1, src=src: e.dma_start(out=dst[:, k0:k1, 0:ncols], in_=src),
                      writes=keys)

        def layer_norm_tile(pref, xt, rows, st, mv, rstd, reads):
            for i in range(4):
                P.op("dve", lambda e, i=i: e.bn_stats(out=st[0:rows, i, :], in_=xt[0:rows, i * 512:(i + 1) * 512]),
                     reads=reads, writes=[pref + "st%d" % i])
            P.op("dve", lambda e: e.bn_aggr(out=mv[0:rows, :], in_=st[0:rows].rearrange("p a b -> p (a b)")),
                 reads=[pref + "st%d" % i for i in range(4)], writes=[pref + "mv"])
            P.op("act", lambda e: e.activation(out=rstd[0:rows, :], in_=mv[0:rows, 1:2], func=AF.Sqrt,
                                               bias=epsb[0:rows, 0:1], scale=1.0),
                 reads=[pref + "mv", "epsb"], writes=[pref + "rstd"])
            P.op("dve", lambda e: e.reciprocal(out=rstd[0:rows, :], in_=rstd[0:rows, :]),
                 reads=[pref + "rstd"], writes=[pref + "rstd"])
            P.op("dve", lambda e: e.tensor_scalar(out=xt[0:rows, :], in0=xt[0:rows, :], scalar1=mv[0:rows, 0:1],
                                                  scalar2=rstd[0:rows, :], op0=ALU.subtract, op1=ALU.mult),
                 reads=reads + [pref + "mv", pref + "rstd"], writes=reads)

        if "0" in phases or "A" in phases:
            with ExitStack() as s1:
                h0T = sb("h0T", [128, 16, T], BF16, s1)
                with ExitStack() as s0:
                  if "0" in phases:
                      gB = sb("gB", [128, D], F32, s0)
                      bB = sb("bB", [128, D], F32, s0)
                      P.dma("sp", "bc0", lambda e: e.dma_start(out=gB[:], in_=bc_d[0]), writes=["gB"])
                      P.dma("sp", "bc1", lambda e: e.dma_start(out=bB[:], in_=bc_d[1]), writes=["bB"])
                      xts = [sb("xt%d" % i, [128, D], F32, s0) for i in range(2)]
                      st = sb("st", [128, 4, 6], F32, s0)
                      mv = sb("mv", [128, 2], F32, s0)
                      rstd = sb("rstd", [128, 1], F32, s0)
                      for ti, (t0, n) in enumerate(TT[dbg.get('tile0', 0):dbg.get('ntiles', 17)]):
                          xt = xts[ti % 2]
                          xk = "xt%d" % (ti % 2)
                          P.dma("sp", "x%d" % (ti % 2), lambda e, xt=xt, t0=t0, n=n: e.dma_start(out=xt[0:n, :], in_=xcat[t0:t0 + n, :]),
                                writes=[xk])
                          layer_norm_tile("p0", xt, n, st, mv, rstd, [xk])
                          P.op(dbg.get("multeng", "pool"), lambda e, xt=xt, n=n: e.tensor_tensor(out=xt[0:n, :], in0=xt[0:n, :], in1=gB[0:n, :], op=ALU.mult),
                               reads=[xk, "gB"], writes=[xk])
                          P.op("dve", lambda e, xt=xt, n=n: e.tensor_tensor(out=xt[0:n, :], in0=xt[0:n, :], in1=bB[0:n, :], op=ALU.add),
                               reads=[xk, "bB"], writes=[xk])
                          if not dbg.get("nostore"):
                              P.dma("sp", "h0st%d" % (ti % 2), lambda e, xt=xt, t0=t0, n=n: e.dma_start(out=H0[t0:t0 + n, :], in_=xt[0:n, :]),
                                    reads=[xk], writes=["H0_%d" % ti])
                          if dbg.get("notr"):
                              continue
                          for gi in range(4):
                              bank = gi % 2
                              for j in range(4):
                                  c = gi * 4 + j
                                  P.op("pe", lambda e, xt=xt, n=n, c=c, j=j, bank=bank: e.transpose(
                                      out=ps[bank][:, j * 128:j * 128 + n], in_=xt[0:n, c * 128:(c + 1) * 128], identity=ident[0:n, 0:n]),
                                      reads=[xk, "ident"], writes=[PSK[bank]])
                              copy_ps(evq(), h0T[:, gi * 4:gi * 4 + 4, t0:t0 + n],
                                      ps[bank][:, :].rearrange("p (j m) -> p j m", m=128)[:, :, 0:n], [PSK[bank]],
                                      ["h0T%d" % (gi * 4 + j) for j in range(4)])
                P.barrier(exclude=("cv",))
                with ExitStack() as sa:
                  if "A" in phases:
                      wb = [sb("winb%d" % i, [128, 16, 256], BF16, sa) for i in range(2)]
                      stg = [sb("stg%d" % i, [128, T + 1], F32, sa) for i in range(2)]
                      stm = [sb("stm%d" % i, [128, T], F32, sa) for i in range(2)]
                      omm = sb("omm", [128, 27], F32, sa)
                      P.op("dve", lambda e: e.tensor_scalar(out=omm[:], in0=cst[:, CST["mu"]:CST["mu"] + 27], scalar1=-1.0, scalar2=1.0,
                                                            op0=ALU.mult, op1=ALU.add), reads=["cst"], writes=["omm"])
                      for i in range(2):
                          P.op("pool", lambda e, i=i: e.memset(stg[i][:, 0:1], 0.0), writes=["stg%d" % i])
                      groups = []
                      j = 0
                      while j < NCH:
                          if j + 1 < NCH and CH[j][1] == 128 and CH[j + 1][1] == 128 and CH[j + 1][0] == CH[j][0] + 128:
                              groups.append([j, j + 1])
                              j += 2
                          else:
                              groups.append([j])
                              j += 1
                      bi = 0
                      for gi, grp in enumerate(groups):
                          wbuf = wb[gi % 2]
                          wk = "winb%d" % (gi % 2)
                          c0 = CH[grp[0]][0]
                          ncols = sum(CH[j][1] for j in grp)
                          wload("win%d" % (gi % 2), wbuf, w_in[:, c0:c0 + ncols], 16, ncols, [wk])
                          for jj, j in enumerate(grp):
                              wd = CH[j][1]
                              so = stg[j % 2]
                              sk = "stg%d" % (j % 2)
                              sks = [sk + "_%d" % q for q in range(len(TG))]
                              for q, (t0, n) in enumerate(TG):
                                  bank = bi % 4
                                  bi += 1
                                  for k in range(16):
                                      P.op("pe", lambda e, wbuf=wbuf, k=k, jj=jj, wd=wd, t0=t0, n=n, bank=bank: e.matmul(
                                          ps[bank][0:wd, 0:n], lhsT=wbuf[:, k, jj * 128:jj * 128 + wd], rhs=h0T[:, k, t0:t0 + n],
                                          start=(k == 0), stop=(k == 15)), reads=[wk, "h0T%d" % k], writes=[PSK[bank]])
                                  copy_ps(evq(), so[0:wd, 1 + t0:1 + t0 + n], ps[bank][0:wd, 0:n], [PSK[bank]], [sks[q]])
                              if j < 27:
                                  sm = stm[j % 2]
                                  mk = "stm%d" % (j % 2)
                                  P.op("act", lambda e, so=so, sm=sm, wd=wd, j=j: e.activation(
                                      out=sm[0:wd, :], in_=so[0:wd, 1:T + 1], func=AF.Copy, scale=omm[0:wd, j:j + 1]),
                                      reads=sks + [sk, "omm"], writes=[mk])
                                  P.op("dve", lambda e, so=so, sm=sm, wd=wd, j=j: e.scalar_tensor_tensor(
                                      out=sm[0:wd, :], in0=so[0:wd, 0:T], scalar=cc("mu", j, wd), in1=sm[0:wd, :], op0=ALU.mult, op1=ALU.add),
                                      reads=sks + [sk, mk, "cst"], writes=[mk])
                                  P.dma("sp", "ptst%d" % (j % 2), lambda e, sm=sm, wd=wd, j=j: e.dma_start(out=PT[j * 128:j * 128 + wd, :], in_=sm[0:wd, :]),
                                        reads=[mk], writes=["PT%d" % j])
                              else:
                                  P.dma("sp", "ptsu%d" % (j % 2), lambda e, so=so, wd=wd, j=j: e.dma_start(out=PT[j * 128:j * 128 + wd, :], in_=so[0:wd, 1:T + 1]),
                                        reads=sks + [sk], writes=["PT%d" % j])
            P.barrier(exclude=("cv",))

        if "B" in phases:
            with ExitStack() as sB:
                MU = sb("MU", [128, 128], F32, sB)
                MUI = sb("MUI", [128, 128], F32, sB)
                ML = sb("ML", [128, 128], F32, sB)
                BO = sb("BO", [128, 128], F32, sB)
                ones = sb("onesT", [128, 128], F32, sB)
                for m_, nm, cm, pat, cmp_ in [(MU, "MU", -1, 1, ALU.is_gt), (MUI, "MUI", -1, 1, ALU.is_ge), (ML, "ML", 1, -1, ALU.is_gt)]:
                    P.op("pool", lambda e, m_=m_: e.memset(m_[:], 1.0), writes=[nm])
                    P.op("pool", lambda e, m_=m_, cm=cm, pat=pat, cmp_=cmp_: e.affine_select(
                        out=m_[:], in_=m_[:], pattern=[[pat, 128]], compare_op=cmp_, fill=0.0, base=0, channel_multiplier=cm),
                        reads=[nm], writes=[nm])
                P.op("pool", lambda e: e.memset(BO[:], 0.0), writes=["BO"])
                P.op("pool", lambda e: e.memset(BO[0:64, 0:64], 1.0), reads=["BO"], writes=["BO"])
                P.op("pool", lambda e: e.memset(BO[64:128, 64:128], 1.0), reads=["BO"], writes=["BO"])
                P.op("pool", lambda e: e.memset(ones[:], 1.0), writes=["ones"])
                omka = sb("omka", [128, 8], F32, sB)
                P.op("dve", lambda e: e.tensor_scalar(out=omka[:], in0=cst[:, CST["k_a"]:CST["k_a"] + 8], scalar1=-1.0, scalar2=1.0,
                                                      op0=ALU.mult, op1=ALU.add), reads=["cst"], writes=["omka"])
                W2A2 = sb("W2A2", [128, 1024], BF16, sB)
                G2a = sb("G2a", [128, 1024], BF16, sB)
                G2b = sb("G2b", [32, 1024], BF16, sB)
                P.dma("pool", "lw0", lambda e: e.dma_start(out=W2A2[:], in_=w2a2_d), writes=["W2A2"])
                P.dma("pool", "lw1", lambda e: e.dma_start(out=G2a[:], in_=g2_d[0:128, :]), writes=["G2a"])
                P.dma("pool", "lw2", lambda e: e.dma_start(out=G2b[:], in_=g2_d[128:160, :]), writes=["G2b"])
                LA = sb("LA", [128, T], BF16, sB)
                G0 = sb("G0", [128, T], BF16, sB)
                G1 = sb("G1", [32, T], BF16, sB)
                with ExitStack() as sl:
                    lst = sb("lstage", [128, T], F32, sl)
                    P.dma("sp", "lst", lambda e: e.dma_start(out=lst[:], in_=PT[24 * 128:25 * 128, :]), reads=["PT24"], writes=["lst"])
                    P.op("act", lambda e: e.activation(out=LA[0:64, :], in_=lst[0:64, :], func=AF.Tanh), reads=["lst"], writes=["LA"])
                    P.op("act", lambda e: e.activation(out=LA[64:128, :], in_=lst[64:128, :], func=AF.Copy), reads=["lst"], writes=["LA"])
                    P.dma("sp", "lst", lambda e: e.dma_start(out=lst[:], in_=PT[25 * 128:26 * 128, :]), reads=["PT25"], writes=["lst"])
                    P.op("act", lambda e: e.activation(out=G0[:], in_=lst[:], func=AF.Sigmoid), reads=["lst"], writes=["G0"])
                    P.dma("sp", "lst", lambda e: e.dma_start(out=lst[0:32, :], in_=PT[26 * 128:26 * 128 + 32, :]), reads=["PT26"], writes=["lst"])
                    P.op("act", lambda e: e.activation(out=G1[:], in_=lst[0:32, :], func=AF.Sigmoid), reads=["lst"], writes=["G1"])
                    P.barrier(exclude=("cv",))
                RKV = [[sb("rkv%d_%d" % (i, q), [128, T], F32, sB) for q in range(3)] for i in range(2)]
                LD = sb("LDt", [128, T], F32, sB)
                At = sb("At", [128, T], F32, sB)
                GT = sb("GTt", [128, T], F32, sB)
                YP = [sb("YP%d" % i, [128, T], BF16, sB) for i in range(2)]
                Mst = sb("Mst", [128, 64], F32, sB)
                Mb = sb("Mb", [128, 64], BF16, sB)

                def mk(name, shape, dt, nbuf=2):
                    return [sb("%s_%d" % (name, i), shape, dt, sB) for i in range(nbuf)]

                cum_ = mk("cum", [128, 128], F32); cumx_ = mk("cumx", [128, 128], F32)
                Ep_ = mk("Ep", [128, 128], F32); Em_ = mk("Em", [128, 128], F32); Ex_ = mk("Ex", [128, 128], F32)
                kk_ = mk("kk", [128, 128], F32); sqrk_ = mk("sqrk", [128, 2, 128], F32); rn_ = mk("rn", [128, 128], F32)
                kkn_ = mk("kkn", [128, 128], F32); tf_ = mk("tf", [128, 128], F32); k2_ = mk("k2", [128, 128], F32)
                b_ = mk("bb", [128, 128], F32); ART_ = mk("ART", [128, 2, 128], BF16); KT_ = mk("KT", [128, 128], BF16)
                BT_ = mk("BT", [128, 128], BF16); KH_ = mk("KH", [128, 128], F32); BH_ = mk("BH", [128, 128], F32)
                TOK_ = mk("TOK", [128, 3, 128], BF16); sbc_ = mk("sbc", [128, 128], F32); bon_ = mk("bon", [128, 128], F32, 3)
                Us_ = mk("Us", [128, 128], BF16); ys_ = mk("ys", [128, 128], F32); yT_ = mk("yT", [128, 128], F32)
                gst_ = mk("gst", [128, 2, 6], F32); gmv_ = mk("gmv", [128, 2, 2], F32); grs_ = mk("grs", [128, 2], F32)
                Bm_ = mk("Bm", [128, 128], F32, 4); Am_ = mk("Am", [128, 128], F32, 4)
                ArbT_ = mk("ArbT", [128, 128], BF16); AakT_ = mk("AakT", [128, 128], BF16); ArkT_ = mk("ArkT", [128, 128], BF16)
                Pm_ = mk("Pm", [128, 128], F32); W0s_ = mk("W0s", [128, 64], F32)

                pcnt = 0
                hcnt = 0
                for pr in range(dbg.get("npair", 8)):
                    cs = pr * 128
                    R, Kt, V = RKV[pr % 2]
                    rkk = ["rkv%d_%d" % (pr % 2, q) for q in range(3)]
                    for q, chn in enumerate([pr, 8 + pr, 16 + pr]):
                        P.dma("sp", "rkv%d_%d" % (pr % 2, q), lambda e, q=q, chn=chn, tl=RKV[pr % 2][q]: e.dma_start(out=tl[:], in_=PT[chn * 128:(chn + 1) * 128, :]),
                              reads=["PT%d" % chn], writes=[rkk[q]])
                    yp = YP[pr % 2]
                    ypk = "YP%d" % (pr % 2)
                    if "nchunk" in dbg:
                        P.op("pool", lambda e, yp=yp: e.memset(yp[:], 0.0), writes=[ypk])
                    for (t0, n) in TG:
                        P.op("pe", lambda e, t0=t0, n=n, cs=cs: e.matmul(ps[0][:, 0:n], lhsT=W2A2[0:64, cs:cs + 128], rhs=LA[0:64, t0:t0 + n], start=True, stop=True),
                             reads=["W2A2", "LA"], writes=[PSK[0]])
                        P.op("act", lambda e, t0=t0, n=n, pr=pr: e.activation(out=LD[:, t0:t0 + n], in_=ps[0][:, 0:n], func=AF.Sigmoid, bias=cc("w0", pr), scale=1.0),
                             reads=[PSK[0], "cst"], writes=["LD"])
                        P.op("pe", lambda e, t0=t0, n=n, cs=cs: e.matmul(ps[1][:, 0:n], lhsT=W2A2[64:128, cs:cs + 128], rhs=LA[64:128, t0:t0 + n], start=True, stop=True),
                             reads=["W2A2", "LA"], writes=[PSK[1]])
                        P.op("act", lambda e, t0=t0, n=n, pr=pr: e.activation(out=At[:, t0:t0 + n], in_=ps[1][:, 0:n], func=AF.Sigmoid, bias=cc("a0", pr), scale=1.0),
                             reads=[PSK[1], "cst"], writes=["At"])
                        P.op("pe", lambda e, t0=t0, n=n, cs=cs: e.matmul(ps[2][:, 0:n], lhsT=G2a[:, cs:cs + 128], rhs=G0[:, t0:t0 + n], start=True, stop=False),
                             reads=["G2a", "G0"], writes=[PSK[2]])
                        P.op("pe", lambda e, t0=t0, n=n, cs=cs: e.matmul(ps[2][:, 0:n], lhsT=G2b[0:32, cs:cs + 128], rhs=G1[0:32, t0:t0 + n], start=False, stop=True),
                             reads=["G2b", "G1"], writes=[PSK[2]])
                        P.op("dve", lambda e, t0=t0, n=n: e.tensor_copy(out=GT[:, t0:t0 + n], in_=ps[2][:, 0:n]), reads=[PSK[2]], writes=["GT"])
                    P.op("dve", lambda e: e.tensor_scalar(out=LD[:], in0=LD[:], scalar1=-0.6065306597126334, scalar2=None, op0=ALU.mult),
                         reads=["LD"], writes=["LD"])
                    P.op("pool", lambda e: e.memset(Mst[:], 0.0), writes=["M"])
                    P.op("pool", lambda e: e.memset(Mb[:], 0.0), writes=["Mb"])
                    chunks = TT[:dbg.get("nchunk", 17)]

                    def pre_gen(ci, pr=pr, R=R, Kt=Kt, V=V, rkk=rkk):
                        t0, C = chunks[ci]
                        z = ci % 2
                        sl_ = slice(t0, t0 + C)
                        cum, cumx, Ep, Em, Ex = cum_[z], cumx_[z], Ep_[z], Em_[z], Ex_[z]
                        kk, sqrk, rn, kkn, tf, k2, bb = kk_[z], sqrk_[z], rn_[z], kkn_[z], tf_[z], k2_[z], b_[z]
                        ART, KT, BT, KH, BH, TOK = ART_[z], KT_[z], BT_[z], KH_[z], BH_[z], TOK_[z]
                        sbc, bon = sbc_[z], bon_[ci % 3]
                        K_ = lambda nm: ("bon_%d" % (ci % 3)) if nm == "bon" else "%s_%d" % (nm, z)
                        P.op("dve", lambda e: e.tensor_tensor_scan(out=cum[:, 0:C], data0=ones[:, 0:C], data1=LD[:, sl_], initial=0.0, op0=ALU.mult, op1=ALU.add),
                             reads=["LD", "ones"], writes=[K_("cum")])
                        P.op("pool", lambda e: e.tensor_scalar(out=kk[:, 0:C], in0=Kt[:, sl_], scalar1=cc("k_k", pr), scalar2=None, op0=ALU.mult),
                             reads=[rkk[1], "cst"], writes=[K_("kk")])
                        P.op("dve", lambda e: e.tensor_scalar(out=tf[:, 0:C], in0=At[:, sl_], scalar1=cc("k_a", pr), scalar2=omka[:, pr:pr + 1], op0=ALU.mult, op1=ALU.add),
                             reads=["At", "cst", "omka"], writes=[K_("tf")])
                        yield
                        P.op("pool", lambda e: e.tensor_tensor(out=cumx[:, 0:C], in0=cum[:, 0:C], in1=LD[:, sl_], op=ALU.subtract),
                             reads=[K_("cum"), "LD"], writes=[K_("cumx")])
                        P.op("act", lambda e: e.activation(out=Ep[:, 0:C], in_=cum[:, 0:C], func=AF.Exp), reads=[K_("cum")], writes=[K_("Ep")])
                        P.op("act", lambda e: e.activation(out=Em[:, 0:C], in_=cum[:, 0:C], func=AF.Exp, scale=-1.0), reads=[K_("cum")], writes=[K_("Em")])
                        P.op("pool", lambda e: e.tensor_tensor(out=sqrk[:, 0, 0:C], in0=kk[:, 0:C], in1=kk[:, 0:C], op=ALU.mult),
                             reads=[K_("kk")], writes=[K_("sq")])
                        P.op("dve", lambda e: e.tensor_tensor(out=k2[:, 0:C], in0=Kt[:, sl_], in1=tf[:, 0:C], op=ALU.mult),
                             reads=[rkk[1], K_("tf")], writes=[K_("k2")])
                        yield
                        P.op("act", lambda e: e.activation(out=Ex[:, 0:C], in_=cumx[:, 0:C], func=AF.Exp), reads=[K_("cumx")], writes=[K_("Ex")])
                        P.op("dve", lambda e: e.scalar_tensor_tensor(out=sqrk[:, 1, 0:C], in0=R[:, sl_], scalar=cc("r_k", pr), in1=k2[:, 0:C], op0=ALU.mult, op1=ALU.mult),
                             reads=[rkk[0], K_("k2"), "cst"], writes=[K_("rk")])
                        yield
                        for hf in range(2):
                            P.op("pe", lambda e, hf=hf: e.matmul(ps[0][:, hf * 128:hf * 128 + C], lhsT=BO[:, :], rhs=sqrk[:, hf, 0:C], start=True, stop=True),
                                 reads=["BO", K_("sq") if hf == 0 else K_("rk")], writes=[PSK[0]])
                        P.op("dve", lambda e: e.tensor_tensor(out=ART[:, 1, 0:C], in0=R[:, sl_], in1=Ep[:, 0:C], op=ALU.mult),
                             reads=[rkk[0], K_("Ep")], writes=[K_("RT")])
                        P.op("pool", lambda e: e.tensor_tensor(out=KT[:, 0:C], in0=k2[:, 0:C], in1=Em[:, 0:C], op=ALU.mult),
                             reads=[K_("k2"), K_("Em")], writes=[K_("KT")])
                        yield
                        P.op("act", lambda e: e.activation(out=rn[:, 0:C], in_=ps[0][:, 0:C], func=AF.Ln), reads=[PSK[0]], writes=[K_("rn")])
                        P.op("act", lambda e: e.activation(out=sbc[:, 0:C], in_=ps[0][:, 128:128 + C], func=AF.Copy), reads=[PSK[0]], writes=[K_("sbc")])
                        P.op("act", lambda e: e.activation(out=rn[:, 0:C], in_=rn[:, 0:C], func=AF.Exp, scale=-0.5), reads=[K_("rn")], writes=[K_("rn")])
                        EC = Ep[:, C - 1:C]
                        P.op("dve", lambda e: e.scalar_tensor_tensor(out=KH[:, 0:C], in0=k2[:, 0:C], scalar=EC, in1=Em[:, 0:C], op0=ALU.mult, op1=ALU.mult),
                             reads=[K_("k2"), K_("Em"), K_("Ep")], writes=[K_("KH")])
                        yield
                        P.op("pool", lambda e: e.tensor_tensor(out=bon[:, 0:C], in0=sbc[:, 0:C], in1=V[:, sl_], op=ALU.mult),
                             reads=[K_("sbc"), rkk[2]], writes=[K_("bon")])
                        P.op("dve", lambda e: e.tensor_tensor(out=kkn[:, 0:C], in0=kk[:, 0:C], in1=rn[:, 0:C], op=ALU.mult),
                             reads=[K_("kk"), K_("rn")], writes=[K_("kkn")])
                        yield
                        P.op("pool", lambda e: e.tensor_tensor(out=bb[:, 0:C], in0=kkn[:, 0:C], in1=At[:, sl_], op=ALU.mult),
                             reads=[K_("kkn"), "At"], writes=[K_("bb")])
                        P.op("dve", lambda e: e.scalar_tensor_tensor(out=ART[:, 0, 0:C], in0=kkn[:, 0:C], scalar=-1.0, in1=Ex[:, 0:C], op0=ALU.mult, op1=ALU.mult),
                             reads=[K_("kkn"), K_("Ex")], writes=[K_("AT")])
                        yield
                        P.op("pool", lambda e: e.tensor_tensor(out=BT[:, 0:C], in0=bb[:, 0:C], in1=Em[:, 0:C], op=ALU.mult),
                             reads=[K_("bb"), K_("Em")], writes=[K_("BT")])
                        P.op("dve", lambda e: e.scalar_tensor_tensor(out=BH[:, 0:C], in0=bb[:, 0:C], scalar=EC, in1=Em[:, 0:C], op0=ALU.mult, op1=ALU.mult),
                             reads=[K_("bb"), K_("Em"), K_("Ep")], writes=[K_("BH")])
                        yield
                        for q, (src, rk_) in enumerate([(KH[:, 0:C], K_("KH")), (BH[:, 0:C], K_("BH")), (V[:, sl_], rkk[2])]):
                            P.op("pe", lambda e, q=q, src=src: e.transpose(out=ps[1][0:C, q * 128:(q + 1) * 128], in_=src, identity=ident[:, :]),
                                 reads=[rk_, "ident"], writes=[PSK[1]])
                        yield
                        P.op("act", lambda e: e.activation(out=TOK[0:C].rearrange("p a b -> p (a b)"), in_=ps[1][0:C, 0:384], func=AF.Copy),
                             reads=[PSK[1]], writes=[K_("TOK")])

                    def head_gen(ci, hh):
                        t0, C = chunks[ci]
                        z = ci % 2
                        ART, KT, BT, TOK, Us = ART_[z], KT_[z], BT_[z], TOK_[z], Us_[z]
                        ys = ys_[z]
                        K_ = lambda nm: "%s_%d" % (nm, z)
                        y_ = hh
                        hs_ = slice(hh * 64, hh * 64 + 64)
                        XB, YB, ZB = 2 + hh, 4 + hh, 6 + hh
                        y4 = 2 * (ci % 2) + hh
                        Bm, Am = Bm_[y4], Am_[y4]
                        y4p = (y4 + 2) % 4
                        zz = ci % 2
                        ArbT, AakT, ArkT, Pm, W0s = ArbT_[hh], AakT_[hh], ArkT_[hh], Pm_[hh], W0s_[hh]
                        H_ = lambda nm: "%s_h%d" % (nm, hh)
                        P.op("pe", lambda e: e.matmul(ps[XB][0:C, 0:2 * C].rearrange("p (a c) -> p a c", a=2), lhsT=BT[hs_, 0:C], rhs=ART[hs_, :, 0:C], start=True, stop=True),
                             reads=[K_("BT"), K_("AT"), K_("RT")], writes=[PSK[XB]])
                        P.op("pe", lambda e: e.matmul(ps[XB][0:C, 256:256 + C], lhsT=ART[hs_, 0, 0:C], rhs=BT[hs_, 0:C], start=True, stop=True),
                             reads=[K_("BT"), K_("AT")], writes=[PSK[XB]])
                        P.op("pe", lambda e: e.matmul(ps[YB][0:C, 0:2 * C].rearrange("p (a c) -> p a c", a=2), lhsT=KT[hs_, 0:C], rhs=ART[hs_, :, 0:C], start=True, stop=True),
                             reads=[K_("KT"), K_("AT"), K_("RT")], writes=[PSK[YB]])
                        yield
                        Bk, Ak = "Bm%d" % y4, "Am%d" % y4
                        P.op("dve", lambda e: e.tensor_tensor(out=Bm[0:C, 0:C], in0=ps[XB][0:C, 0:C], in1=MU[0:C, 0:C], op=ALU.mult), reads=[PSK[XB], "MU"], writes=[Bk])
                        P.op("dve", lambda e: e.tensor_tensor(out=Am[0:C, 0:C], in0=ps[XB][0:C, 256:256 + C], in1=ML[0:C, 0:C], op=ALU.mult), reads=[PSK[XB], "ML"], writes=[Ak])
                        yield
                        P.op("pool", lambda e: e.tensor_tensor(out=Pm[0:C, 0:C], in0=Bm[0:C, 0:C], in1=ident[0:C, 0:C], op=ALU.add), reads=[Bk, "ident"], writes=[H_("Pm")])
                        P.op("dve", lambda e: e.tensor_tensor(out=ArbT[0:C, 0:C], in0=ps[XB][0:C, C:2 * C], in1=MUI[0:C, 0:C], op=ALU.mult), reads=[PSK[XB], "MUI"], writes=[H_("ArbT")])
                        L = 7 if C == 128 else 4
                        Acur, Bcur, Akc, Bkc = Am, Bm, Ak, Bk
                        for l in range(1, L):
                            odd = (l % 2) == 1
                            An = Am_[y4p] if odd else Am_[y4]
                            Bn = Bm_[y4p] if odd else Bm_[y4]
                            Ank = "Am%d" % (y4p if odd else y4)
                            Bnk = "Bm%d" % (y4p if odd else y4)
                            P.op("pe", lambda e, Acur=Acur, Bcur=Bcur: e.matmul(ps[ZB][0:C, 0:C], lhsT=Bcur[0:C, 0:C], rhs=Acur[0:C, 0:C], start=True, stop=True),
                                 reads=[Akc, Bkc], writes=[PSK[ZB]])
                            if l < L - 1:
                                P.op("pe", lambda e, Acur=Acur, Bcur=Bcur: e.matmul(ps[ZB][0:C, 128:128 + C], lhsT=Acur[0:C, 0:C], rhs=Bcur[0:C, 0:C], start=True, stop=True),
                                     reads=[Akc, Bkc], writes=[PSK[ZB]])
                            yield
                            if l == 1:
                                P.op("dve", lambda e: e.tensor_tensor(out=AakT[0:C, 0:C], in0=ps[YB][0:C, 0:C], in1=MU[0:C, 0:C], op=ALU.mult), reads=[PSK[YB], "MU"], writes=[H_("AakT")])
                                P.op("dve", lambda e: e.tensor_tensor(out=ArkT[0:C, 0:C], in0=ps[YB][0:C, C:2 * C], in1=MUI[0:C, 0:C], op=ALU.mult), reads=[PSK[YB], "MUI"], writes=[H_("ArkT")])
                            if l < L - 1:
                                P.op("act", lambda e, An=An, Bn=Bn: e.activation(out=An[0:C, 0:C], in_=ps[ZB][0:C, 0:C], func=AF.Copy), reads=[PSK[ZB]], writes=[Ank])
                                P.op("act", lambda e, An=An, Bn=Bn: e.activation(out=Bn[0:C, 0:C], in_=ps[ZB][0:C, 128:128 + C], func=AF.Copy), reads=[PSK[ZB]], writes=[Bnk])
                            else:
                                P.op("act", lambda e, An=An: e.activation(out=An[0:C, 0:C], in_=ps[ZB][0:C, 0:C], func=AF.Copy), reads=[PSK[ZB]], writes=[Ank])
                            yield
                            P.op("pe", lambda e, An=An: e.matmul(ps[YB][0:C, 256:256 + C], lhsT=An[0:C, 0:C], rhs=Pm[0:C, 0:C], start=True, stop=True),
                                 reads=[Ank, H_("Pm")], writes=[PSK[YB]])
                            yield
                            P.op("dve", lambda e: e.tensor_tensor(out=Pm[0:C, 0:C], in0=Pm[0:C, 0:C], in1=ps[YB][0:C, 256:256 + C], op=ALU.add), reads=[H_("Pm"), PSK[YB]], writes=[H_("Pm")])
                            Acur, Bcur, Akc, Bkc = An, Bn, Ank, Bnk
                        yield
                        Vt_h = TOK[0:C, 2, hh * 64:hh * 64 + 64]
                        P.op("pe", lambda e: e.matmul(ps[ZB][0:C, 256:320], lhsT=ART[hs_, 0, 0:C], rhs=Mb[hs_, :], start=True, stop=False),
                             reads=[K_("AT"), "Mb"], writes=[PSK[ZB]])
                        P.op("pe", lambda e: e.matmul(ps[ZB][0:C, 256:320], lhsT=AakT[0:C, 0:C], rhs=Vt_h, start=False, stop=True),
                             reads=[H_("AakT"), K_("TOK")], writes=[PSK[ZB]])
                        yield
                        P.op("act", lambda e: e.activation(out=W0s[0:C, :], in_=ps[ZB][0:C, 256:320], func=AF.Copy), reads=[PSK[ZB]], writes=[H_("W0s")])
                        yield
                        P.op("pe", lambda e: e.matmul(ps[ZB][0:C, 320:384], lhsT=Pm[0:C, 0:C], rhs=W0s[0:C, :], start=True, stop=True),
                             reads=[H_("Pm"), H_("W0s")], writes=[PSK[ZB]])
                        yield
                        P.op("act", lambda e: e.activation(out=Us[0:C, hh * 64:hh * 64 + 64], in_=ps[ZB][0:C, 320:384], func=AF.Copy), reads=[PSK[ZB]], writes=[K_("Us%d" % hh)])
                        yield
                        P.op("pe", lambda e: e.matmul(ps[ZB][0:C, 384:448], lhsT=ART[hs_, 1, 0:C], rhs=Mb[hs_, :], start=True, stop=False),
                             reads=[K_("RT"), "Mb"], writes=[PSK[ZB]])
                        P.op("pe", lambda e: e.matmul(ps[ZB][0:C, 384:448], lhsT=ArbT[0:C, 0:C], rhs=Us[0:C, hh * 64:hh * 64 + 64], start=False, stop=False),
                             reads=[H_("ArbT"), K_("Us%d" % hh)], writes=[PSK[ZB]])
                        P.op("pe", lambda e: e.matmul(ps[ZB][0:C, 384:448], lhsT=ArkT[0:C, 0:C], rhs=Vt_h, start=False, stop=True),
                             reads=[H_("ArkT"), K_("TOK")], writes=[PSK[ZB]])
                        yield
                        P.op("act", lambda e: e.activation(out=ys[0:C, hh * 64:hh * 64 + 64], in_=ps[ZB][0:C, 384:448], func=AF.Copy), reads=[PSK[ZB]], writes=[K_("ys%d" % hh)])

                    def state_step(ci):
                        t0, C = chunks[ci]
                        z = ci % 2
                        TOK, Us, Ep = TOK_[z], Us_[z], Ep_[z]
                        K_ = lambda nm: "%s_%d" % (nm, z)
                        P.op("pe", lambda e: e.matmul(ps[2][:, 384:512], lhsT=TOK[0:C, 1, :], rhs=Us[0:C, :], start=True, stop=False),
                             reads=[K_("TOK"), K_("Us0"), K_("Us1")], writes=[PSK[2]])
                        P.op("pe", lambda e: e.matmul(ps[2][:, 384:512], lhsT=TOK[0:C, 0, :], rhs=TOK[0:C, 2, :], start=False, stop=True),
                             reads=[K_("TOK")], writes=[PSK[2]])
                        for hh in range(2):
                            hs_ = slice(hh * 64, hh * 64 + 64)
                            P.op("dve", lambda e, hs_=hs_, hh=hh: e.scalar_tensor_tensor(out=Mst[hs_, :], in0=Mst[hs_, :], scalar=Ep[hs_, C - 1:C], in1=ps[2][hs_, 384 + hh * 64:384 + hh * 64 + 64], op0=ALU.mult, op1=ALU.add),
                                 reads=["M", K_("Ep"), PSK[2]], writes=["M"])
                        P.op("pool", lambda e: e.tensor_copy(out=Mb[:], in_=Mst[:]), reads=["M"], writes=["Mb"])

                    def post_gen(ci, pr=pr, yp=yp, ypk=ypk):
                        t0, C = chunks[ci]
                        z = ci % 2
                        sl_ = slice(t0, t0 + C)
                        ys, yT, bon = ys_[z], yT_[z], bon_[ci % 3]
                        gst, gmv, grs = gst_[z], gmv_[z], grs_[z]
                        K_ = lambda nm: ("bon_%d" % (ci % 3)) if nm == "bon" else "%s_%d" % (nm, z)
                        for hh in range(2):
                            P.op("dve", lambda e, hh=hh: e.bn_stats(out=gst[0:C, hh, :], in_=ys[0:C, hh * 64:hh * 64 + 64]), reads=[K_("ys%d" % hh)], writes=[K_("gst%d" % hh)])
                        yield
                        for hh in range(2):
                            P.op("dve", lambda e, hh=hh: e.bn_aggr(out=gmv[0:C, hh, :], in_=gst[0:C, hh, :]), reads=[K_("gst%d" % hh)], writes=[K_("gmv%d" % hh)])
                        yield
                        P.op("act", lambda e: e.activation(out=grs[0:C, :], in_=gmv[0:C, :, 1], func=AF.Ln, bias=epsb[0:C, 1:2], scale=1.0),
                             reads=[K_("gmv0"), K_("gmv1"), "epsb"], writes=[K_("grs")])
                        P.op("act", lambda e: e.activation(out=grs[0:C, :], in_=grs[0:C, :], func=AF.Exp, scale=-0.5), reads=[K_("grs")], writes=[K_("grs")])
                        yield
                        for hh in range(2):
                            P.op("dve", lambda e, hh=hh: e.tensor_scalar(out=ys[0:C, hh * 64:hh * 64 + 64], in0=ys[0:C, hh * 64:hh * 64 + 64],
                                                                  scalar1=gmv[0:C, hh, 0:1], scalar2=grs[0:C, hh:hh + 1], op0=ALU.subtract, op1=ALU.mult),
                                 reads=[K_("ys%d" % hh), K_("gmv%d" % hh), K_("grs")], writes=[K_("ys%d" % hh)])
                        yield
                        P.op("pe", lambda e: e.transpose(out=ps[1][:, 384:384 + C], in_=ys[0:C, :], identity=ident[0:C, 0:C]), reads=[K_("ys0"), K_("ys1"), "ident"], writes=[PSK[1]])
                        yield
                        P.op("act", lambda e: e.activation(out=yT[:, 0:C], in_=ps[1][:, 384:384 + C], func=AF.Identity, scale=cc("gn_g", pr), bias=cc("gn_b", pr)),
                             reads=[PSK[1], "cst"], writes=[K_("yT")])
                        yield
                        P.op("pool", lambda e: e.tensor_tensor(out=yT[:, 0:C], in0=yT[:, 0:C], in1=bon[:, 0:C], op=ALU.add), reads=[K_("yT"), K_("bon")], writes=[K_("yT")])
                        P.op("pool", lambda e: e.tensor_tensor(out=yp[:, sl_], in0=yT[:, 0:C], in1=GT[:, sl_], op=ALU.mult), reads=[K_("yT"), "GT"], writes=[ypk])

                    def run_il(gens):
                        gens = list(gens)
                        while gens:
                            for g in list(gens):
                                try:
                                    next(g)
                                except StopIteration:
                                    gens.remove(g)

                    nchk = len(chunks)
                    run_il([pre_gen(0)])
                    for ci in range(nchk):
                        gens = [head_gen(ci, 0), head_gen(ci, 1)]
                        if ci + 1 < nchk:
                            gens.append(pre_gen(ci + 1))
                        if ci >= 1:
                            gens.append(post_gen(ci - 1))
                        run_il(gens)
                        state_step(ci)
                    run_il([post_gen(nchk - 1)])
                    P.dma("sp", "ypst%d" % (pr % 2), lambda e, yp=yp, pr=pr: e.dma_start(out=YTD[pr * 128:(pr + 1) * 128, :], in_=yp[:]), reads=[ypk], writes=["YTD%d" % pr])
            P.barrier(exclude=("cv",))

        if "C" in phases:
            with ExitStack() as sC:
                MUc = sb("MUc", [128, 128], F32, sC)
                MUb = sb("MUb", [128, 128], BF16, sC)
                MLc = sb("MLc", [128, 128], F32, sC)
                onesc = sb("onesc", [128, 128], F32, sC)
                P.op("pool", lambda e: e.memset(MUc[:], 1.0), writes=["MUc"])
                P.op("pool", lambda e: e.affine_select(out=MUc[:], in_=MUc[:], pattern=[[1, 128]], compare_op=ALU.is_gt, fill=0.0, base=0, channel_multiplier=-1),
                     reads=["MUc"], writes=["MUc"])
                P.op("pool", lambda e: e.tensor_copy(out=MUb[:], in_=MUc[:]), reads=["MUc"], writes=["MUb"])
                P.op("pool", lambda e: e.memset(MLc[:], 1.0), writes=["MLc"])
                P.op("pool", lambda e: e.affine_select(out=MLc[:], in_=MLc[:], pattern=[[-1, 128]], compare_op=ALU.is_gt, fill=0.0, base=0, channel_multiplier=1),
                     reads=["MLc"], writes=["MLc"])
                P.op("pool", lambda e: e.memset(onesc[:], 1.0), writes=["onesc"])
                QKV = [sb("cqkv%d" % q, [128, T], F32, sC) for q in range(3)]
                qb = sb("cqb", [128, T], BF16, sC)
                kb = sb("ckb", [128, T], BF16, sC)
                vt = sb("cvt", [128, 17, 128], BF16, sC)
                YS = [sb("cYS%d" % i, [128, T], BF16, sC) for i in range(2)]
                NB = 4
                SP_ = [sb("cSP%d" % i, [128, 2048], F32, sC) for i in range(NB)]
                E_ = SP_
                SU_ = [sb("cSU%d" % i, [128, 2048 + 128], F32, sC) for i in range(NB)]
                LB_ = [sb("cLB%d" % i, [128, 2048], F32, sC) for i in range(NB)]
                AT_ = [sb("cAT%d" % i, [128, 2048], BF16, sC) for i in range(NB)]
                m0_ = [sb("cm0_%d" % i, [16, 4, 128], F32, sC) for i in range(NB)]
                a0_ = [sb("ca0_%d" % i, [16, 128], BF16, sC) for i in range(NB)]
                os_ = [sb("cos%d" % i, [128, 128], F32, sC) for i in range(4)]
                junk_ = [sb("cjunk%d" % i, [128, 2, 64], F32, sC) for i in range(4)]
                ssq_ = [sb("cssq%d" % i, [128, 2], F32, sC) for i in range(4)]
                for i in range(NB):
                    P.op("pool", lambda e, i=i: e.memset(SU_[i][:, 0:128], 0.0), writes=["cSU%d" % i])
                hc = 0
                bc_ = 0
                for pr in range(dbg.get("npair", 8)):
                    for q, chn in enumerate([27 + pr, 35 + pr, 43 + pr]):
                        P.dma("sp", "cqkv%d" % q, lambda e, q=q, chn=chn: e.dma_start(out=QKV[q][:], in_=PT[chn * 128:(chn + 1) * 128, :]),
                              reads=["PT%d" % chn], writes=["cqkv%d" % q])
                    P.op("act", lambda e: e.activation(out=qb[:], in_=QKV[0][:], func=AF.Copy, scale=0.125), reads=["cqkv0"], writes=["cqb"])
                    P.op("pool", lambda e: e.tensor_copy(out=kb[:], in_=QKV[1][:]), reads=["cqkv1"], writes=["ckb"])
                    for ti, (t0, n) in enumerate(TT):
                        P.op("pe", lambda e, t0=t0, n=n: e.transpose(out=ps[6][0:n, 0:128], in_=QKV[2][:, t0:t0 + n], identity=ident[:, :]),
                             reads=["cqkv2", "ident"], writes=[PSK[6]])
                        P.op("act", lambda e, ti=ti, n=n: e.activation(out=vt[0:n, ti, :], in_=ps[6][0:n, 0:128], func=AF.Copy), reads=[PSK[6]], writes=["cvt%d" % ti])
                    ys_t = YS[pr % 2]
                    ysk = "cYS%d" % (pr % 2)
                    if "nblk" in dbg:
                        P.op("pool", lambda e, ys_t=ys_t: e.memset(ys_t[:], 0.0), writes=[ysk])
                    blocks = TT[:dbg.get("nblk", 17)]

                    def chead_gen(bi, hh, g):
                        tq, Cq = blocks[bi]
                        z_ = g
                        hs_ = slice(hh * 64, hh * 64 + 64)
                        E, SPm, SU, LBt, ATT, m0, a0 = E_[z_], SP_[z_], SU_[z_], LB_[z_], AT_[z_], m0_[z_], a0_[z_]
                        Z_ = lambda nm: "%s%d" % (nm, z_)
                        osb = os_[bi % 4]; osk = "cos%d" % (bi % 4)
                        Zb = [2 * g, 2 * g]
                        eb = 2 * g + 1
                        mb = 2 * g + 1
                        nk = bi
                        ncol = nk * Cq
                        PW = 3
                        nbank = (nk + PW - 1) // PW
                        qsl = slice(tq, tq + Cq)
                        P.op("pe", lambda e: e.matmul(ps[mb][0:16, 0:Cq], lhsT=kb[hs_, 0:16], rhs=qb[hs_, qsl], start=True, stop=True),
                             reads=["ckb", "cqb"], writes=[PSK[mb]])
                        yield
                        P.op("act", lambda e: e.activation(out=m0[:, 0, 0:Cq], in_=ps[mb][0:16, 0:Cq], func=AF.Exp), reads=[PSK[mb]], writes=[Z_("cm0a")])
                        P.op("act", lambda e: e.activation(out=m0[:, 1, 0:Cq], in_=m0[:, 0, 0:Cq], func=AF.Ln, bias=epsb[0:16, 3:4], scale=1.0), reads=[Z_("cm0a"), "epsb"], writes=[Z_("cm0b")])
                        yield
                        P.op("dve", lambda e: e.tensor_tensor(out=m0[:, 2, 0:Cq], in0=ps[mb][0:16, 0:Cq], in1=m0[:, 1, 0:Cq], op=ALU.subtract), reads=[PSK[mb], Z_("cm0b")], writes=[Z_("cm0c")])
                        if bi == 0:
                            P.op("pool", lambda e: e.tensor_tensor(out=m0[:, 1, 0:Cq], in0=m0[:, 1, 0:Cq], in1=MUc[0:16, 0:Cq], op=ALU.mult), reads=[Z_("cm0b"), "MUc"], writes=[Z_("cm0b")])
                        yield
                        for b in range(nbank):
                            zb = Zb[b % 2]
                            c0, c1 = b * PW * Cq, min(ncol, (b + 1) * PW * Cq)
                            for j in range(PW * b, min(nk, PW * b + PW)):
                                kc = bi - j
                                s0 = TT[kc][0]
                                P.op("pe", lambda e, j=j, s0=s0, zb=zb: e.matmul(ps[zb][:, (j % PW) * Cq:(j % PW + 1) * Cq], lhsT=kb[hs_, s0:s0 + 128], rhs=qb[hs_, qsl], start=True, stop=True),
                                     reads=["ckb", "cqb"], writes=[PSK[zb]])
                            yield
                            P.op("act", lambda e, zb=zb, c0=c0, c1=c1: e.activation(out=E[:, c0:c1], in_=ps[zb][:, 0:c1 - c0], func=AF.Exp), reads=[PSK[zb]], writes=[Z_("cE") + "_%d" % b])
                            yield
                            P.op("act", lambda e, c0=c0, c1=c1: e.activation(out=SPm[:, c0:c1], in_=E[:, c0:c1], func=AF.Ln, bias=epsb[:, 3:4], scale=1.0),
                                 reads=[Z_("cE") + "_%d" % b, "epsb"], writes=[Z_("cSP") + "_%d" % b])
                            yield
                            P.op("dve", lambda e, zb=zb, c0=c0, c1=c1: e.tensor_tensor(out=LBt[:, c0:c1], in0=ps[zb][:, 0:c1 - c0], in1=SPm[:, c0:c1], op=ALU.subtract),
                                 reads=[PSK[zb], Z_("cSP") + "_%d" % b], writes=[Z_("cLB") + "_%d" % b])
                            if b == 0:
                                P.op("pool", lambda e: e.tensor_tensor(out=SPm[:, 0:Cq], in0=SPm[:, 0:Cq], in1=MUc[:, 0:Cq], op=ALU.mult), reads=[Z_("cSP") + "_0", "MUc"], writes=[Z_("cSP") + "_0"])
                            yield
                            for j in range(PW * b, min(nk, PW * b + PW)):
                                P.op("pool", lambda e, j=j: e.tensor_tensor(out=SU[:, (j + 1) * Cq:(j + 2) * Cq], in0=SU[:, j * Cq:(j + 1) * Cq], in1=SPm[:, j * Cq:(j + 1) * Cq], op=ALU.add),
                                     reads=[Z_("cSP") + "_%d" % b, Z_("cSU")], writes=[Z_("cSU")])
                                yield
                        for b in range(nbank):
                            c0, c1 = b * PW * Cq, min(ncol, (b + 1) * PW * Cq)
                            P.op("pe", lambda e, c0=c0, c1=c1: e.matmul(ps[eb][:, 0:c1 - c0], lhsT=MLc[:, :], rhs=SPm[:, c0:c1], start=True, stop=False),
                                 reads=["MLc", Z_("cSP") + "_%d" % b], writes=[PSK[eb]])
                            P.op("pe", lambda e, c0=c0, c1=c1: e.matmul(ps[eb][:, 0:c1 - c0], lhsT=onesc[:, :], rhs=SU[:, c0:c1], start=False, stop=True),
                                 reads=["onesc", Z_("cSU")], writes=[PSK[eb]])
                            yield
                            P.op("dve", lambda e, c0=c0, c1=c1: e.tensor_tensor(out=LBt[:, c0:c1], in0=LBt[:, c0:c1], in1=ps[eb][:, 0:c1 - c0], op=ALU.subtract),
                                 reads=[PSK[eb], Z_("cLB") + "_%d" % b], writes=[Z_("cLB") + "_%d" % b])
                            yield
                        if nk > 0:
                            P.op("act", lambda e: e.activation(out=ATT[:, 0:ncol], in_=LBt[:, 0:ncol], func=AF.Exp),
                                 reads=[Z_("cLB") + "_%d" % b for b in range(nbank)], writes=[Z_("cAT")])
                        P.op("pe", lambda e: e.matmul(ps[mb][0:16, 128:128 + Cq], lhsT=MLc[0:16, 0:16], rhs=m0[:, 1, 0:Cq], start=True, stop=(nk == 0)),
                             reads=["MLc", Z_("cm0b")], writes=[PSK[mb]])
                        if nk > 0:
                            P.op("pe", lambda e: e.matmul(ps[mb][0:16, 128:128 + Cq], lhsT=onesc[:, 0:16], rhs=SU[:, nk * Cq:(nk + 1) * Cq], start=False, stop=True),
                                 reads=["onesc", Z_("cSU")], writes=[PSK[mb]])
                        yield
                        if nk > 0:
                            P.op("pool", lambda e: e.tensor_tensor(out=ATT[:, 0:Cq], in0=ATT[:, 0:Cq], in1=MUb[:, 0:Cq], op=ALU.mult), reads=[Z_("cAT"), "MUb"], writes=[Z_("cAT")])
                        P.op("dve", lambda e: e.tensor_tensor(out=m0[:, 3, 0:Cq], in0=m0[:, 2, 0:Cq], in1=ps[mb][0:16, 128:128 + Cq], op=ALU.subtract), reads=[PSK[mb], Z_("cm0c")], writes=[Z_("cm0d")])
                        yield
                        P.op("act", lambda e: e.activation(out=a0[:, 0:Cq], in_=m0[:, 3, 0:Cq], func=AF.Exp), reads=[Z_("cm0d")], writes=[Z_("ca0")])
                        if bi == 0:
                            P.op("pool", lambda e: e.tensor_tensor(out=a0[:, 0:Cq], in0=a0[:, 0:Cq], in1=MUb[0:16, 0:Cq], op=ALU.mult), reads=[Z_("ca0"), "MUb"], writes=[Z_("ca0")])
                        yield
                        oc = slice(256, 320)
                        for j in range(nk):
                            kc = bi - j
                            P.op("pe", lambda e, j=j, kc=kc: e.matmul(ps[mb][0:Cq, oc], lhsT=ATT[:, j * Cq:(j + 1) * Cq], rhs=vt[:, kc, hh * 64:hh * 64 + 64], start=(j == 0), stop=False),
                                 reads=[Z_("cAT"), "cvt%d" % kc], writes=[PSK[mb]])
                        P.op("pe", lambda e: e.matmul(ps[mb][0:Cq, oc], lhsT=a0[:, 0:Cq], rhs=vt[0:16, 0, hh * 64:hh * 64 + 64], start=(nk == 0), stop=True),
                             reads=[Z_("ca0"), "cvt0"], writes=[PSK[mb]])
                        yield
                        P.op("act", lambda e: e.activation(out=osb[0:Cq, hh * 64:hh * 64 + 64], in_=ps[mb][0:Cq, oc], func=AF.Copy), reads=[PSK[mb]], writes=[osk + "_%d" % hh])

                    def cpost_gen(bi, pr=pr, ys_t=ys_t, ysk=ysk):
                        tq, Cq = blocks[bi]
                        osb = os_[bi % 4]; osk = "cos%d" % (bi % 4)
                        ssq = ssq_[bi % 4]; ssk = "cssq%d" % (bi % 4)
                        jk = junk_[bi % 4]
                        for hh in range(2):
                            P.op("act", lambda e, hh=hh: e.activation(out=jk[0:Cq, hh, :], in_=osb[0:Cq, hh * 64:hh * 64 + 64], func=AF.Square, accum_out=ssq[0:Cq, hh:hh + 1]),
                                 reads=[osk + "_%d" % hh], writes=[ssk + "_%d" % hh, "cjunk%d_%d" % (bi % 4, hh)])
                        yield
                        P.op("act", lambda e: e.activation(out=ssq[0:Cq, :], in_=ssq[0:Cq, :], func=AF.Ln, bias=epsb[0:Cq, 2:3], scale=1.0 / 64.0),
                             reads=[ssk + "_0", ssk + "_1", "epsb"], writes=[ssk + "_0", ssk + "_1"])
                        P.op("act", lambda e: e.activation(out=ssq[0:Cq, :], in_=ssq[0:Cq, :], func=AF.Exp, scale=-0.5), reads=[ssk + "_0", ssk + "_1"], writes=[ssk + "_0", ssk + "_1"])
                        yield
                        for hh in range(2):
                            P.op("dve", lambda e, hh=hh: e.tensor_scalar(out=osb[0:Cq, hh * 64:hh * 64 + 64], in0=osb[0:Cq, hh * 64:hh * 64 + 64], scalar1=ssq[0:Cq, hh:hh + 1], scalar2=None, op0=ALU.mult),
                                 reads=[osk + "_%d" % hh, ssk + "_%d" % hh], writes=[osk + "_%d" % hh])
                        yield
                        pbk = 0 if bi % 2 == 0 else 2
                        P.op("pe", lambda e: e.transpose(out=ps[pbk][:, 384:384 + Cq], in_=osb[0:Cq, :], identity=ident[0:Cq, 0:Cq]), reads=[osk + "_0", osk + "_1", "ident"], writes=[PSK[pbk]])
                        yield
                        P.op("act", lambda e: e.activation(out=ys_t[:, tq:tq + Cq], in_=ps[pbk][:, 384:384 + Cq], func=AF.Identity, scale=cc("sb_g", pr), bias=0.0),
                             reads=[PSK[pbk], "cst"], writes=[ysk])

                    def run_ilc(gens):
                        gens = list(gens)
                        while gens:
                            for g in list(gens):
                                try:
                                    next(g)
                                except StopIteration:
                                    gens.remove(g)

                    nblk_ = len(blocks)
                    for bi in range(0, nblk_, 2):
                        gens = [chead_gen(bi, 0, 0), chead_gen(bi, 1, 1)]
                        if bi + 1 < nblk_:
                            gens += [chead_gen(bi + 1, 0, 2), chead_gen(bi + 1, 1, 3)]
                        for pb in (bi - 2, bi - 1):
                            if pb >= 0:
                                gens.append(cpost_gen(pb))
                        run_ilc(gens)
                    last0 = ((nblk_ - 1) // 2) * 2
                    run_ilc([cpost_gen(pb) for pb in range(last0, nblk_)])
                    P.dma("sp", "ysst%d" % (pr % 2), lambda e, ys_t=ys_t, pr=pr: e.dma_start(out=YTD[1024 + pr * 128:1024 + (pr + 1) * 128, :], in_=ys_t[:]), reads=[ysk], writes=["YTD%d" % (8 + pr)])
            P.barrier(exclude=("cv",))

        if "D" in phases:
            with ExitStack() as sd:
                wo = sb("wo", [128, 16, D], BF16, sd)
                yt = sb("ytres", [128, 16, T], BF16, sd)
                g1B = sb("g1B", [128, D], F32, sd)
                b1B = sb("b1B", [128, D], F32, sd)
                P.dma("sp", "bc0", lambda e: e.dma_start(out=g1B[:], in_=bc_d[2]), writes=["g1B"])
                P.dma("sp", "bc1", lambda e: e.dma_start(out=b1B[:], in_=bc_d[3]), writes=["b1B"])
                for k in range(16):
                    P.dma("sp", "ytl%d" % k, lambda e, k=k: e.dma_start(out=yt[:, k, :], in_=YTD[k * 128:(k + 1) * 128, :]),
                          reads=["YTD%d" % k], writes=["yt%d" % k])
                for k0 in range(0, 16, 4):
                    src = w_out[k0 * 128:(k0 + 4) * 128, :].rearrange("(k p) n -> p k n", p=128)
                    P.dma("pool", "wo%d" % (k0 // 4), lambda e, k0=k0, src=src: e.dma_start(out=wo[:, k0:k0 + 4, :], in_=src),
                          writes=["wo%d" % k for k in range(k0, k0 + 4)])
                xts = [sb("dxt%d" % i, [128, D], F32, sd) for i in range(2)]
                hts = [sb("dht%d" % i, [128, 16, 128], BF16, sd) for i in range(2)]
                st = sb("dst", [128, 4, 6], F32, sd)
                mv = sb("dmv", [128, 2], F32, sd)
                rstd = sb("drstd", [128, 1], F32, sd)
                for ti, (t0, n) in enumerate(TT):
                    xt = xts[ti % 2]
                    xk = "dxt%d" % (ti % 2)
                    ht = hts[ti % 2]
                    hk = "dht%d" % (ti % 2)
                    P.dma("sp", "dx%d" % (ti % 2), lambda e, xt=xt, t0=t0, n=n: e.dma_start(out=xt[0:n, :], in_=H0[t0:t0 + n, :]),
                          reads=["H0_%d" % ti], writes=[xk])
                    for g in range(4):
                        for k in range(16):
                            P.op("pe", lambda e, g=g, k=k, t0=t0, n=n: e.matmul(
                                ps[g][0:n, :], lhsT=yt[:, k, t0:t0 + n], rhs=wo[:, k, g * 512:(g + 1) * 512],
                                start=(k == 0), stop=(k == 15)), reads=["yt%d" % k, "wo%d" % k], writes=[PSK[g]])
                        P.op("dve", lambda e, g=g, xt=xt, n=n: e.scalar_tensor_tensor(
                            out=xt[0:n, g * 512:(g + 1) * 512], in0=xt[0:n, g * 512:(g + 1) * 512], scalar=ALPHA,
                            in1=ps[g][0:n, :], op0=ALU.mult, op1=ALU.add), reads=[xk, PSK[g]], writes=[xk])
                    layer_norm_tile("pd", xt, n, st, mv, rstd, [xk])
                    P.op("pool", lambda e, xt=xt, n=n: e.tensor_tensor(out=xt[0:n, :], in0=xt[0:n, :], in1=g1B[0:n, :], op=ALU.mult),
                         reads=[xk, "g1B"], writes=[xk])
                    P.op("dve", lambda e, xt=xt, n=n: e.tensor_tensor(out=xt[0:n, :], in0=xt[0:n, :], in1=b1B[0:n, :], op=ALU.add),
                         reads=[xk, "b1B"], writes=[xk])
                    P.dma("sp", "h1st%d" % (ti % 2), lambda e, xt=xt, t0=t0, n=n: e.dma_start(out=H1[t0:t0 + n, :], in_=xt[0:n, :]),
                          reads=[xk], writes=["H1_%d" % ti])
                    for gi in range(4):
                        bank = 4 + gi % 2
                        for j in range(4):
                            c = gi * 4 + j
                            P.op("pe", lambda e, xt=xt, n=n, c=c, j=j, bank=bank: e.transpose(
                                out=ps[bank][:, j * 128:j * 128 + n], in_=xt[0:n, c * 128:(c + 1) * 128], identity=ident[0:n, 0:n]),
                                reads=[xk, "ident"], writes=[PSK[bank]])
                        copy_ps(evq(), ht[:, gi * 4:gi * 4 + 4, 0:n],
                                ps[bank][:, :].rearrange("p (j m) -> p j m", m=128)[:, :, 0:n], [PSK[bank]], [hk + "_%d" % gi])
                    for k0 in range(0, 16, 4):
                        dst = H1T[k0 * 128:(k0 + 4) * 128, t0:t0 + n].rearrange("(k p) t -> p k t", p=128)
                        P.dma("sp", "h1t%d" % (ti % 2), lambda e, ht=ht, k0=k0, n=n, dst=dst: e.dma_start(out=dst, in_=ht[:, k0:k0 + 4, 0:n]),
                              reads=[hk + "_%d" % (k0 // 4)], writes=["H1T_%d_%d" % (ti, k0)])
            P.barrier(exclude=("cv",))

        if "E" in phases:
            P.barrier()
            with ExitStack() as se:
                g2B = sb("g2B", [128, D], F32, se)
                b2B = sb("b2B", [128, D], F32, se)
                P.dma("sp", "bc0", lambda e: e.dma_start(out=g2B[:], in_=bc_d[4]), writes=["g2B"])
                P.dma("sp", "bc1", lambda e: e.dma_start(out=b2B[:], in_=bc_d[5]), writes=["b2B"])
                NTOK = 528
                hs = sb("ehs", [128, 16, NTOK], BF16, se)
                acc = sb("eacc", [128, 5, D], F32, se)
                carry = sb("ecarry", [128, 44, 2], F32, se)
                P.op("pool", lambda e: e.memset(carry[:].rearrange("p a b -> p (a b)"), 0.0), writes=["carry%d" % i for i in range(44)])
                wus = [sb("ewu%d" % i, [128, 16, 1024], BF16, se) for i in range(2)]
                wds = [sb("ewd%d" % i, [128, 4, D], BF16, se) for i in range(2)]
                gs = [sb("egs%d" % i, [128, NTOK + 2], F32, se) for i in range(2)]
                vs = [sb("evs%d" % i, [128, NTOK], F32, se) for i in range(2)]
                tb = [sb("etb%d" % i, [128, NTOK], F32, se) for i in range(2)]
                ab = [sb("eab%d" % i, [128, 4, NTOK], BF16, se) for i in range(2)]
                st = sb("est", [128, 4, 6], F32, se)
                mv = sb("emv", [128, 2], F32, se)
                rstd = sb("erstd", [128, 1], F32, se)
                STS = [(0, [(0, 16), (16, 512)], TT[0:5])] + [(16 + 512 * i, [(16 + 512 * i, 512)], TT[1 + 4 * i:5 + 4 * i]) for i in range(1, 4)]
                blk = 0
                cidx = 0
                dbank = 0
                for sti, (ts, groups, tiles) in enumerate(STS[:dbg.get("nst", 4)]):
                    ntok = sum(n for _, n in groups)
                    for k0 in range(0, 16, 4):
                        src = H1T[k0 * 128:(k0 + 4) * 128, ts:ts + ntok].rearrange("(k p) t -> p k t", p=128)
                        P.dma("sp", "ehs", lambda e, k0=k0, src=src, ntok=ntok: e.dma_start(out=hs[:, k0:k0 + 4, 0:ntok], in_=src),
                              reads=["H1T_%d_%d" % (TT.index(tl), k0) for tl in tiles], writes=["hs"])
                    for li, (t0, n) in enumerate(tiles):
                        P.dma("sp", "eacc%d" % li, lambda e, li=li, t0=t0, n=n: e.dma_start(out=acc[0:n, li, :], in_=H1[t0:t0 + n, :]),
                              reads=["H1_%d" % (TT.index((t0, n)))], writes=["acc%d" % li])
                        P.op("pool", lambda e, li=li, n=n: e.tensor_scalar(out=acc[0:n, li, :], in0=acc[0:n, li, :], scalar1=ALPHA, scalar2=None, op0=ALU.mult),
                             reads=["acc%d" % li], writes=["acc%d" % li])
                    for fb in range(dbg.get("nfb", 11)):
                        wu = wus[blk % 2]; wuk = "ewu%d" % (blk % 2)
                        wd = wds[blk % 2]; wdk = "ewd%d" % (blk % 2)
                        abt = ab[blk % 2]; abk = "eab%d" % (blk % 2)
                        P.dma("sp", "wu%d" % (blk % 2), lambda e, wu=wu, fb=fb: e.dma_start(out=wu[:, :, :], in_=WUB[fb]), reads=["WUB%d" % fb], writes=[wuk])
                        P.dma("sp", "wd%d" % (blk % 2), lambda e, wd=wd, fb=fb: e.dma_start(out=wd[:, :, :], in_=WDB[fb]), reads=["WDB%d" % fb], writes=[wdk])
                        blk += 1
                        for c in range(4):
                            fc = fb * 4 + c
                            g_ = gs[cidx % 2]; gk = "egs%d" % (cidx % 2)
                            v_ = vs[cidx % 2]; vk = "evs%d" % (cidx % 2)
                            t_ = tb[cidx % 2]; tk = "etb%d" % (cidx % 2)
                            cidx += 1
                            P.op("pool", lambda e, g_=g_, fc=fc: e.tensor_copy(out=g_[:, 0:2], in_=carry[:, fc, :]), reads=["carry%d" % fc], writes=[gk + "c"])
                            off = 0
                            gkeys = []
                            vkeys = []
                            for qi, (t0, n) in enumerate(groups):
                                lo = t0 - ts
                                for k in range(16):
                                    P.op("pe", lambda e, wu=wu, k=k, c=c, lo=lo, n=n: e.matmul(
                                        ps[0][:, 0:n], lhsT=wu[:, k, c * 128:(c + 1) * 128], rhs=hs[:, k, lo:lo + n],
                                        start=(k == 0), stop=(k == 15)), reads=[wuk, "hs"], writes=[PSK[0]])
                                P.op("act", lambda e, g_=g_, lo=lo, n=n: e.activation(out=g_[:, 2 + lo:2 + lo + n], in_=ps[0][:, 0:n], func=AF.Copy),
                                     reads=[PSK[0]], writes=[gk + "_%d" % qi])
                                gkeys.append(gk + "_%d" % qi)
                                for k in range(16):
                                    P.op("pe", lambda e, wu=wu, k=k, c=c, lo=lo, n=n: e.matmul(
                                        ps[1][:, 0:n], lhsT=wu[:, k, 512 + c * 128:512 + (c + 1) * 128], rhs=hs[:, k, lo:lo + n],
                                        start=(k == 0), stop=(k == 15)), reads=[wuk, "hs"], writes=[PSK[1]])
                                P.op("dve", lambda e, v_=v_, lo=lo, n=n: e.tensor_copy(out=v_[:, lo:lo + n], in_=ps[1][:, 0:n]),
                                     reads=[PSK[1]], writes=[vk + "_%d" % qi])
                                vkeys.append(vk + "_%d" % qi)
                            gall = gkeys + [gk + "c"]
                            P.op("act", lambda e, g_=g_, t_=t_, fc=fc, ntok=ntok: e.activation(
                                out=t_[:, 0:ntok], in_=g_[:, 2:2 + ntok], func=AF.Identity, scale=cc("cw2", fc), bias=cc("cb", fc)),
                                reads=gall + ["cst"], writes=[tk])
                            P.op("dve", lambda e, g_=g_, t_=t_, fc=fc, ntok=ntok: e.scalar_tensor_tensor(
                                out=t_[:, 0:ntok], in0=g_[:, 1:1 + ntok], scalar=cc("cw1", fc), in1=t_[:, 0:ntok], op0=ALU.mult, op1=ALU.add),
                                reads=gall + [tk, "cst"], writes=[tk])
                            P.op("dve", lambda e, g_=g_, t_=t_, fc=fc, ntok=ntok: e.scalar_tensor_tensor(
                                out=t_[:, 0:ntok], in0=g_[:, 0:ntok], scalar=cc("cw0", fc), in1=t_[:, 0:ntok], op0=ALU.mult, op1=ALU.add),
                                reads=gall + [tk, "cst"], writes=[tk])
                            P.op("pool", lambda e, g_=g_, fc=fc, ntok=ntok: e.tensor_copy(out=carry[:, fc, :], in_=g_[:, ntok:ntok + 2]),
                                 reads=gall, writes=["carry%d" % fc])
                            P.op("act", lambda e, t_=t_, ntok=ntok: e.activation(out=t_[:, 0:ntok], in_=t_[:, 0:ntok], func=AF.Silu),
                                 reads=[tk], writes=[tk])
                            P.op("pool", lambda e, t_=t_, v_=v_, abt=abt, c=c, ntok=ntok: e.tensor_tensor(
                                out=abt[:, c, 0:ntok], in0=t_[:, 0:ntok], in1=v_[:, 0:ntok], op=ALU.mult),
                                reads=[tk] + vkeys, writes=[abk + "_%d" % c])
                        for li, (t0, n) in enumerate(tiles):
                            lo = t0 - ts
                            for g in range(4):
                                bank = 4 + dbank % 4
                                dbank += 1
                                for c in range(4):
                                    P.op("pe", lambda e, abt=abt, wd=wd, c=c, lo=lo, n=n, g=g, bank=bank: e.matmul(
                                        ps[bank][0:n, :], lhsT=abt[:, c, lo:lo + n], rhs=wd[:, c, g * 512:(g + 1) * 512],
                                        start=(c == 0), stop=(c == 3)), reads=[abk + "_%d" % c, wdk], writes=[PSK[bank]])
                                P.op("dve", lambda e, li=li, n=n, g=g, bank=bank: e.tensor_tensor(
                                    out=acc[0:n, li, g * 512:(g + 1) * 512], in0=acc[0:n, li, g * 512:(g + 1) * 512], in1=ps[bank][0:n, :], op=ALU.add),
                                    reads=["acc%d" % li, PSK[bank]], writes=["acc%d" % li])
                    for li, (t0, n) in enumerate(tiles):
                        if t0 < NMETA:
                            continue
                        at = acc[:, li, :]
                        layer_norm_tile("pe", at, n, st, mv, rstd, ["acc%d" % li])
                        P.op("pool", lambda e, at=at, n=n: e.tensor_tensor(out=at[0:n, :], in0=at[0:n, :], in1=g2B[0:n, :], op=ALU.mult),
                             reads=["acc%d" % li, "g2B"], writes=["acc%d" % li])
                        P.op("dve", lambda e, at=at, n=n: e.tensor_tensor(out=at[0:n, :], in0=at[0:n, :], in1=b2B[0:n, :], op=ALU.add),
                             reads=["acc%d" % li, "b2B"], writes=["acc%d" % li])
                        P.dma("sp", "ost%d" % li, lambda e, at=at, t0=t0, n=n: e.dma_start(out=out_d[t0 - NMETA:t0 - NMETA + n, :], in_=at[0:n, :]),
                              reads=["acc%d" % li], writes=["out%d" % t0])
            P.barrier(exclude=("cv",))

        P.emit(final_wait_chans=[c for c in P.chan_order])
    nc.used_inputs = used_inputs
    nc.prog_stats = {e: len(v) for e, v in P.ops.items()}
    return nc


def _pc(v, n):
    return np.ascontiguousarray(np.asarray(v, np.float32).reshape(n, 128).T)


def host_consts(inp):
    cst = np.zeros((128, NCST), np.float32)

    def put(name, arr):
        cst[:, CST[name]:CST[name] + arr.shape[1]] = arr

    put("emb_g", _pc(inp["emb_ln_g"], 16)); put("emb_b", _pc(inp["emb_ln_b"], 16))
    put("ln1_g", _pc(inp["ln1_g"][0], 16)); put("ln1_b", _pc(inp["ln1_b"][0], 16))
    put("ln2_g", _pc(inp["ln2_g"][0], 16)); put("ln2_b", _pc(inp["ln2_b"][0], 16))
    mu = np.zeros(27 * 128, np.float32)
    mu[:3360] = inp["rwkv_mu"][0]
    put("mu", _pc(mu, 27))
    for nm, key in [("w0", "rwkv_w0"), ("a0", "rwkv_a0"), ("k_k", "rwkv_k_k"), ("k_a", "rwkv_k_a"),
                    ("gn_g", "rwkv_gn_g"), ("gn_b", "rwkv_gn_b"), ("sb_g", "sb_norm_g")]:
        put(nm, _pc(inp[key][0], 8))
    put("r_k", _pc(inp["rwkv_r_k"][0].reshape(-1), 8))
    cw = inp["ffn_conv_w"][0]
    put("cw0", _pc(cw[0], 44)); put("cw1", _pc(cw[1], 44)); put("cw2", _pc(cw[2], 44))
    put("cb", _pc(inp["ffn_conv_b"][0], 44))
    bc = np.stack([np.broadcast_to(np.asarray(v, np.float32)[None, :], (128, D)) for v in
                   [inp["emb_ln_g"], inp["emb_ln_b"], inp["ln1_g"][0], inp["ln1_b"][0], inp["ln2_g"][0], inp["ln2_b"][0]]])
    return cst, np.ascontiguousarray(bc)


def make_in_maps(inp):
    inp = {k: np.asarray(v) for k, v in inp.items()}
    cst, bc = host_consts(inp)
    shared = dict(cst=cst, bc=bc,
                  w_in=np.ascontiguousarray(inp["w_in"][0], dtype=np.float32),
                  w2a2=np.ascontiguousarray(np.concatenate([inp["rwkv_w2"][0], inp["rwkv_a2"][0]], 0), dtype=np.float32),
                  g2=np.ascontiguousarray(inp["rwkv_g2"][0], dtype=np.float32),
                  w_out=np.ascontiguousarray(inp["w_out"][0], dtype=np.float32),
                  w_up=np.ascontiguousarray(inp["ffn_w_up"][0], dtype=np.float32),
                  w_dn=np.ascontiguousarray(inp["ffn_w_down"][0], dtype=np.float32))
    maps = []
    for b in range(NCORES):
        m = dict(shared)
        m["xcat"] = np.ascontiguousarray(np.concatenate([inp["meta_tokens"], inp["x"][b]], 0), dtype=np.float32)
        maps.append(m)
    return maps


_NC_CACHE = {}


def kernel(**inputs):
    if "nc" not in _NC_CACHE:
        _NC_CACHE["nc"] = build_nc()
    nc = _NC_CACHE["nc"]
    maps = make_in_maps(inputs)
    res = run_bass_kernel_spmd(nc, maps, core_ids=list(range(NCORES)))
    return np.stack([np.asarray(r["out"], np.float32) for r in res.results], 0)
```

```python
import numpy as np
from contextlib import ExitStack
import concourse.bass as bass
import concourse.mybir as mybir
from concourse.bass_utils import run_bass_kernel_spmd

F32 = mybir.dt.float32
BF16 = mybir.dt.bfloat16
AF = mybir.ActivationFunctionType
ALU = mybir.AluOpType
AX = mybir.AxisListType

D = 2048
T = 2064
NMETA = 16
DFF = 5632
NCORES = 8
ALPHA = 2.0 ** 0.25
TT = [(0, 16)] + [(16 + 128 * i, 128) for i in range(16)]
TG = [(0, 16)] + [(16 + 512 * i, 512) for i in range(4)]
CH = [(128 * i, 128) for i in range(24)] + [(3072, 128), (3200, 128), (3328, 32)] + \
     [(3360 + 128 * i, 128) for i in range(24)]
NCH = len(CH)

ENGS = ["pe", "act", "dve", "pool", "sp"]


class Prog:
    def __init__(self, nc):
        self.nc = nc
        self.ops = {e: [] for e in ENGS}
        self.lastw = {}
        self.readers = {}
        self.known = {e: {} for e in ENGS}
        self.chan_cnt = {}
        self.chan_order = []
        self.pending = {e: [] for e in ENGS}

    def _deps(self, eng, reads, writes):
        idx = len(self.ops[eng])
        deps = {}

        def add(tok, raw):
            src, i = tok
            if src == eng:
                if eng in ("pe", "sp"):
                    return
            if self.known[eng].get(src, -1) >= i:
                return
            if deps.get(src, -1) < i:
                deps[src] = i

        for tok in self.pending[eng]:
            if tok[0] != eng:
                add(tok, True)
        self.pending[eng] = []
        for r in reads:
            w = self.lastw.get(r)
            if w is not None:
                add(w, True)
            if r.startswith("ps") and len(r) == 3:
                for tok in self.readers.get(r, {}).items():
                    if tok[0] != eng:
                        add(tok, False)
        for r in writes:
            w = self.lastw.get(r)
            if w is not None:
                add(w, False)
            for tok in self.readers.get(r, {}).items():
                add(tok, False)
        for src, i in deps.items():
            self.known[eng][src] = i
            if not src.startswith("dma:"):
                self.ops[src][i]["signal"] = True
        return idx, list(deps.items())

    def op(self, eng, fn, reads=(), writes=()):
        idx, deps = self._deps(eng, reads, writes)
        self.ops[eng].append(dict(fn=fn, deps=deps, signal=False, chan=None))
        tok = (eng, idx)
        for r in reads:
            self.readers.setdefault(r, {})[eng] = idx
        for r in writes:
            self.lastw[r] = tok
            self.readers[r] = {}
        return tok

    def dma(self, queue, chan, fn, reads=(), writes=()):
        idx, deps = self._deps(queue, reads, writes)
        if chan not in self.chan_cnt:
            self.chan_cnt[chan] = 0
            self.chan_order.append(chan)
        self.chan_cnt[chan] += 1
        cnt = self.chan_cnt[chan]
        self.ops[queue].append(dict(fn=fn, deps=deps, signal=False, chan=chan))
        tok = ("dma:" + chan, cnt)
        for r in reads:
            self.readers.setdefault(r, {})["dma:" + chan] = cnt
        for r in writes:
            self.lastw[r] = tok
            self.readers[r] = {}
        return tok

    def barrier(self, exclude=()):
        toks = []
        for e in ["pe", "act", "dve", "pool"]:
            for i in range(len(self.ops[e]) - 1, -1, -1):
                if self.ops[e][i]["chan"] is None:
                    self.ops[e][i]["signal"] = True
                    toks.append((e, i))
                    break
        for c in self.chan_order:
            if c not in exclude:
                toks.append(("dma:" + c, self.chan_cnt[c]))
        for e in ENGS:
            self.pending[e] = list(toks)

    def emit(self, final_wait_chans=()):
        nc = self.nc
        with ExitStack() as es:
            sems = {}
            for e in ["pe", "act", "dve", "pool"]:
                sems[e] = es.enter_context(nc.semaphore("s_" + e))
            for c in self.chan_order:
                sems["dma:" + c] = es.enter_context(nc.semaphore("d_" + c))
            cnts = {}
            for e in ["pe", "act", "dve", "pool"]:
                c = 0
                arr = []
                for o in self.ops[e]:
                    if o["signal"] and o["chan"] is None:
                        c += 1
                    arr.append(c)
                cnts[e] = arr

            def run(ename, eng):
                for o in self.ops[ename]:
                    for src, i in o["deps"]:
                        if src.startswith("dma:"):
                            eng.wait_ge(sems[src], 16 * i)
                        else:
                            eng.wait_ge(sems[src], cnts[src][i])
                    ins = o["fn"](eng)
                    if o["chan"] is not None:
                        ins.then_inc(sems["dma:" + o["chan"]], 16)
                    elif o["signal"]:
                        ins.then_inc(sems[ename], 1)
                if ename == "sp":
                    for c in final_wait_chans:
                        eng.wait_ge(sems["dma:" + c], 16 * self.chan_cnt[c])

            block = es.enter_context(nc.Block())

            @block.tensor
            def _(e):
                run("pe", e)

            @block.scalar
            def _(e):
                run("act", e)

            @block.vector
            def _(e):
                run("dve", e)

            @block.gpsimd
            def _(e):
                run("pool", e)

            @block.sync
            def _(e):
                run("sp", e)


def _cst_layout():
    off = {}
    c = 0
    for name, n in [("emb_g", 16), ("emb_b", 16), ("ln1_g", 16), ("ln1_b", 16), ("ln2_g", 16), ("ln2_b", 16),
                    ("mu", 27), ("w0", 8), ("a0", 8), ("k_k", 8), ("k_a", 8), ("r_k", 8), ("gn_g", 8),
                    ("gn_b", 8), ("sb_g", 8), ("cw0", 44), ("cw1", 44), ("cw2", 44), ("cb", 44)]:
        off[name] = c
        c += n
    return off, c


CST, NCST = _cst_layout()


def build_nc(dbg=None):
    dbg = dbg or {}
    phases = dbg.get("phases", "0ABCDE")
    taps = dbg.get("taps", [])
    inject = dbg.get("inject", [])
    nc = bass.Bass("TRN2", target_bir_lowering=False)
    P = Prog(nc)

    used_inputs = []

    def din(name, shape, dt=F32, need="0ABCDE"):
        if not any(p in phases for p in need):
            return None
        used_inputs.append(name)
        return nc.dram_tensor(name, list(shape), dt, kind="ExternalInput").ap()

    def dscratch(name, shape, dt):
        if name in inject:
            used_inputs.append(name)
            return nc.dram_tensor(name, list(shape), dt, kind="ExternalInput").ap()
        if name in taps:
            return nc.dram_tensor(name, list(shape), dt, kind="ExternalOutput").ap()
        return nc.dram_tensor(name, list(shape), dt, kind="Internal").ap()

    xcat = din("xcat", [T, D], need="0")
    cst_d = din("cst", [128, NCST])
    bc_d = din("bc", [6, 128, D])
    w_in = din("w_in", [D, 6432], need="A")
    w2a2_d = din("w2a2", [128, 1024], need="B")
    g2_d = din("g2", [160, 1024], need="B")
    w_out = din("w_out", [D, D], need="D")
    w_up = din("w_up", [D, 2 * DFF], need="E")
    w_dn = din("w_dn", [DFF, D], need="E")
    out_d = nc.dram_tensor("out", [T - NMETA, D], F32, kind="ExternalOutput").ap()

    H0 = dscratch("H0", [T, D], F32)
    PT = dscratch("PT", [NCH * 128, T], F32)
    YTD = dscratch("YTD", [D, T], BF16)
    H1 = dscratch("H1", [T, D], F32)
    H1T = dscratch("H1T", [D, T], BF16)
    WUB = dscratch("WUB", [11, 128, 16, 1024], BF16)
    WDB = dscratch("WDB", [11, 128, 4, D], BF16)

    es = ExitStack()
    with es:
        def sb(name, shape, dt, st=es):
            return st.enter_context(nc.sbuf_tensor(name, list(shape), dt))

        ps = [es.enter_context(nc.psum_tensor("ps%d" % i, [128, 512], F32)) for i in range(8)]
        PSK = ["ps%d" % i for i in range(8)]

        cst = sb("cst_t", [128, NCST], F32)
        ident = sb("ident", [128, 128], F32)
        P.dma("sp", "cst", lambda e: e.dma_start(out=cst[:], in_=cst_d), writes=["cst"])
        P.op("pool", lambda e: e.memset(ident[:], 0.0), writes=["ident"])
        P.op("pool", lambda e: e.affine_select(out=ident[:], in_=ident[:], pattern=[[-1, 128]],
                                               compare_op=ALU.not_equal, fill=1.0, base=0, channel_multiplier=1),
             reads=["ident"], writes=["ident"])
        epsb = sb("epsb", [128, 4], F32)
        for j, v in enumerate([1e-5, 64e-5, 1e-6, 1.0]):
            P.op("pool", lambda e, j=j, v=v: e.memset(epsb[:, j:j + 1], v), writes=["epsb"])

        if "E" in phases:
            for fb in range(11):
                for half, c0 in enumerate([fb * 512, DFF + fb * 512]):
                    src = w_up[:, c0:c0 + 512].rearrange("(k p) n -> p k n", p=128)
                    P.dma("pool", "cv", lambda e, fb=fb, half=half, src=src: e.dma_start(out=WUB[fb, :, :, half * 512:(half + 1) * 512], in_=src),
                          writes=["WUB%d" % fb])
                srcd = w_dn[fb * 512:(fb + 1) * 512, :].rearrange("(k p) n -> p k n", p=128)
                P.dma("pool", "cv", lambda e, fb=fb, srcd=srcd: e.dma_start(out=WDB[fb], in_=srcd), writes=["WDB%d" % fb])

        def cc(name, j=0, rows=128):
            o = CST[name] + j
            return cst[0:rows, o:o + 1]

        rr = {"n": 0}

        def evq():
            rr["n"] += 1
            if dbg.get("evq"):
                return dbg["evq"]
            return "act" if rr["n"] % 2 else "dve"

        def copy_ps(eng, out, in_, reads, writes):
            if eng == "act":
                P.op("act", lambda e: e.activation(out=out, in_=in_, func=AF.Copy), reads=reads, writes=writes)
            else:
                P.op(eng, lambda e: e.tensor_copy(out=out, in_=in_), reads=reads, writes=writes)

        def wload(queue_chan, dst, src_rows_ap, nk, ncols, keys):
            for k0 in range(0, nk, 4):
                k1 = min(nk, k0 + 4)
                src = src_rows_ap[k0 * 128:k1 * 128, :].rearrange("(k p) n -> p k n", p=128)
                P.dma("pool", queue_chan, lambda e, k0=k0, k1=k1, src=src: e.dma_start(out=dst[:, k0:k1, 0:ncols], in_=src),
                      writes=keys)

        def layer_norm_tile(pref, xt, rows, st, mv, rstd, reads):
            for i in range(4):
                P.op("dve", lambda e, i=i: e.bn_stats(out=st[0:rows, i, :], in_=xt[0:rows, i * 512:(i + 1) * 512]),
                     reads=reads, writes=[pref + "st%d" % i])
            P.op("dve", lambda e: e.bn_aggr(out=mv[0:rows, :], in_=st[0:rows].rearrange("p a b -> p (a b)")),
                 reads=[pref + "st%d" % i for i in range(4)], writes=[pref + "mv"])
            P.op("act", lambda e: e.activation(out=rstd[0:rows, :], in_=mv[0:rows, 1:2], func=AF.Sqrt,
                                               bias=epsb[0:rows, 0:1], scale=1.0),
                 reads=[pref + "mv", "epsb"], writes=[pref + "rstd"])
            P.op("dve", lambda e: e.reciprocal(out=rstd[0:rows, :], in_=rstd[0:rows, :]),
                 reads=[pref + "rstd"], writes=[pref + "rstd"])
            P.op("dve", lambda e: e.tensor_scalar(out=xt[0:rows, :], in0=xt[0:rows, :], scalar1=mv[0:rows, 0:1],
                                                  scalar2=rstd[0:rows, :], op0=ALU.subtract, op1=ALU.mult),
                 reads=reads + [pref + "mv", pref + "rstd"], writes=reads)

        if "0" in phases or "A" in phases:
            with ExitStack() as s1:
                h0T = sb("h0T", [128, 16, T], BF16, s1)
                with ExitStack() as s0:
                  if "0" in phases:
                      gB = sb("gB", [128, D], F32, s0)
                      bB = sb("bB", [128, D], F32, s0)
                      P.dma("sp", "bc0", lambda e: e.dma_start(out=gB[:], in_=bc_d[0]), writes=["gB"])
                      P.dma("sp", "bc1", lambda e: e.dma_start(out=bB[:], in_=bc_d[1]), writes=["bB"])
                      xts = [sb("xt%d" % i, [128, D], F32, s0) for i in range(2)]
                      st = sb("st", [128, 4, 6], F32, s0)
                      mv = sb("mv", [128, 2], F32, s0)
                      rstd = sb("rstd", [128, 1], F32, s0)
                      for ti, (t0, n) in enumerate(TT[dbg.get('tile0', 0):dbg.get('ntiles', 17)]):
                          xt = xts[ti % 2]
                          xk = "xt%d" % (ti % 2)
                          P.dma("sp", "x%d" % (ti % 2), lambda e, xt=xt, t0=t0, n=n: e.dma_start(out=xt[0:n, :], in_=xcat[t0:t0 + n, :]),
                                writes=[xk])
                          layer_norm_tile("p0", xt, n, st, mv, rstd, [xk])
                          P.op(dbg.get("multeng", "pool"), lambda e, xt=xt, n=n: e.tensor_tensor(out=xt[0:n, :], in0=xt[0:n, :], in1=gB[0:n, :], op=ALU.mult),
                               reads=[xk, "gB"], writes=[xk])
                          P.op("dve", lambda e, xt=xt, n=n: e.tensor_tensor(out=xt[0:n, :], in0=xt[0:n, :], in1=bB[0:n, :], op=ALU.add),
                               reads=[xk, "bB"], writes=[xk])
                          if not dbg.get("nostore"):
                              P.dma("sp", "h0st%d" % (ti % 2), lambda e, xt=xt, t0=t0, n=n: e.dma_start(out=H0[t0:t0 + n, :], in_=xt[0:n, :]),
                                    reads=[xk], writes=["H0_%d" % ti])
                          if dbg.get("notr"):
                              continue
                          for gi in range(4):
                              bank = gi % 2
                              for j in range(4):
                                  c = gi * 4 + j
                                  P.op("pe", lambda e, xt=xt, n=n, c=c, j=j, bank=bank: e.transpose(
                                      out=ps[bank][:, j * 128:j * 128 + n], in_=xt[0:n, c * 128:(c + 1) * 128], identity=ident[0:n, 0:n]),
                                      reads=[xk, "ident"], writes=[PSK[bank]])
                              copy_ps(evq(), h0T[:, gi * 4:gi * 4 + 4, t0:t0 + n],
                                      ps[bank][:, :].rearrange("p (j m) -> p j m", m=128)[:, :, 0:n], [PSK[bank]],
                                      ["h0T%d" % (gi * 4 + j) for j in range(4)])
                P.barrier(exclude=("cv",))
                with ExitStack() as sa:
                  if "A" in phases:
                      wb = [sb("winb%d" % i, [128, 16, 256], BF16, sa) for i in range(2)]
                      stg = [sb("stg%d" % i, [128, T + 1], F32, sa) for i in range(2)]
                      stm = [sb("stm%d" % i, [128, T], F32, sa) for i in range(2)]
                      omm = sb("omm", [128, 27], F32, sa)
                      P.op("dve", lambda e: e.tensor_scalar(out=omm[:], in0=cst[:, CST["mu"]:CST["mu"] + 27], scalar1=-1.0, scalar2=1.0,
                                                            op0=ALU.mult, op1=ALU.add), reads=["cst"], writes=["omm"])
                      for i in range(2):
                          P.op("pool", lambda e, i=i: e.memset(stg[i][:, 0:1], 0.0), writes=["stg%d" % i])
                      groups = []
                      j = 0
                      while j < NCH:
                          if j + 1 < NCH and CH[j][1] == 128 and CH[j + 1][1] == 128 and CH[j + 1][0] == CH[j][0] + 128:
                              groups.append([j, j + 1])
                              j += 2
                          else:
                              groups.append([j])
                              j += 1
                      bi = 0
                      for gi, grp in enumerate(groups):
                          wbuf = wb[gi % 2]
                          wk = "winb%d" % (gi % 2)
                          c0 = CH[grp[0]][0]
                          ncols = sum(CH[j][1] for j in grp)
                          wload("win%d" % (gi % 2), wbuf, w_in[:, c0:c0 + ncols], 16, ncols, [wk])
                          for jj, j in enumerate(grp):
                              wd = CH[j][1]
                              so = stg[j % 2]
                              sk = "stg%d" % (j % 2)
                              sks = [sk + "_%d" % q for q in range(len(TG))]
                              for q, (t0, n) in enumerate(TG):
                                  bank = bi % 4
                                  bi += 1
                                  for k in range(16):
                                      P.op("pe", lambda e, wbuf=wbuf, k=k, jj=jj, wd=wd, t0=t0, n=n, bank=bank: e.matmul(
                                          ps[bank][0:wd, 0:n], lhsT=wbuf[:, k, jj * 128:jj * 128 + wd], rhs=h0T[:, k, t0:t0 + n],
                                          start=(k == 0), stop=(k == 15)), reads=[wk, "h0T%d" % k], writes=[PSK[bank]])
                                  copy_ps(evq(), so[0:wd, 1 + t0:1 + t0 + n], ps[bank][0:wd, 0:n], [PSK[bank]], [sks[q]])
                              if j < 27:
                                  sm = stm[j % 2]
                                  mk = "stm%d" % (j % 2)
                                  P.op("act", lambda e, so=so, sm=sm, wd=wd, j=j: e.activation(
                                      out=sm[0:wd, :], in_=so[0:wd, 1:T + 1], func=AF.Copy, scale=omm[0:wd, j:j + 1]),
                                      reads=sks + [sk, "omm"], writes=[mk])
                                  P.op("dve", lambda e, so=so, sm=sm, wd=wd, j=j: e.scalar_tensor_tensor(
                                      out=sm[0:wd, :], in0=so[0:wd, 0:T], scalar=cc("mu", j, wd), in1=sm[0:wd, :], op0=ALU.mult, op1=ALU.add),
                                      reads=sks + [sk, mk, "cst"], writes=[mk])
                                  P.dma("sp", "ptst%d" % (j % 2), lambda e, sm=sm, wd=wd, j=j: e.dma_start(out=PT[j * 128:j * 128 + wd, :], in_=sm[0:wd, :]),
                                        reads=[mk], writes=["PT%d" % j])
                              else:
                                  P.dma("sp", "ptsu%d" % (j % 2), lambda e, so=so, wd=wd, j=j: e.dma_start(out=PT[j * 128:j * 128 + wd, :], in_=so[0:wd, 1:T + 1]),
                                        reads=sks + [sk], writes=["PT%d" % j])
            P.barrier(exclude=("cv",))

        if "B" in phases:
            with ExitStack() as sB:
                MU = sb("MU", [128, 128], F32, sB)
                MUI = sb("MUI", [128, 128], F32, sB)
                ML = sb("ML", [128, 128], F32, sB)
                BO = sb("BO", [128, 128], F32, sB)
                ones = sb("onesT", [128, 128], F32, sB)
                for m_, nm, cm, pat, cmp_ in [(MU, "MU", -1, 1, ALU.is_gt), (MUI, "MUI", -1, 1, ALU.is_ge), (ML, "ML", 1, -1, ALU.is_gt)]:
                    P.op("pool", lambda e, m_=m_: e.memset(m_[:], 1.0), writes=[nm])
                    P.op("pool", lambda e, m_=m_, cm=cm, pat=pat, cmp_=cmp_: e.affine_select(
                        out=m_[:], in_=m_[:], pattern=[[pat, 128]], compare_op=cmp_, fill=0.0, base=0, channel_multiplier=cm),
                        reads=[nm], writes=[nm])
                P.op("pool", lambda e: e.memset(BO[:], 0.0), writes=["BO"])
                P.op("pool", lambda e: e.memset(BO[0:64, 0:64], 1.0), reads=["BO"], writes=["BO"])
                P.op("pool", lambda e: e.memset(BO[64:128, 64:128], 1.0), reads=["BO"], writes=["BO"])
                P.op("pool", lambda e: e.memset(ones[:], 1.0), writes=["ones"])
                omka = sb("omka", [128, 8], F32, sB)
                P.op("dve", lambda e: e.tensor_scalar(out=omka[:], in0=cst[:, CST["k_a"]:CST["k_a"] + 8], scalar1=-1.0, scalar2=1.0,
                                                      op0=ALU.mult, op1=ALU.add), reads=["cst"], writes=["omka"])
                W2A2 = sb("W2A2", [128, 1024], BF16, sB)
                G2a = sb("G2a", [128, 1024], BF16, sB)
                G2b = sb("G2b", [32, 1024], BF16, sB)
                P.dma("pool", "lw0", lambda e: e.dma_start(out=W2A2[:], in_=w2a2_d), writes=["W2A2"])
                P.dma("pool", "lw1", lambda e: e.dma_start(out=G2a[:], in_=g2_d[0:128, :]), writes=["G2a"])
                P.dma("pool", "lw2", lambda e: e.dma_start(out=G2b[:], in_=g2_d[128:160, :]), writes=["G2b"])
                LA = sb("LA", [128, T], BF16, sB)
                G0 = sb("G0", [128, T], BF16, sB)
                G1 = sb("G1", [32, T], BF16, sB)
                with ExitStack() as sl:
                    lst = sb("lstage", [128, T], F32, sl)
                    P.dma("sp", "lst", lambda e: e.dma_start(out=lst[:], in_=PT[24 * 128:25 * 128, :]), reads=["PT24"], writes=["lst"])
                    P.op("act", lambda e: e.activation(out=LA[0:64, :], in_=lst[0:64, :], func=AF.Tanh), reads=["lst"], writes=["LA"])
                    P.op("act", lambda e: e.activation(out=LA[64:128, :], in_=lst[64:128, :], func=AF.Copy), reads=["lst"], writes=["LA"])
                    P.dma("sp", "lst", lambda e: e.dma_start(out=lst[:], in_=PT[25 * 128:26 * 128, :]), reads=["PT25"], writes=["lst"])
                    P.op("act", lambda e: e.activation(out=G0[:], in_=lst[:], func=AF.Sigmoid), reads=["lst"], writes=["G0"])
                    P.dma("sp", "lst", lambda e: e.dma_start(out=lst[0:32, :], in_=PT[26 * 128:26 * 128 + 32, :]), reads=["PT26"], writes=["lst"])
                    P.op("act", lambda e: e.activation(out=G1[:], in_=lst[0:32, :], func=AF.Sigmoid), reads=["lst"], writes=["G1"])
                    P.barrier(exclude=("cv",))
                RKV = [[sb("rkv%d_%d" % (i, q), [128, T], F32, sB) for q in range(3)] for i in range(2)]
                LD = sb("LDt", [128, T], F32, sB)
                At = sb("At", [128, T], F32, sB)
                GT = sb("GTt", [128, T], F32, sB)
                YP = [sb("YP%d" % i, [128, T], BF16, sB) for i in range(2)]
                Mst = sb("Mst", [128, 64], F32, sB)
                Mb = sb("Mb", [128, 64], BF16, sB)

                def mk(name, shape, dt, nbuf=2):
                    return [sb("%s_%d" % (name, i), shape, dt, sB) for i in range(nbuf)]

                cum_ = mk("cum", [128, 128], F32); cumx_ = mk("cumx", [128, 128], F32)
                Ep_ = mk("Ep", [128, 128], F32); Em_ = mk("Em", [128, 128], F32); Ex_ = mk("Ex", [128, 128], F32)
                kk_ = mk("kk", [128, 128], F32); sqrk_ = mk("sqrk", [128, 2, 128], F32); rn_ = mk("rn", [128, 128], F32)
                kkn_ = mk("kkn", [128, 128], F32); tf_ = mk("tf", [128, 128], F32); k2_ = mk("k2", [128, 128], F32)
                b_ = mk("bb", [128, 128], F32); ART_ = mk("ART", [128, 2, 128], BF16); KT_ = mk("KT", [128, 128], BF16)
                BT_ = mk("BT", [128, 128], BF16); KH_ = mk("KH", [128, 128], F32); BH_ = mk("BH", [128, 128], F32)
                TOK_ = mk("TOK", [128, 3, 128], BF16); sbc_ = mk("sbc", [128, 128], F32); bon_ = mk("bon", [128, 128], F32, 3)
                Us_ = mk("Us", [128, 128], BF16); ys_ = mk("ys", [128, 128], F32); yT_ = mk("yT", [128, 128], F32)
                gst_ = mk("gst", [128, 2, 6], F32); gmv_ = mk("gmv", [128, 2, 2], F32); grs_ = mk("grs", [128, 2], F32)
                Bm_ = mk("Bm", [128, 128], F32, 4); Am_ = mk("Am", [128, 128], F32, 4)
                ArbT_ = mk("ArbT", [128, 128], BF16); AakT_ = mk("AakT", [128, 128], BF16); ArkT_ = mk("ArkT", [128, 128], BF16)
                Pm_ = mk("Pm", [128, 128], F32); W0s_ = mk("W0s", [128, 64], F32)

                pcnt = 0
                hcnt = 0
                for pr in range(dbg.get("npair", 8)):
                    cs = pr * 128
                    R, Kt, V = RKV[pr % 2]
                    rkk = ["rkv%d_%d" % (pr % 2, q) for q in range(3)]
                    for q, chn in enumerate([pr, 8 + pr, 16 + pr]):
                        P.dma("sp", "rkv%d_%d" % (pr % 2, q), lambda e, q=q, chn=chn, tl=RKV[pr % 2][q]: e.dma_start(out=tl[:], in_=PT[chn * 128:(chn + 1) * 128, :]),
                              reads=["PT%d" % chn], writes=[rkk[q]])
                    yp = YP[pr % 2]
                    ypk = "YP%d" % (pr % 2)
                    if "nchunk" in dbg:
                        P.op("pool", lambda e, yp=yp: e.memset(yp[:], 0.0), writes=[ypk])
                    for (t0, n) in TG:
                        P.op("pe", lambda e, t0=t0, n=n, cs=cs: e.matmul(ps[0][:, 0:n], lhsT=W2A2[0:64, cs:cs + 128], rhs=LA[0:64, t0:t0 + n], start=True, stop=True),
                             reads=["W2A2", "LA"], writes=[PSK[0]])
                        P.op("act", lambda e, t0=t0, n=n, pr=pr: e.activation(out=LD[:, t0:t0 + n], in_=ps[0][:, 0:n], func=AF.Sigmoid, bias=cc("w0", pr), scale=1.0),
                             reads=[PSK[0], "cst"], writes=["LD"])
                        P.op("pe", lambda e, t0=t0, n=n, cs=cs: e.matmul(ps[1][:, 0:n], lhsT=W2A2[64:128, cs:cs + 128], rhs=LA[64:128, t0:t0 + n], start=True, stop=True),
                             reads=["W2A2", "LA"], writes=[PSK[1]])
                        P.op("act", lambda e, t0=t0, n=n, pr=pr: e.activation(out=At[:, t0:t0 + n], in_=ps[1][:, 0:n], func=AF.Sigmoid, bias=cc("a0", pr), scale=1.0),
                             reads=[PSK[1], "cst"], writes=["At"])
                        P.op("pe", lambda e, t0=t0, n=n, cs=cs: e.matmul(ps[2][:, 0:n], lhsT=G2a[:, cs:cs + 128], rhs=G0[:, t0:t0 + n], start=True, stop=False),
                             reads=["G2a", "G0"], writes=[PSK[2]])
                        P.op("pe", lambda e, t0=t0, n=n, cs=cs: e.matmul(ps[2][:, 0:n], lhsT=G2b[0:32, cs:cs + 128], rhs=G1[0:32, t0:t0 + n], start=False, stop=True),
                             reads=["G2b", "G1"], writes=[PSK[2]])
                        P.op("dve", lambda e, t0=t0, n=n: e.tensor_copy(out=GT[:, t0:t0 + n], in_=ps[2][:, 0:n]), reads=[PSK[2]], writes=["GT"])
                    P.op("dve", lambda e: e.tensor_scalar(out=LD[:], in0=LD[:], scalar1=-0.6065306597126334, scalar2=None, op0=ALU.mult),
                         reads=["LD"], writes=["LD"])
                    P.op("pool", lambda e: e.memset(Mst[:], 0.0), writes=["M"])
                    P.op("pool", lambda e: e.memset(Mb[:], 0.0), writes=["Mb"])
                    chunks = TT[:dbg.get("nchunk", 17)]

                    def pre_gen(ci, pr=pr, R=R, Kt=Kt, V=V, rkk=rkk):
                        t0, C = chunks[ci]
                        z = ci % 2
                        sl_ = slice(t0, t0 + C)
                        cum, cumx, Ep, Em, Ex = cum_[z], cumx_[z], Ep_[z], Em_[z], Ex_[z]
                        kk, sqrk, rn, kkn, tf, k2, bb = kk_[z], sqrk_[z], rn_[z], kkn_[z], tf_[z], k2_[z], b_[z]
                        ART, KT, BT, KH, BH, TOK = ART_[z], KT_[z], BT_[z], KH_[z], BH_[z], TOK_[z]
                        sbc, bon = sbc_[z], bon_[ci % 3]
                        K_ = lambda nm: ("bon_%d" % (ci % 3)) if nm == "bon" else "%s_%d" % (nm, z)
                        P.op("dve", lambda e: e.tensor_tensor_scan(out=cum[:, 0:C], data0=ones[:, 0:C], data1=LD[:, sl_], initial=0.0, op0=ALU.mult, op1=ALU.add),
                             reads=["LD", "ones"], writes=[K_("cum")])
                        P.op("pool", lambda e: e.tensor_scalar(out=kk[:, 0:C], in0=Kt[:, sl_], scalar1=cc("k_k", pr), scalar2=None, op0=ALU.mult),
                             reads=[rkk[1], "cst"], writes=[K_("kk")])
                        P.op("dve", lambda e: e.tensor_scalar(out=tf[:, 0:C], in0=At[:, sl_], scalar1=cc("k_a", pr), scalar2=omka[:, pr:pr + 1], op0=ALU.mult, op1=ALU.add),
                             reads=["At", "cst", "omka"], writes=[K_("tf")])
                        yield
                        P.op("pool", lambda e: e.tensor_tensor(out=cumx[:, 0:C], in0=cum[:, 0:C], in1=LD[:, sl_], op=ALU.subtract),
                             reads=[K_("cum"), "LD"], writes=[K_("cumx")])
                        P.op("act", lambda e: e.activation(out=Ep[:, 0:C], in_=cum[:, 0:C], func=AF.Exp), reads=[K_("cum")], writes=[K_("Ep")])
                        P.op("act", lambda e: e.activation(out=Em[:, 0:C], in_=cum[:, 0:C], func=AF.Exp, scale=-1.0), reads=[K_("cum")], writes=[K_("Em")])
                        P.op("pool", lambda e: e.tensor_tensor(out=sqrk[:, 0, 0:C], in0=kk[:, 0:C], in1=kk[:, 0:C], op=ALU.mult),
                             reads=[K_("kk")], writes=[K_("sq")])
                        P.op("dve", lambda e: e.tensor_tensor(out=k2[:, 0:C], in0=Kt[:, sl_], in1=tf[:, 0:C], op=ALU.mult),
                             reads=[rkk[1], K_("tf")], writes=[K_("k2")])
                        yield
                        P.op("act", lambda e: e.activation(out=Ex[:, 0:C], in_=cumx[:, 0:C], func=AF.Exp), reads=[K_("cumx")], writes=[K_("Ex")])
                        P.op("dve", lambda e: e.scalar_tensor_tensor(out=sqrk[:, 1, 0:C], in0=R[:, sl_], scalar=cc("r_k", pr), in1=k2[:, 0:C], op0=ALU.mult, op1=ALU.mult),
                             reads=[rkk[0], K_("k2"), "cst"], writes=[K_("rk")])
                        yield
                        for hf in range(2):
                            P.op("pe", lambda e, hf=hf: e.matmul(ps[0][:, hf * 128:hf * 128 + C], lhsT=BO[:, :], rhs=sqrk[:, hf, 0:C], start=True, stop=True),
                                 reads=["BO", K_("sq") if hf == 0 else K_("rk")], writes=[PSK[0]])
                        P.op("dve", lambda e: e.tensor_tensor(out=ART[:, 1, 0:C], in0=R[:, sl_], in1=Ep[:, 0:C], op=ALU.mult),
                             reads=[rkk[0], K_("Ep")], writes=[K_("RT")])
                        P.op("pool", lambda e: e.tensor_tensor(out=KT[:, 0:C], in0=k2[:, 0:C], in1=Em[:, 0:C], op=ALU.mult),
                             reads=[K_("k2"), K_("Em")], writes=[K_("KT")])
                        yield
                        P.op("act", lambda e: e.activation(out=rn[:, 0:C], in_=ps[0][:, 0:C], func=AF.Ln), reads=[PSK[0]], writes=[K_("rn")])
                        P.op("act", lambda e: e.activation(out=sbc[:, 0:C], in_=ps[0][:, 128:128 + C], func=AF.Copy), reads=[PSK[0]], writes=[K_("sbc")])
                        P.op("act", lambda e: e.activation(out=rn[:, 0:C], in_=rn[:, 0:C], func=AF.Exp, scale=-0.5), reads=[K_("rn")], writes=[K_("rn")])
                        EC = Ep[:, C - 1:C]
                        P.op("dve", lambda e: e.scalar_tensor_tensor(out=KH[:, 0:C], in0=k2[:, 0:C], scalar=EC, in1=Em[:, 0:C], op0=ALU.mult, op1=ALU.mult),
                             reads=[K_("k2"), K_("Em"), K_("Ep")], writes=[K_("KH")])
                        yield
                        P.op("pool", lambda e: e.tensor_tensor(out=bon[:, 0:C], in0=sbc[:, 0:C], in1=V[:, sl_], op=ALU.mult),
                             reads=[K_("sbc"), rkk[2]], writes=[K_("bon")])
                        P.op("dve", lambda e: e.tensor_tensor(out=kkn[:, 0:C], in0=kk[:, 0:C], in1=rn[:, 0:C], op=ALU.mult),
                             reads=[K_("kk"), K_("rn")], writes=[K_("kkn")])
                        yield
                        P.op("pool", lambda e: e.tensor_tensor(out=bb[:, 0:C], in0=kkn[:, 0:C], in1=At[:, sl_], op=ALU.mult),
                             reads=[K_("kkn"), "At"], writes=[K_("bb")])
                        P.op("dve", lambda e: e.scalar_tensor_tensor(out=ART[:, 0, 0:C], in0=kkn[:, 0:C], scalar=-1.0, in1=Ex[:, 0:C], op0=ALU.mult, op1=ALU.mult),
                             reads=[K_("kkn"), K_("Ex")], writes=[K_("AT")])
                        yield
                        P.op("pool", lambda e: e.tensor_tensor(out=BT[:, 0:C], in0=bb[:, 0:C], in1=Em[:, 0:C], op=ALU.mult),
                             reads=[K_("bb"), K_("Em")], writes=[K_("BT")])
                        P.op("dve", lambda e: e.scalar_tensor_tensor(out=BH[:, 0:C], in0=bb[:, 0:C], scalar=EC, in1=Em[:, 0:C], op0=ALU.mult, op1=ALU.mult),
                             reads=[K_("bb"), K_("Em"), K_("Ep")], writes=[K_("BH")])
                        yield
                        for q, (src, rk_) in enumerate([(KH[:, 0:C], K_("KH")), (BH[:, 0:C], K_("BH")), (V[:, sl_], rkk[2])]):
                            P.op("pe", lambda e, q=q, src=src: e.transpose(out=ps[1][0:C, q * 128:(q + 1) * 128], in_=src, identity=ident[:, :]),
                                 reads=[rk_, "ident"], writes=[PSK[1]])
                        yield
                        P.op("act", lambda e: e.activation(out=TOK[0:C].rearrange("p a b -> p (a b)"), in_=ps[1][0:C, 0:384], func=AF.Copy),
                             reads=[PSK[1]], writes=[K_("TOK")])

                    def head_gen(ci, hh):
                        t0, C = chunks[ci]
                        z = ci % 2
                        ART, KT, BT, TOK, Us = ART_[z], KT_[z], BT_[z], TOK_[z], Us_[z]
                        ys = ys_[z]
                        K_ = lambda nm: "%s_%d" % (nm, z)
                        y_ = hh
                        hs_ = slice(hh * 64, hh * 64 + 64)
                        XB, YB, ZB = 2 + hh, 4 + hh, 6 + hh
                        y4 = 2 * (ci % 2) + hh
                        Bm, Am = Bm_[y4], Am_[y4]
                        y4p = (y4 + 2) % 4
                        zz = ci % 2
                        ArbT, AakT, ArkT, Pm, W0s = ArbT_[hh], AakT_[hh], ArkT_[hh], Pm_[hh], W0s_[hh]
                        H_ = lambda nm: "%s_h%d" % (nm, hh)
                        P.op("pe", lambda e: e.matmul(ps[XB][0:C, 0:2 * C].rearrange("p (a c) -> p a c", a=2), lhsT=BT[hs_, 0:C], rhs=ART[hs_, :, 0:C], start=True, stop=True),
                             reads=[K_("BT"), K_("AT"), K_("RT")], writes=[PSK[XB]])
                        P.op("pe", lambda e: e.matmul(ps[XB][0:C, 256:256 + C], lhsT=ART[hs_, 0, 0:C], rhs=BT[hs_, 0:C], start=True, stop=True),
                             reads=[K_("BT"), K_("AT")], writes=[PSK[XB]])
                        P.op("pe", lambda e: e.matmul(ps[YB][0:C, 0:2 * C].rearrange("p (a c) -> p a c", a=2), lhsT=KT[hs_, 0:C], rhs=ART[hs_, :, 0:C], start=True, stop=True),
                             reads=[K_("KT"), K_("AT"), K_("RT")], writes=[PSK[YB]])
                        yield
                        Bk, Ak = "Bm%d" % y4, "Am%d" % y4
                        P.op("dve", lambda e: e.tensor_tensor(out=Bm[0:C, 0:C], in0=ps[XB][0:C, 0:C], in1=MU[0:C, 0:C], op=ALU.mult), reads=[PSK[XB], "MU"], writes=[Bk])
                        P.op("dve", lambda e: e.tensor_tensor(out=Am[0:C, 0:C], in0=ps[XB][0:C, 256:256 + C], in1=ML[0:C, 0:C], op=ALU.mult), reads=[PSK[XB], "ML"], writes=[Ak])
                        yield
                        P.op("pool", lambda e: e.tensor_tensor(out=Pm[0:C, 0:C], in0=Bm[0:C, 0:C], in1=ident[0:C, 0:C], op=ALU.add), reads=[Bk, "ident"], writes=[H_("Pm")])
                        P.op("dve", lambda e: e.tensor_tensor(out=ArbT[0:C, 0:C], in0=ps[XB][0:C, C:2 * C], in1=MUI[0:C, 0:C], op=ALU.mult), reads=[PSK[XB], "MUI"], writes=[H_("ArbT")])
                        L = 7 if C == 128 else 4
                        Acur, Bcur, Akc, Bkc = Am, Bm, Ak, Bk
                        for l in range(1, L):
                            odd = (l % 2) == 1
                            An = Am_[y4p] if odd else Am_[y4]
                            Bn = Bm_[y4p] if odd else Bm_[y4]
                            Ank = "Am%d" % (y4p if odd else y4)
                            Bnk = "Bm%d" % (y4p if odd else y4)
                            P.op("pe", lambda e, Acur=Acur, Bcur=Bcur: e.matmul(ps[ZB][0:C, 0:C], lhsT=Bcur[0:C, 0:C], rhs=Acur[0:C, 0:C], start=True, stop=True),
                                 reads=[Akc, Bkc], writes=[PSK[ZB]])
                            if l < L - 1:
                                P.op("pe", lambda e, Acur=Acur, Bcur=Bcur: e.matmul(ps[ZB][0:C, 128:128 + C], lhsT=Acur[0:C, 0:C], rhs=Bcur[0:C, 0:C], start=True, stop=True),
                                     reads=[Akc, Bkc], writes=[PSK[ZB]])
                            yield
                            if l == 1:
                                P.op("dve", lambda e: e.tensor_tensor(out=AakT[0:C, 0:C], in0=ps[YB][0:C, 0:C], in1=MU[0:C, 0:C], op=ALU.mult), reads=[PSK[YB], "MU"], writes=[H_("AakT")])
                                P.op("dve", lambda e: e.tensor_tensor(out=ArkT[0:C, 0:C], in0=ps[YB][0:C, C:2 * C], in1=MUI[0:C, 0:C], op=ALU.mult), reads=[PSK[YB], "MUI"], writes=[H_("ArkT")])
                            if l < L - 1:
                                P.op("act", lambda e, An=An, Bn=Bn: e.activation(out=An[0:C, 0:C], in_=ps[ZB][0:C, 0:C], func=AF.Copy), reads=[PSK[ZB]], writes=[Ank])
                                P.op("act", lambda e, An=An, Bn=Bn: e.activation(out=Bn[0:C, 0:C], in_=ps[ZB][0:C, 128:128 + C], func=AF.Copy), reads=[PSK[ZB]], writes=[Bnk])
                            else:
                                P.op("act", lambda e, An=An: e.activation(out=An[0:C, 0:C], in_=ps[ZB][0:C, 0:C], func=AF.Copy), reads=[PSK[ZB]], writes=[Ank])
                            yield
                            P.op("pe", lambda e, An=An: e.matmul(ps[YB][0:C, 256:256 + C], lhsT=An[0:C, 0:C], rhs=Pm[0:C, 0:C], start=True, stop=True),
                                 reads=[Ank, H_("Pm")], writes=[PSK[YB]])
                            yield
                            P.op("dve", lambda e: e.tensor_tensor(out=Pm[0:C, 0:C], in0=Pm[0:C, 0:C], in1=ps[YB][0:C, 256:256 + C], op=ALU.add), reads=[H_("Pm"), PSK[YB]], writes=[H_("Pm")])
                            Acur, Bcur, Akc, Bkc = An, Bn, Ank, Bnk
                        yield
                        Vt_h = TOK[0:C, 2, hh * 64:hh * 64 + 64]
                        P.op("pe", lambda e: e.matmul(ps[ZB][0:C, 256:320], lhsT=ART[hs_, 0, 0:C], rhs=Mb[hs_, :], start=True, stop=False),
                             reads=[K_("AT"), "Mb"], writes=[PSK[ZB]])
                        P.op("pe", lambda e: e.matmul(ps[ZB][0:C, 256:320], lhsT=AakT[0:C, 0:C], rhs=Vt_h, start=False, stop=True),
                             reads=[H_("AakT"), K_("TOK")], writes=[PSK[ZB]])
                        yield
                        P.op("act", lambda e: e.activation(out=W0s[0:C, :], in_=ps[ZB][0:C, 256:320], func=AF.Copy), reads=[PSK[ZB]], writes=[H_("W0s")])
                        yield
                        P.op("pe", lambda e: e.matmul(ps[ZB][0:C, 320:384], lhsT=Pm[0:C, 0:C], rhs=W0s[0:C, :], start=True, stop=True),
                             reads=[H_("Pm"), H_("W0s")], writes=[PSK[ZB]])
                        yield
                        P.op("act", lambda e: e.activation(out=Us[0:C, hh * 64:hh * 64 + 64], in_=ps[ZB][0:C, 320:384], func=AF.Copy), reads=[PSK[ZB]], writes=[K_("Us%d" % hh)])
                        yield
                        P.op("pe", lambda e: e.matmul(ps[ZB][0:C, 384:448], lhsT=ART[hs_, 1, 0:C], rhs=Mb[hs_, :], start=True, stop=False),
                             reads=[K_("RT"), "Mb"], writes=[PSK[ZB]])
                        P.op("pe", lambda e: e.matmul(ps[ZB][0:C, 384:448], lhsT=ArbT[0:C, 0:C], rhs=Us[0:C, hh * 64:hh * 64 + 64], start=False, stop=False),
                             reads=[H_("ArbT"), K_("Us%d" % hh)], writes=[PSK[ZB]])
                        P.op("pe", lambda e: e.matmul(ps[ZB][0:C, 384:448], lhsT=ArkT[0:C, 0:C], rhs=Vt_h, start=False, stop=True),
                             reads=[H_("ArkT"), K_("TOK")], writes=[PSK[ZB]])
                        yield
                        P.op("act", lambda e: e.activation(out=ys[0:C, hh * 64:hh * 64 + 64], in_=ps[ZB][0:C, 384:448], func=AF.Copy), reads=[PSK[ZB]], writes=[K_("ys%d" % hh)])

                    def state_step(ci):
                        t0, C = chunks[ci]
                        z = ci % 2
                        TOK, Us, Ep = TOK_[z], Us_[z], Ep_[z]
                        K_ = lambda nm: "%s_%d" % (nm, z)
                        P.op("pe", lambda e: e.matmul(ps[2][:, 384:512], lhsT=TOK[0:C, 1, :], rhs=Us[0:C, :], start=True, stop=False),
                             reads=[K_("TOK"), K_("Us0"), K_("Us1")], writes=[PSK[2]])
                        P.op("pe", lambda e: e.matmul(ps[2][:, 384:512], lhsT=TOK[0:C, 0, :], rhs=TOK[0:C, 2, :], start=False, stop=True),
                             reads=[K_("TOK")], writes=[PSK[2]])
                        for hh in range(2):
                            hs_ = slice(hh * 64, hh * 64 + 64)
                            P.op("dve", lambda e, hs_=hs_, hh=hh: e.scalar_tensor_tensor(out=Mst[hs_, :], in0=Mst[hs_, :], scalar=Ep[hs_, C - 1:C], in1=ps[2][hs_, 384 + hh * 64:384 + hh * 64 + 64], op0=ALU.mult, op1=ALU.add),
                                 reads=["M", K_("Ep"), PSK[2]], writes=["M"])
                        P.op("pool", lambda e: e.tensor_copy(out=Mb[:], in_=Mst[:]), reads=["M"], writes=["Mb"])

                    def post_gen(ci, pr=pr, yp=yp, ypk=ypk):
                        t0, C = chunks[ci]
                        z = ci % 2
                        sl_ = slice(t0, t0 + C)
                        ys, yT, bon = ys_[z], yT_[z], bon_[ci % 3]
                        gst, gmv, grs = gst_[z], gmv_[z], grs_[z]
                        K_ = lambda nm: ("bon_%d" % (ci % 3)) if nm == "bon" else "%s_%d" % (nm, z)
                        for hh in range(2):
                            P.op("dve", lambda e, hh=hh: e.bn_stats(out=gst[0:C, hh, :], in_=ys[0:C, hh * 64:hh * 64 + 64]), reads=[K_("ys%d" % hh)], writes=[K_("gst%d" % hh)])
                        yield
                        for hh in range(2):
                            P.op("dve", lambda e, hh=hh: e.bn_aggr(out=gmv[0:C, hh, :], in_=gst[0:C, hh, :]), reads=[K_("gst%d" % hh)], writes=[K_("gmv%d" % hh)])
                        yield
                        P.op("act", lambda e: e.activation(out=grs[0:C, :], in_=gmv[0:C, :, 1], func=AF.Ln, bias=epsb[0:C, 1:2], scale=1.0),
                             reads=[K_("gmv0"), K_("gmv1"), "epsb"], writes=[K_("grs")])
                        P.op("act", lambda e: e.activation(out=grs[0:C, :], in_=grs[0:C, :], func=AF.Exp, scale=-0.5), reads=[K_("grs")], writes=[K_("grs")])
                        yield
                        for hh in range(2):
                            P.op("dve", lambda e, hh=hh: e.tensor_scalar(out=ys[0:C, hh * 64:hh * 64 + 64], in0=ys[0:C, hh * 64:hh * 64 + 64],
                                                                  scalar1=gmv[0:C, hh, 0:1], scalar2=grs[0:C, hh:hh + 1], op0=ALU.subtract, op1=ALU.mult),
                                 reads=[K_("ys%d" % hh), K_("gmv%d" % hh), K_("grs")], writes=[K_("ys%d" % hh)])
                        yield
                        P.op("pe", lambda e: e.transpose(out=ps[1][:, 384:384 + C], in_=ys[0:C, :], identity=ident[0:C, 0:C]), reads=[K_("ys0"), K_("ys1"), "ident"], writes=[PSK[1]])
                        yield
                        P.op("act", lambda e: e.activation(out=yT[:, 0:C], in_=ps[1][:, 384:384 + C], func=AF.Identity, scale=cc("gn_g", pr), bias=cc("gn_b", pr)),
                             reads=[PSK[1], "cst"], writes=[K_("yT")])
                        yield
                        P.op("pool", lambda e: e.tensor_tensor(out=yT[:, 0:C], in0=yT[:, 0:C], in1=bon[:, 0:C], op=ALU.add), reads=[K_("yT"), K_("bon")], writes=[K_("yT")])
                        P.op("pool", lambda e: e.tensor_tensor(out=yp[:, sl_], in0=yT[:, 0:C], in1=GT[:, sl_], op=ALU.mult), reads=[K_("yT"), "GT"], writes=[ypk])

                    def run_il(gens):
                        gens = list(gens)
                        while gens:
                            for g in list(gens):
                                try:
                                    next(g)
                                except StopIteration:
                                    gens.remove(g)

                    nchk = len(chunks)
                    run_il([pre_gen(0)])
                    for ci in range(nchk):
                        gens = [head_gen(ci, 0), head_gen(ci, 1)]
                        if ci + 1 < nchk:
                            gens.append(pre_gen(ci + 1))
                        if ci >= 1:
                            gens.append(post_gen(ci - 1))
                        run_il(gens)
                        state_step(ci)
                    run_il([post_gen(nchk - 1)])
                    P.dma("sp", "ypst%d" % (pr % 2), lambda e, yp=yp, pr=pr: e.dma_start(out=YTD[pr * 128:(pr + 1) * 128, :], in_=yp[:]), reads=[ypk], writes=["YTD%d" % pr])
            P.barrier(exclude=("cv",))

        if "C" in phases:
            with ExitStack() as sC:
                MUc = sb("MUc", [128, 128], F32, sC)
                MUb = sb("MUb", [128, 128], BF16, sC)
                MLc = sb("MLc", [128, 128], F32, sC)
                onesc = sb("onesc", [128, 128], F32, sC)
                P.op("pool", lambda e: e.memset(MUc[:], 1.0), writes=["MUc"])
                P.op("pool", lambda e: e.affine_select(out=MUc[:], in_=MUc[:], pattern=[[1, 128]], compare_op=ALU.is_gt, fill=0.0, base=0, channel_multiplier=-1),
                     reads=["MUc"], writes=["MUc"])
                P.op("pool", lambda e: e.tensor_copy(out=MUb[:], in_=MUc[:]), reads=["MUc"], writes=["MUb"])
                P.op("pool", lambda e: e.memset(MLc[:], 1.0), writes=["MLc"])
                P.op("pool", lambda e: e.affine_select(out=MLc[:], in_=MLc[:], pattern=[[-1, 128]], compare_op=ALU.is_gt, fill=0.0, base=0, channel_multiplier=1),
                     reads=["MLc"], writes=["MLc"])
                P.op("pool", lambda e: e.memset(onesc[:], 1.0), writes=["onesc"])
                QKV = [sb("cqkv%d" % q, [128, T], F32, sC) for q in range(3)]
                qb = sb("cqb", [128, T], BF16, sC)
                kb = sb("ckb", [128, T], BF16, sC)
                vt = sb("cvt", [128, 17, 128], BF16, sC)
                YS = [sb("cYS%d" % i, [128, T], BF16, sC) for i in range(2)]
                NB = 4
                SP_ = [sb("cSP%d" % i, [128, 2048], F32, sC) for i in range(NB)]
                E_ = SP_
                SU_ = [sb("cSU%d" % i, [128, 2048 + 128], F32, sC) for i in range(NB)]
                LB_ = [sb("cLB%d" % i, [128, 2048], F32, sC) for i in range(NB)]
                AT_ = [sb("cAT%d" % i, [128, 2048], BF16, sC) for i in range(NB)]
                m0_ = [sb("cm0_%d" % i, [16, 4, 128], F32, sC) for i in range(NB)]
                a0_ = [sb("ca0_%d" % i, [16, 128], BF16, sC) for i in range(NB)]
                os_ = [sb("cos%d" % i, [128, 128], F32, sC) for i in range(4)]
                junk_ = [sb("cjunk%d" % i, [128, 2, 64], F32, sC) for i in range(4)]
                ssq_ = [sb("cssq%d" % i, [128, 2], F32, sC) for i in range(4)]
                for i in range(NB):
                    P.op("pool", lambda e, i=i: e.memset(SU_[i][:, 0:128], 0.0), writes=["cSU%d" % i])
                hc = 0
                bc_ = 0
                for pr in range(dbg.get("npair", 8)):
                    for q, chn in enumerate([27 + pr, 35 + pr, 43 + pr]):
                        P.dma("sp", "cqkv%d" % q, lambda e, q=q, chn=chn: e.dma_start(out=QKV[q][:], in_=PT[chn * 128:(chn + 1) * 128, :]),
                              reads=["PT%d" % chn], writes=["cqkv%d" % q])
                    P.op("act", lambda e: e.activation(out=qb[:], in_=QKV[0][:], func=AF.Copy, scale=0.125), reads=["cqkv0"], writes=["cqb"])
                    P.op("pool", lambda e: e.tensor_copy(out=kb[:], in_=QKV[1][:]), reads=["cqkv1"], writes=["ckb"])
                    for ti, (t0, n) in enumerate(TT):
                        P.op("pe", lambda e, t0=t0, n=n: e.transpose(out=ps[6][0:n, 0:128], in_=QKV[2][:, t0:t0 + n], identity=ident[:, :]),
                             reads=["cqkv2", "ident"], writes=[PSK[6]])
                        P.op("act", lambda e, ti=ti, n=n: e.activation(out=vt[0:n, ti, :], in_=ps[6][0:n, 0:128], func=AF.Copy), reads=[PSK[6]], writes=["cvt%d" % ti])
                    ys_t = YS[pr % 2]
                    ysk = "cYS%d" % (pr % 2)
                    if "nblk" in dbg:
                        P.op("pool", lambda e, ys_t=ys_t: e.memset(ys_t[:], 0.0), writes=[ysk])
                    blocks = TT[:dbg.get("nblk", 17)]

                    def chead_gen(bi, hh, g):
                        tq, Cq = blocks[bi]
                        z_ = g
                        hs_ = slice(hh * 64, hh * 64 + 64)
                        E, SPm, SU, LBt, ATT, m0, a0 = E_[z_], SP_[z_], SU_[z_], LB_[z_], AT_[z_], m0_[z_], a0_[z_]
                        Z_ = lambda nm: "%s%d" % (nm, z_)
                        osb = os_[bi % 4]; osk = "cos%d" % (bi % 4)
                        Zb = [2 * g, 2 * g]
                        eb = 2 * g + 1
                        mb = 2 * g + 1
                        nk = bi
                        ncol = nk * Cq
                        PW = 3
                        nbank = (nk + PW - 1) // PW
                        qsl = slice(tq, tq + Cq)
                        P.op("pe", lambda e: e.matmul(ps[mb][0:16, 0:Cq], lhsT=kb[hs_, 0:16], rhs=qb[hs_, qsl], start=True, stop=True),
                             reads=["ckb", "cqb"], writes=[PSK[mb]])
                        yield
                        P.op("act", lambda e: e.activation(out=m0[:, 0, 0:Cq], in_=ps[mb][0:16, 0:Cq], func=AF.Exp), reads=[PSK[mb]], writes=[Z_("cm0a")])
                        P.op("act", lambda e: e.activation(out=m0[:, 1, 0:Cq], in_=m0[:, 0, 0:Cq], func=AF.Ln, bias=epsb[0:16, 3:4], scale=1.0), reads=[Z_("cm0a"), "epsb"], writes=[Z_("cm0b")])
                        yield
                        P.op("dve", lambda e: e.tensor_tensor(out=m0[:, 2, 0:Cq], in0=ps[mb][0:16, 0:Cq], in1=m0[:, 1, 0:Cq], op=ALU.subtract), reads=[PSK[mb], Z_("cm0b")], writes=[Z_("cm0c")])
                        if bi == 0:
                            P.op("pool", lambda e: e.tensor_tensor(out=m0[:, 1, 0:Cq], in0=m0[:, 1, 0:Cq], in1=MUc[0:16, 0:Cq], op=ALU.mult), reads=[Z_("cm0b"), "MUc"], writes=[Z_("cm0b")])
                        yield
                        for b in range(nbank):
                            zb = Zb[b % 2]
                            c0, c1 = b * PW * Cq, min(ncol, (b + 1) * PW * Cq)
                            for j in range(PW * b, min(nk, PW * b + PW)):
                                kc = bi - j
                                s0 = TT[kc][0]
                                P.op("pe", lambda e, j=j, s0=s0, zb=zb: e.matmul(ps[zb][:, (j % PW) * Cq:(j % PW + 1) * Cq], lhsT=kb[hs_, s0:s0 + 128], rhs=qb[hs_, qsl], start=True, stop=True),
                                     reads=["ckb", "cqb"], writes=[PSK[zb]])
                            yield
                            P.op("act", lambda e, zb=zb, c0=c0, c1=c1: e.activation(out=E[:, c0:c1], in_=ps[zb][:, 0:c1 - c0], func=AF.Exp), reads=[PSK[zb]], writes=[Z_("cE") + "_%d" % b])
                            yield
                            P.op("act", lambda e, c0=c0, c1=c1: e.activation(out=SPm[:, c0:c1], in_=E[:, c0:c1], func=AF.Ln, bias=epsb[:, 3:4], scale=1.0),
                                 reads=[Z_("cE") + "_%d" % b, "epsb"], writes=[Z_("cSP") + "_%d" % b])
                            yield
                            P.op("dve", lambda e, zb=zb, c0=c0, c1=c1: e.tensor_tensor(out=LBt[:, c0:c1], in0=ps[zb][:, 0:c1 - c0], in1=SPm[:, c0:c1], op=ALU.subtract),
                                 reads=[PSK[zb], Z_("cSP") + "_%d" % b], writes=[Z_("cLB") + "_%d" % b])
                            if b == 0:
                                P.op("pool", lambda e: e.tensor_tensor(out=SPm[:, 0:Cq], in0=SPm[:, 0:Cq], in1=MUc[:, 0:Cq], op=ALU.mult), reads=[Z_("cSP") + "_0", "MUc"], writes=[Z_("cSP") + "_0"])
                            yield
                            for j in range(PW * b, min(nk, PW * b + PW)):
                                P.op("pool", lambda e, j=j: e.tensor_tensor(out=SU[:, (j + 1) * Cq:(j + 2) * Cq], in0=SU[:, j * Cq:(j + 1) * Cq], in1=SPm[:, j * Cq:(j + 1) * Cq], op=ALU.add),
                                     reads=[Z_("cSP") + "_%d" % b, Z_("cSU")], writes=[Z_("cSU")])
                                yield
                        for b in range(nbank):
                            c0, c1 = b * PW * Cq, min(ncol, (b + 1) * PW * Cq)
                            P.op("pe", lambda e, c0=c0, c1=c1: e.matmul(ps[eb][:, 0:c1 - c0], lhsT=MLc[:, :], rhs=SPm[:, c0:c1], start=True, stop=False),
                                 reads=["MLc", Z_("cSP") + "_%d" % b], writes=[PSK[eb]])
                            P.op("pe", lambda e, c0=c0, c1=c1: e.matmul(ps[eb][:, 0:c1 - c0], lhsT=onesc[:, :], rhs=SU[:, c0:c1], start=False, stop=True),
                                 reads=["onesc", Z_("cSU")], writes=[PSK[eb]])
                            yield
                            P.op("dve", lambda e, c0=c0, c1=c1: e.tensor_tensor(out=LBt[:, c0:c1], in0=LBt[:, c0:c1], in1=ps[eb][:, 0:c1 - c0], op=ALU.subtract),
                                 reads=[PSK[eb], Z_("cLB") + "_%d" % b], writes=[Z_("cLB") + "_%d" % b])
                            yield
                        if nk > 0:
                            P.op("act", lambda e: e.activation(out=ATT[:, 0:ncol], in_=LBt[:, 0:ncol], func=AF.Exp),
                                 reads=[Z_("cLB") + "_%d" % b for b in range(nbank)], writes=[Z_("cAT")])
                        P.op("pe", lambda e: e.matmul(ps[mb][0:16, 128:128 + Cq], lhsT=MLc[0:16, 0:16], rhs=m0[:, 1, 0:Cq], start=True, stop=(nk == 0)),
                             reads=["MLc", Z_("cm0b")], writes=[PSK[mb]])
                        if nk > 0:
                            P.op("pe", lambda e: e.matmul(ps[mb][0:16, 128:128 + Cq], lhsT=onesc[:, 0:16], rhs=SU[:, nk * Cq:(nk + 1) * Cq], start=False, stop=True),
                                 reads=["onesc", Z_("cSU")], writes=[PSK[mb]])
                        yield
                        if nk > 0:
                            P.op("pool", lambda e: e.tensor_tensor(out=ATT[:, 0:Cq], in0=ATT[:, 0:Cq], in1=MUb[:, 0:Cq], op=ALU.mult), reads=[Z_("cAT"), "MUb"], writes=[Z_("cAT")])
                        P.op("dve", lambda e: e.tensor_tensor(out=m0[:, 3, 0:Cq], in0=m0[:, 2, 0:Cq], in1=ps[mb][0:16, 128:128 + Cq], op=ALU.subtract), reads=[PSK[mb], Z_("cm0c")], writes=[Z_("cm0d")])
                        yield
                        P.op("act", lambda e: e.activation(out=a0[:, 0:Cq], in_=m0[:, 3, 0:Cq], func=AF.Exp), reads=[Z_("cm0d")], writes=[Z_("ca0")])
                        if bi == 0:
                            P.op("pool", lambda e: e.tensor_tensor(out=a0[:, 0:Cq], in0=a0[:, 0:Cq], in1=MUb[0:16, 0:Cq], op=ALU.mult), reads=[Z_("ca0"), "MUb"], writes=[Z_("ca0")])
                        yield
                        oc = slice(256, 320)
                        for j in range(nk):
                            kc = bi - j
                            P.op("pe", lambda e, j=j, kc=kc: e.matmul(ps[mb][0:Cq, oc], lhsT=ATT[:, j * Cq:(j + 1) * Cq], rhs=vt[:, kc, hh * 64:hh * 64 + 64], start=(j == 0), stop=False),
                                 reads=[Z_("cAT"), "cvt%d" % kc], writes=[PSK[mb]])
                        P.op("pe", lambda e: e.matmul(ps[mb][0:Cq, oc], lhsT=a0[:, 0:Cq], rhs=vt[0:16, 0, hh * 64:hh * 64 + 64], start=(nk == 0), stop=True),
                             reads=[Z_("ca0"), "cvt0"], writes=[PSK[mb]])
                        yield
                        P.op("act", lambda e: e.activation(out=osb[0:Cq, hh * 64:hh * 64 + 64], in_=ps[mb][0:Cq, oc], func=AF.Copy), reads=[PSK[mb]], writes=[osk + "_%d" % hh])

                    def cpost_gen(bi, pr=pr, ys_t=ys_t, ysk=ysk):
                        tq, Cq = blocks[bi]
                        osb = os_[bi % 4]; osk = "cos%d" % (bi % 4)
                        ssq = ssq_[bi % 4]; ssk = "cssq%d" % (bi % 4)
                        jk = junk_[bi % 4]
                        for hh in range(2):
                            P.op("act", lambda e, hh=hh: e.activation(out=jk[0:Cq, hh, :], in_=osb[0:Cq, hh * 64:hh * 64 + 64], func=AF.Square, accum_out=ssq[0:Cq, hh:hh + 1]),
                                 reads=[osk + "_%d" % hh], writes=[ssk + "_%d" % hh, "cjunk%d_%d" % (bi % 4, hh)])
                        yield
                        P.op("act", lambda e: e.activation(out=ssq[0:Cq, :], in_=ssq[0:Cq, :], func=AF.Ln, bias=epsb[0:Cq, 2:3], scale=1.0 / 64.0),
                             reads=[ssk + "_0", ssk + "_1", "epsb"], writes=[ssk + "_0", ssk + "_1"])
                        P.op("act", lambda e: e.activation(out=ssq[0:Cq, :], in_=ssq[0:Cq, :], func=AF.Exp, scale=-0.5), reads=[ssk + "_0", ssk + "_1"], writes=[ssk + "_0", ssk + "_1"])
                        yield
                        for hh in range(2):
                            P.op("dve", lambda e, hh=hh: e.tensor_scalar(out=osb[0:Cq, hh * 64:hh * 64 + 64], in0=osb[0:Cq, hh * 64:hh * 64 + 64], scalar1=ssq[0:Cq, hh:hh + 1], scalar2=None, op0=ALU.mult),
                                 reads=[osk + "_%d" % hh, ssk + "_%d" % hh], writes=[osk + "_%d" % hh])
                        yield
                        pbk = 0 if bi % 2 == 0 else 2
                        P.op("pe", lambda e: e.transpose(out=ps[pbk][:, 384:384 + Cq], in_=osb[0:Cq, :], identity=ident[0:Cq, 0:Cq]), reads=[osk + "_0", osk + "_1", "ident"], writes=[PSK[pbk]])
                        yield
                        P.op("act", lambda e: e.activation(out=ys_t[:, tq:tq + Cq], in_=ps[pbk][:, 384:384 + Cq], func=AF.Identity, scale=cc("sb_g", pr), bias=0.0),
                             reads=[PSK[pbk], "cst"], writes=[ysk])

                    def run_ilc(gens):
                        gens = list(gens)
                        while gens:
                            for g in list(gens):
                                try:
                                    next(g)
                                except StopIteration:
                                    gens.remove(g)

                    nblk_ = len(blocks)
                    for bi in range(0, nblk_, 2):
                        gens = [chead_gen(bi, 0, 0), chead_gen(bi, 1, 1)]
                        if bi + 1 < nblk_:
                            gens += [chead_gen(bi + 1, 0, 2), chead_gen(bi + 1, 1, 3)]
                        for pb in (bi - 2, bi - 1):
                            if pb >= 0:
                                gens.append(cpost_gen(pb))
                        run_ilc(gens)
                    last0 = ((nblk_ - 1) // 2) * 2
                    run_ilc([cpost_gen(pb) for pb in range(last0, nblk_)])
                    P.dma("sp", "ysst%d" % (pr % 2), lambda e, ys_t=ys_t, pr=pr: e.dma_start(out=YTD[1024 + pr * 128:1024 + (pr + 1) * 128, :], in_=ys_t[:]), reads=[ysk], writes=["YTD%d" % (8 + pr)])
            P.barrier(exclude=("cv",))

        if "D" in phases:
            with ExitStack() as sd:
                wo = sb("wo", [128, 16, D], BF16, sd)
                yt = sb("ytres", [128, 16, T], BF16, sd)
                g1B = sb("g1B", [128, D], F32, sd)
                b1B = sb("b1B", [128, D], F32, sd)
                P.dma("sp", "bc0", lambda e: e.dma_start(out=g1B[:], in_=bc_d[2]), writes=["g1B"])
                P.dma("sp", "bc1", lambda e: e.dma_start(out=b1B[:], in_=bc_d[3]), writes=["b1B"])
                for k in range(16):
                    P.dma("sp", "ytl%d" % k, lambda e, k=k: e.dma_start(out=yt[:, k, :], in_=YTD[k * 128:(k + 1) * 128, :]),
                          reads=["YTD%d" % k], writes=["yt%d" % k])
                for k0 in range(0, 16, 4):
                    src = w_out[k0 * 128:(k0 + 4) * 128, :].rearrange("(k p) n -> p k n", p=128)
                    P.dma("pool", "wo%d" % (k0 // 4), lambda e, k0=k0, src=src: e.dma_start(out=wo[:, k0:k0 + 4, :], in_=src),
                          writes=["wo%d" % k for k in range(k0, k0 + 4)])
                xts = [sb("dxt%d" % i, [128, D], F32, sd) for i in range(2)]
                hts = [sb("dht%d" % i, [128, 16, 128], BF16, sd) for i in range(2)]
                st = sb("dst", [128, 4, 6], F32, sd)
                mv = sb("dmv", [128, 2], F32, sd)
                rstd = sb("drstd", [128, 1], F32, sd)
                for ti, (t0, n) in enumerate(TT):
                    xt = xts[ti % 2]
                    xk = "dxt%d" % (ti % 2)
                    ht = hts[ti % 2]
                    hk = "dht%d" % (ti % 2)
                    P.dma("sp", "dx%d" % (ti % 2), lambda e, xt=xt, t0=t0, n=n: e.dma_start(out=xt[0:n, :], in_=H0[t0:t0 + n, :]),
                          reads=["H0_%d" % ti], writes=[xk])
                    for g in range(4):
                        for k in range(16):
                            P.op("pe", lambda e, g=g, k=k, t0=t0, n=n: e.matmul(
                                ps[g][0:n, :], lhsT=yt[:, k, t0:t0 + n], rhs=wo[:, k, g * 512:(g + 1) * 512],
                                start=(k == 0), stop=(k == 15)), reads=["yt%d" % k, "wo%d" % k], writes=[PSK[g]])
                        P.op("dve", lambda e, g=g, xt=xt, n=n: e.scalar_tensor_tensor(
                            out=xt[0:n, g * 512:(g + 1) * 512], in0=xt[0:n, g * 512:(g + 1) * 512], scalar=ALPHA,
                            in1=ps[g][0:n, :], op0=ALU.mult, op1=ALU.add), reads=[xk, PSK[g]], writes=[xk])
                    layer_norm_tile("pd", xt, n, st, mv, rstd, [xk])
                    P.op("pool", lambda e, xt=xt, n=n: e.tensor_tensor(out=xt[0:n, :], in0=xt[0:n, :], in1=g1B[0:n, :], op=ALU.mult),
                         reads=[xk, "g1B"], writes=[xk])
                    P.op("dve", lambda e, xt=xt, n=n: e.tensor_tensor(out=xt[0:n, :], in0=xt[0:n, :], in1=b1B[0:n, :], op=ALU.add),
                         reads=[xk, "b1B"], writes=[xk])
                    P.dma("sp", "h1st%d" % (ti % 2), lambda e, xt=xt, t0=t0, n=n: e.dma_start(out=H1[t0:t0 + n, :], in_=xt[0:n, :]),
                          reads=[xk], writes=["H1_%d" % ti])
                    for gi in range(4):
                        bank = 4 + gi % 2
                        for j in range(4):
                            c = gi * 4 + j
                            P.op("pe", lambda e, xt=xt, n=n, c=c, j=j, bank=bank: e.transpose(
                                out=ps[bank][:, j * 128:j * 128 + n], in_=xt[0:n, c * 128:(c + 1) * 128], identity=ident[0:n, 0:n]),
                                reads=[xk, "ident"], writes=[PSK[bank]])
                        copy_ps(evq(), ht[:, gi * 4:gi * 4 + 4, 0:n],
                                ps[bank][:, :].rearrange("p (j m) -> p j m", m=128)[:, :, 0:n], [PSK[bank]], [hk + "_%d" % gi])
                    for k0 in range(0, 16, 4):
                        dst = H1T[k0 * 128:(k0 + 4) * 128, t0:t0 + n].rearrange("(k p) t -> p k t", p=128)
                        P.dma("sp", "h1t%d" % (ti % 2), lambda e, ht=ht, k0=k0, n=n, dst=dst: e.dma_start(out=dst, in_=ht[:, k0:k0 + 4, 0:n]),
                              reads=[hk + "_%d" % (k0 // 4)], writes=["H1T_%d_%d" % (ti, k0)])
            P.barrier(exclude=("cv",))

        if "E" in phases:
            P.barrier()
            with ExitStack() as se:
                g2B = sb("g2B", [128, D], F32, se)
                b2B = sb("b2B", [128, D], F32, se)
                P.dma("sp", "bc0", lambda e: e.dma_start(out=g2B[:], in_=bc_d[4]), writes=["g2B"])
                P.dma("sp", "bc1", lambda e: e.dma_start(out=b2B[:], in_=bc_d[5]), writes=["b2B"])
                NTOK = 528
                hs = sb("ehs", [128, 16, NTOK], BF16, se)
                acc = sb("eacc", [128, 5, D], F32, se)
                carry = sb("ecarry", [128, 44, 2], F32, se)
                P.op("pool", lambda e: e.memset(carry[:].rearrange("p a b -> p (a b)"), 0.0), writes=["carry%d" % i for i in range(44)])
                wus = [sb("ewu%d" % i, [128, 16, 1024], BF16, se) for i in range(2)]
                wds = [sb("ewd%d" % i, [128, 4, D], BF16, se) for i in range(2)]
                gs = [sb("egs%d" % i, [128, NTOK + 2], F32, se) for i in range(2)]
                vs = [sb("evs%d" % i, [128, NTOK], F32, se) for i in range(2)]
                tb = [sb("etb%d" % i, [128, NTOK], F32, se) for i in range(2)]
                ab = [sb("eab%d" % i, [128, 4, NTOK], BF16, se) for i in range(2)]
                st = sb("est", [128, 4, 6], F32, se)
                mv = sb("emv", [128, 2], F32, se)
                rstd = sb("erstd", [128, 1], F32, se)
                STS = [(0, [(0, 16), (16, 512)], TT[0:5])] + [(16 + 512 * i, [(16 + 512 * i, 512)], TT[1 + 4 * i:5 + 4 * i]) for i in range(1, 4)]
                blk = 0
                cidx = 0
                dbank = 0
                for sti, (ts, groups, tiles) in enumerate(STS[:dbg.get("nst", 4)]):
                    ntok = sum(n for _, n in groups)
                    for k0 in range(0, 16, 4):
                        src = H1T[k0 * 128:(k0 + 4) * 128, ts:ts + ntok].rearrange("(k p) t -> p k t", p=128)
                        P.dma("sp", "ehs", lambda e, k0=k0, src=src, ntok=ntok: e.dma_start(out=hs[:, k0:k0 + 4, 0:ntok], in_=src),
                              reads=["H1T_%d_%d" % (TT.index(tl), k0) for tl in tiles], writes=["hs"])
                    for li, (t0, n) in enumerate(tiles):
                        P.dma("sp", "eacc%d" % li, lambda e, li=li, t0=t0, n=n: e.dma_start(out=acc[0:n, li, :], in_=H1[t0:t0 + n, :]),
                              reads=["H1_%d" % (TT.index((t0, n)))], writes=["acc%d" % li])
                        P.op("pool", lambda e, li=li, n=n: e.tensor_scalar(out=acc[0:n, li, :], in0=acc[0:n, li, :], scalar1=ALPHA, scalar2=None, op0=ALU.mult),
                             reads=["acc%d" % li], writes=["acc%d" % li])
                    for fb in range(dbg.get("nfb", 11)):
                        wu = wus[blk % 2]; wuk = "ewu%d" % (blk % 2)
                        wd = wds[blk % 2]; wdk = "ewd%d" % (blk % 2)
                        abt = ab[blk % 2]; abk = "eab%d" % (blk % 2)
                        P.dma("sp", "wu%d" % (blk % 2), lambda e, wu=wu, fb=fb: e.dma_start(out=wu[:, :, :], in_=WUB[fb]), reads=["WUB%d" % fb], writes=[wuk])
                        P.dma("sp", "wd%d" % (blk % 2), lambda e, wd=wd, fb=fb: e.dma_start(out=wd[:, :, :], in_=WDB[fb]), reads=["WDB%d" % fb], writes=[wdk])
                        blk += 1
                        for c in range(4):
                            fc = fb * 4 + c
                            g_ = gs[cidx % 2]; gk = "egs%d" % (cidx % 2)
                            v_ = vs[cidx % 2]; vk = "evs%d" % (cidx % 2)
                            t_ = tb[cidx % 2]; tk = "etb%d" % (cidx % 2)
                            cidx += 1
                            P.op("pool", lambda e, g_=g_, fc=fc: e.tensor_copy(out=g_[:, 0:2], in_=carry[:, fc, :]), reads=["carry%d" % fc], writes=[gk + "c"])
                            off = 0
                            gkeys = []
                            vkeys = []
                            for qi, (t0, n) in enumerate(groups):
                                lo = t0 - ts
                                for k in range(16):
                                    P.op("pe", lambda e, wu=wu, k=k, c=c, lo=lo, n=n: e.matmul(
                                        ps[0][:, 0:n], lhsT=wu[:, k, c * 128:(c + 1) * 128], rhs=hs[:, k, lo:lo + n],
                                        start=(k == 0), stop=(k == 15)), reads=[wuk, "hs"], writes=[PSK[0]])
                                P.op("act", lambda e, g_=g_, lo=lo, n=n: e.activation(out=g_[:, 2 + lo:2 + lo + n], in_=ps[0][:, 0:n], func=AF.Copy),
                                     reads=[PSK[0]], writes=[gk + "_%d" % qi])
                                gkeys.append(gk + "_%d" % qi)
                                for k in range(16):
                                    P.op("pe", lambda e, wu=wu, k=k, c=c, lo=lo, n=n: e.matmul(
                                        ps[1][:, 0:n], lhsT=wu[:, k, 512 + c * 128:512 + (c + 1) * 128], rhs=hs[:, k, lo:lo + n],
                                        start=(k == 0), stop=(k == 15)), reads=[wuk, "hs"], writes=[PSK[1]])
                                P.op("dve", lambda e, v_=v_, lo=lo, n=n: e.tensor_copy(out=v_[:, lo:lo + n], in_=ps[1][:, 0:n]),
                                     reads=[PSK[1]], writes=[vk + "_%d" % qi])
                                vkeys.append(vk + "_%d" % qi)
                            gall = gkeys + [gk + "c"]
                            P.op("act", lambda e, g_=g_, t_=t_, fc=fc, ntok=ntok: e.activation(
                                out=t_[:, 0:ntok], in_=g_[:, 2:2 + ntok], func=AF.Identity, scale=cc("cw2", fc), bias=cc("cb", fc)),
                                reads=gall + ["cst"], writes=[tk])
                            P.op("dve", lambda e, g_=g_, t_=t_, fc=fc, ntok=ntok: e.scalar_tensor_tensor(
                                out=t_[:, 0:ntok], in0=g_[:, 1:1 + ntok], scalar=cc("cw1", fc), in1=t_[:, 0:ntok], op0=ALU.mult, op1=ALU.add),
                                reads=gall + [tk, "cst"], writes=[tk])
                            P.op("dve", lambda e, g_=g_, t_=t_, fc=fc, ntok=ntok: e.scalar_tensor_tensor(
                                out=t_[:, 0:ntok], in0=g_[:, 0:ntok], scalar=cc("cw0", fc), in1=t_[:, 0:ntok], op0=ALU.mult, op1=ALU.add),
                                reads=gall + [tk, "cst"], writes=[tk])
                            P.op("pool", lambda e, g_=g_, fc=fc, ntok=ntok: e.tensor_copy(out=carry[:, fc, :], in_=g_[:, ntok:ntok + 2]),
                                 reads=gall, writes=["carry%d" % fc])
                            P.op("act", lambda e, t_=t_, ntok=ntok: e.activation(out=t_[:, 0:ntok], in_=t_[:, 0:ntok], func=AF.Silu),
                                 reads=[tk], writes=[tk])
                            P.op("pool", lambda e, t_=t_, v_=v_, abt=abt, c=c, ntok=ntok: e.tensor_tensor(
                                out=abt[:, c, 0:ntok], in0=t_[:, 0:ntok], in1=v_[:, 0:ntok], op=ALU.mult),
                                reads=[tk] + vkeys, writes=[abk + "_%d" % c])
                        for li, (t0, n) in enumerate(tiles):
                            lo = t0 - ts
                            for g in range(4):
                                bank = 4 + dbank % 4
                                dbank += 1
                                for c in range(4):
                                    P.op("pe", lambda e, abt=abt, wd=wd, c=c, lo=lo, n=n, g=g, bank=bank: e.matmul(
                                        ps[bank][0:n, :], lhsT=abt[:, c, lo:lo + n], rhs=wd[:, c, g * 512:(g + 1) * 512],
                                        start=(c == 0), stop=(c == 3)), reads=[abk + "_%d" % c, wdk], writes=[PSK[bank]])
                                P.op("dve", lambda e, li=li, n=n, g=g, bank=bank: e.tensor_tensor(
                                    out=acc[0:n, li, g * 512:(g + 1) * 512], in0=acc[0:n, li, g * 512:(g + 1) * 512], in1=ps[bank][0:n, :], op=ALU.add),
                                    reads=["acc%d" % li, PSK[bank]], writes=["acc%d" % li])
                    for li, (t0, n) in enumerate(tiles):
                        if t0 < NMETA:
                            continue
                        at = acc[:, li, :]
                        layer_norm_tile("pe", at, n, st, mv, rstd, ["acc%d" % li])
                        P.op("pool", lambda e, at=at, n=n: e.tensor_tensor(out=at[0:n, :], in0=at[0:n, :], in1=g2B[0:n, :], op=ALU.mult),
                             reads=["acc%d" % li, "g2B"], writes=["acc%d" % li])
                        P.op("dve", lambda e, at=at, n=n: e.tensor_tensor(out=at[0:n, :], in0=at[0:n, :], in1=b2B[0:n, :], op=ALU.add),
                             reads=["acc%d" % li, "b2B"], writes=["acc%d" % li])
                        P.dma("sp", "ost%d" % li, lambda e, at=at, t0=t0, n=n: e.dma_start(out=out_d[t0 - NMETA:t0 - NMETA + n, :], in_=at[0:n, :]),
                              reads=["acc%d" % li], writes=["out%d" % t0])
            P.barrier(exclude=("cv",))

        P.emit(final_wait_chans=[c for c in P.chan_order])
    nc.used_inputs = used_inputs
    nc.prog_stats = {e: len(v) for e, v in P.ops.items()}
    return nc


def _pc(v, n):
    return np.ascontiguousarray(np.asarray(v, np.float32).reshape(n, 128).T)


def host_consts(inp):
    cst = np.zeros((128, NCST), np.float32)

    def put(name, arr):
        cst[:, CST[name]:CST[name] + arr.shape[1]] = arr

    put("emb_g", _pc(inp["emb_ln_g"], 16)); put("emb_b", _pc(inp["emb_ln_b"], 16))
    put("ln1_g", _pc(inp["ln1_g"][0], 16)); put("ln1_b", _pc(inp["ln1_b"][0], 16))
    put("ln2_g", _pc(inp["ln2_g"][0], 16)); put("ln2_b", _pc(inp["ln2_b"][0], 16))
    mu = np.zeros(27 * 128, np.float32)
    mu[:3360] = inp["rwkv_mu"][0]
    put("mu", _pc(mu, 27))
    for nm, key in [("w0", "rwkv_w0"), ("a0", "rwkv_a0"), ("k_k", "rwkv_k_k"), ("k_a", "rwkv_k_a"),
                    ("gn_g", "rwkv_gn_g"), ("gn_b", "rwkv_gn_b"), ("sb_g", "sb_norm_g")]:
        put(nm, _pc(inp[key][0], 8))
    put("r_k", _pc(inp["rwkv_r_k"][0].reshape(-1), 8))
    cw = inp["ffn_conv_w"][0]
    put("cw0", _pc(cw[0], 44)); put("cw1", _pc(cw[1], 44)); put("cw2", _pc(cw[2], 44))
    put("cb", _pc(inp["ffn_conv_b"][0], 44))
    bc = np.stack([np.broadcast_to(np.asarray(v, np.float32)[None, :], (128, D)) for v in
                   [inp["emb_ln_g"], inp["emb_ln_b"], inp["ln1_g"][0], inp["ln1_b"][0], inp["ln2_g"][0], inp["ln2_b"][0]]])
    return cst, np.ascontiguousarray(bc)


def make_in_maps(inp):
    inp = {k: np.asarray(v) for k, v in inp.items()}
    cst, bc = host_consts(inp)
    shared = dict(cst=cst, bc=bc,
                  w_in=np.ascontiguousarray(inp["w_in"][0], dtype=np.float32),
                  w2a2=np.ascontiguousarray(np.concatenate([inp["rwkv_w2"][0], inp["rwkv_a2"][0]], 0), dtype=np.float32),
                  g2=np.ascontiguousarray(inp["rwkv_g2"][0], dtype=np.float32),
                  w_out=np.ascontiguousarray(inp["w_out"][0], dtype=np.float32),
                  w_up=np.ascontiguousarray(inp["ffn_w_up"][0], dtype=np.float32),
                  w_dn=np.ascontiguousarray(inp["ffn_w_down"][0], dtype=np.float32))
    maps = []
    for b in range(NCORES):
        m = dict(shared)
        m["xcat"] = np.ascontiguousarray(np.concatenate([inp["meta_tokens"], inp["x"][b]], 0), dtype=np.float32)
        maps.append(m)
    return maps


_NC_CACHE = {}


def kernel(**inputs):
    if "nc" not in _NC_CACHE:
        _NC_CACHE["nc"] = build_nc()
    nc = _NC_CACHE["nc"]
    maps = make_in_maps(inputs)
    res = run_bass_kernel_spmd(nc, maps, core_ids=list(range(NCORES)))
    return np.stack([np.asarray(r["out"], np.float32) for r in res.results], 0)
```

```python
import numpy as np
from contextlib import ExitStack
import concourse.bass as bass
import concourse.mybir as mybir
from concourse.bass_utils import run_bass_kernel_spmd

F32 = mybir.dt.float32
BF16 = mybir.dt.bfloat16
AF = mybir.ActivationFunctionType
ALU = mybir.AluOpType
AX = mybir.AxisListType

D = 2048
T = 2064
NMETA = 16
DFF = 5632
NCORES = 8
ALPHA = 2.0 ** 0.25
TT = [(0, 16)] + [(16 + 128 * i, 128) for i in range(16)]
TG = [(0, 16)] + [(16 + 512 * i, 512) for i in range(4)]
CH = [(128 * i, 128) for i in range(24)] + [(3072, 128), (3200, 128), (3328, 32)] + \
     [(3360 + 128 * i, 128) for i in range(24)]
NCH = len(CH)

ENGS = ["pe", "act", "dve", "pool", "sp"]


class Prog:
    def __init__(self, nc):
        self.nc = nc
        self.ops = {e: [] for e in ENGS}
        self.lastw = {}
        self.readers = {}
        self.known = {e: {} for e in ENGS}
        self.chan_cnt = {}
        self.chan_order = []
        self.pending = {e: [] for e in ENGS}

    def _deps(self, eng, reads, writes):
        idx = len(self.ops[eng])
        deps = {}

        def add(tok, raw):
            src, i = tok
            if src == eng:
                if eng in ("pe", "sp"):
                    return
            if self.known[eng].get(src, -1) >= i:
                return
            if deps.get(src, -1) < i:
                deps[src] = i

        for tok in self.pending[eng]:
            if tok[0] != eng:
                add(tok, True)
        self.pending[eng] = []
        for r in reads:
            w = self.lastw.get(r)
            if w is not None:
                add(w, True)
            if r.startswith("ps") and len(r) == 3:
                for tok in self.readers.get(r, {}).items():
                    if tok[0] != eng:
                        add(tok, False)
        for r in writes:
            w = self.lastw.get(r)
            if w is not None:
                add(w, False)
            for tok in self.readers.get(r, {}).items():
                add(tok, False)
        for src, i in deps.items():
            self.known[eng][src] = i
            if not src.startswith("dma:"):
                self.ops[src][i]["signal"] = True
        return idx, list(deps.items())

    def op(self, eng, fn, reads=(), writes=()):
        idx, deps = self._deps(eng, reads, writes)
        self.ops[eng].append(dict(fn=fn, deps=deps, signal=False, chan=None))
        tok = (eng, idx)
        for r in reads:
            self.readers.setdefault(r, {})[eng] = idx
        for r in writes:
            self.lastw[r] = tok
            self.readers[r] = {}
        return tok

    def dma(self, queue, chan, fn, reads=(), writes=()):
        idx, deps = self._deps(queue, reads, writes)
        if chan not in self.chan_cnt:
            self.chan_cnt[chan] = 0
            self.chan_order.append(chan)
        self.chan_cnt[chan] += 1
        cnt = self.chan_cnt[chan]
        self.ops[queue].append(dict(fn=fn, deps=deps, signal=False, chan=chan))
        tok = ("dma:" + chan, cnt)
        for r in reads:
            self.readers.setdefault(r, {})["dma:" + chan] = cnt
        for r in writes:
            self.lastw[r] = tok
            self.readers[r] = {}
        return tok

    def barrier(self, exclude=()):
        toks = []
        for e in ["pe", "act", "dve", "pool"]:
            for i in range(len(self.ops[e]) - 1, -1, -1):
                if self.ops[e][i]["chan"] is None:
                    self.ops[e][i]["signal"] = True
                    toks.append((e, i))
                    break
        for c in self.chan_order:
            if c not in exclude:
                toks.append(("dma:" + c, self.chan_cnt[c]))
        for e in ENGS:
            self.pending[e] = list(toks)

    def emit(self, final_wait_chans=()):
        nc = self.nc
        with ExitStack() as es:
            sems = {}
            for e in ["pe", "act", "dve", "pool"]:
                sems[e] = es.enter_context(nc.semaphore("s_" + e))
            for c in self.chan_order:
                sems["dma:" + c] = es.enter_context(nc.semaphore("d_" + c))
            cnts = {}
            for e in ["pe", "act", "dve", "pool"]:
                c = 0
                arr = []
                for o in self.ops[e]:
                    if o["signal"] and o["chan"] is None:
                        c += 1
                    arr.append(c)
                cnts[e] = arr

            def run(ename, eng):
                for o in self.ops[ename]:
                    for src, i in o["deps"]:
                        if src.startswith("dma:"):
                            eng.wait_ge(sems[src], 16 * i)
                        else:
                            eng.wait_ge(sems[src], cnts[src][i])
                    ins = o["fn"](eng)
                    if o["chan"] is not None:
                        ins.then_inc(sems["dma:" + o["chan"]], 16)
                    elif o["signal"]:
                        ins.then_inc(sems[ename], 1)
                if ename == "sp":
                    for c in final_wait_chans:
                        eng.wait_ge(sems["dma:" + c], 16 * self.chan_cnt[c])

            block = es.enter_context(nc.Block())

            @block.tensor
            def _(e):
                run("pe", e)

            @block.scalar
            def _(e):
                run("act", e)

            @block.vector
            def _(e):
                run("dve", e)

            @block.gpsimd
            def _(e):
                run("pool", e)

            @block.sync
            def _(e):
                run("sp", e)


def _cst_layout():
    off = {}
    c = 0
    for name, n in [("emb_g", 16), ("emb_b", 16), ("ln1_g", 16), ("ln1_b", 16), ("ln2_g", 16), ("ln2_b", 16),
                    ("mu", 27), ("w0", 8), ("a0", 8), ("k_k", 8), ("k_a", 8), ("r_k", 8), ("gn_g", 8),
                    ("gn_b", 8), ("sb_g", 8), ("cw0", 44), ("cw1", 44), ("cw2", 44), ("cb", 44)]:
        off[name] = c
        c += n
    return off, c


CST, NCST = _cst_layout()


def build_nc(dbg=None):
    dbg = dbg or {}
    phases = dbg.get("phases", "0ABCDE")
    taps = dbg.get("taps", [])
    inject = dbg.get("inject", [])
    nc = bass.Bass("TRN2", target_bir_lowering=False)
    P = Prog(nc)

    used_inputs = []

    def din(name, shape, dt=F32, need="0ABCDE"):
        if not any(p in phases for p in need):
            return None
        used_inputs.append(name)
        return nc.dram_tensor(name, list(shape), dt, kind="ExternalInput").ap()

    def dscratch(name, shape, dt):
        if name in inject:
            used_inputs.append(name)
            return nc.dram_tensor(name, list(shape), dt, kind="ExternalInput").ap()
        if name in taps:
            return nc.dram_tensor(name, list(shape), dt, kind="ExternalOutput").ap()
        return nc.dram_tensor(name, list(shape), dt, kind="Internal").ap()

    xcat = din("xcat", [T, D], need="0")
    cst_d = din("cst", [128, NCST])
    bc_d = din("bc", [6, 128, D])
    w_in = din("w_in", [D, 6432], need="A")
    w2a2_d = din("w2a2", [128, 1024], need="B")
    g2_d = din("g2", [160, 1024], need="B")
    w_out = din("w_out", [D, D], need="D")
    w_up = din("w_up", [D, 2 * DFF], need="E")
    w_dn = din("w_dn", [DFF, D], need="E")
    out_d = nc.dram_tensor("out", [T - NMETA, D], F32, kind="ExternalOutput").ap()

    H0 = dscratch("H0", [T, D], F32)
    PT = dscratch("PT", [NCH * 128, T], F32)
    YTD = dscratch("YTD", [D, T], BF16)
    H1 = dscratch("H1", [T, D], F32)
    H1T = dscratch("H1T", [D, T], BF16)
    WUB = dscratch("WUB", [11, 128, 16, 1024], BF16)
    WDB = dscratch("WDB", [11, 128, 4, D], BF16)

    es = ExitStack()
    with es:
        def sb(name, shape, dt, st=es):
            return st.enter_context(nc.sbuf_tensor(name, list(shape), dt))

        ps = [es.enter_context(nc.psum_tensor("ps%d" % i, [128, 512], F32)) for i in range(8)]
        PSK = ["ps%d" % i for i in range(8)]

        cst = sb("cst_t", [128, NCST], F32)
        ident = sb("ident", [128, 128], F32)
        P.dma("sp", "cst", lambda e: e.dma_start(out=cst[:], in_=cst_d), writes=["cst"])
        P.op("pool", lambda e: e.memset(ident[:], 0.0), writes=["ident"])
        P.op("pool", lambda e: e.affine_select(out=ident[:], in_=ident[:], pattern=[[-1, 128]],
                                               compare_op=ALU.not_equal, fill=1.0, base=0, channel_multiplier=1),
             reads=["ident"], writes=["ident"])
        epsb = sb("epsb", [128, 4], F32)
        for j, v in enumerate([1e-5, 64e-5, 1e-6, 1.0]):
            P.op("pool", lambda e, j=j, v=v: e.memset(epsb[:, j:j + 1], v), writes=["epsb"])

        if "E" in phases:
            for fb in range(11):
                for half, c0 in enumerate([fb * 512, DFF + fb * 512]):
                    src = w_up[:, c0:c0 + 512].rearrange("(k p) n -> p k n", p=128)
                    P.dma("pool", "cv", lambda e, fb=fb, half=half, src=src: e.dma_start(out=WUB[fb, :, :, half * 512:(half + 1) * 512], in_=src),
                          writes=["WUB%d" % fb])
                srcd = w_dn[fb * 512:(fb + 1) * 512, :].rearrange("(k p) n -> p k n", p=128)
                P.dma("pool", "cv", lambda e, fb=fb, srcd=srcd: e.dma_start(out=WDB[fb], in_=srcd), writes=["WDB%d" % fb])

        def cc(name, j=0, rows=128):
            o = CST[name] + j
            return cst[0:rows, o:o + 1]

        rr = {"n": 0}

        def evq():
            rr["n"] += 1
            if dbg.get("evq"):
                return dbg["evq"]
            return "act" if rr["n"] % 2 else "dve"

        def copy_ps(eng, out, in_, reads, writes):
            if eng == "act":
                P.op("act", lambda e: e.activation(out=out, in_=in_, func=AF.Copy), reads=reads, writes=writes)
            else:
                P.op(eng, lambda e: e.tensor_copy(out=out, in_=in_), reads=reads, writes=writes)

        def wload(queue_chan, dst, src_rows_ap, nk, ncols, keys):
            for k0 in range(0, nk, 4):
                k1 = min(nk, k0 + 4)
                src = src_rows_ap[k0 * 128:k1 * 128, :].rearrange("(k p) n -> p k n", p=128)
                P.dma("pool", queue_chan, lambda e, k0=k0, k1=k1, src=src: e.dma_start(out=dst[:, k0:k1, 0:ncols], in_=src),
                      writes=keys)

        def layer_norm_tile(pref, xt, rows, st, mv, rstd, reads):
            for i in range(4):
                P.op("dve", lambda e, i=i: e.bn_stats(out=st[0:rows, i, :], in_=xt[0:rows, i * 512:(i + 1) * 512]),
                     reads=reads, writes=[pref + "st%d" % i])
            P.op("dve", lambda e: e.bn_aggr(out=mv[0:rows, :], in_=st[0:rows].rearrange("p a b -> p (a b)")),
                 reads=[pref + "st%d" % i for i in range(4)], writes=[pref + "mv"])
            P.op("act", lambda e: e.activation(out=rstd[0:rows, :], in_=mv[0:rows, 1:2], func=AF.Sqrt,
                                               bias=epsb[0:rows, 0:1], scale=1.0),
                 reads=[pref + "mv", "epsb"], writes=[pref + "rstd"])
            P.op("dve", lambda e: e.reciprocal(out=rstd[0:rows, :], in_=rstd[0:rows, :]),
                 reads=[pref + "rstd"], writes=[pref + "rstd"])
            P.op("dve", lambda e: e.tensor_scalar(out=xt[0:rows, :], in0=xt[0:rows, :], scalar1=mv[0:rows, 0:1],
                                                  scalar2=rstd[0:rows, :], op0=ALU.subtract, op1=ALU.mult),
                 reads=reads + [pref + "mv", pref + "rstd"], writes=reads)

        if "0" in phases or "A" in phases:
            with ExitStack() as s1:
                h0T = sb("h0T", [128, 16, T], BF16, s1)
                with ExitStack() as s0:
                  if "0" in phases:
                      gB = sb("gB", [128, D], F32, s0)
                      bB = sb("bB", [128, D], F32, s0)
                      P.dma("sp", "bc0", lambda e: e.dma_start(out=gB[:], in_=bc_d[0]), writes=["gB"])
                      P.dma("sp", "bc1", lambda e: e.dma_start(out=bB[:], in_=bc_d[1]), writes=["bB"])
                      xts = [sb("xt%d" % i, [128, D], F32, s0) for i in range(2)]
                      st = sb("st", [128, 4, 6], F32, s0)
                      mv = sb("mv", [128, 2], F32, s0)
                      rstd = sb("rstd", [128, 1], F32, s0)
                      for ti, (t0, n) in enumerate(TT[dbg.get('tile0', 0):dbg.get('ntiles', 17)]):
                          xt = xts[ti % 2]
                          xk = "xt%d" % (ti % 2)
                          P.dma("sp", "x%d" % (ti % 2), lambda e, xt=xt, t0=t0, n=n: e.dma_start(out=xt[0:n, :], in_=xcat[t0:t0 + n, :]),
                                writes=[xk])
                          layer_norm_tile("p0", xt, n, st, mv, rstd, [xk])
                          P.op(dbg.get("multeng", "pool"), lambda e, xt=xt, n=n: e.tensor_tensor(out=xt[0:n, :], in0=xt[0:n, :], in1=gB[0:n, :], op=ALU.mult),
                               reads=[xk, "gB"], writes=[xk])
                          P.op("dve", lambda e, xt=xt, n=n: e.tensor_tensor(out=xt[0:n, :], in0=xt[0:n, :], in1=bB[0:n, :], op=ALU.add),
                               reads=[xk, "bB"], writes=[xk])
                          if not dbg.get("nostore"):
                              P.dma("sp", "h0st%d" % (ti % 2), lambda e, xt=xt, t0=t0, n=n: e.dma_start(out=H0[t0:t0 + n, :], in_=xt[0:n, :]),
                                    reads=[xk], writes=["H0_%d" % ti])
                          if dbg.get("notr"):
                              continue
                          for gi in range(4):
                              bank = gi % 2
                              for j in range(4):
                                  c = gi * 4 + j
                                  P.op("pe", lambda e, xt=xt, n=n, c=c, j=j, bank=bank: e.transpose(
                                      out=ps[bank][:, j * 128:j * 128 + n], in_=xt[0:n, c * 128:(c + 1) * 128], identity=ident[0:n, 0:n]),
                                      reads=[xk, "ident"], writes=[PSK[bank]])
                              copy_ps(evq(), h0T[:, gi * 4:gi * 4 + 4, t0:t0 + n],
                                      ps[bank][:, :].rearrange("p (j m) -> p j m", m=128)[:, :, 0:n], [PSK[bank]],
                                      ["h0T%d" % (gi * 4 + j) for j in range(4)])
                P.barrier(exclude=("cv",))
                with ExitStack() as sa:
                  if "A" in phases:
                      wb = [sb("winb%d" % i, [128, 16, 256], BF16, sa) for i in range(2)]
                      stg = [sb("stg%d" % i, [128, T + 1], F32, sa) for i in range(2)]
                      stm = [sb("stm%d" % i, [128, T], F32, sa) for i in range(2)]
                      omm = sb("omm", [128, 27], F32, sa)
                      P.op("dve", lambda e: e.tensor_scalar(out=omm[:], in0=cst[:, CST["mu"]:CST["mu"] + 27], scalar1=-1.0, scalar2=1.0,
                                                            op0=ALU.mult, op1=ALU.add), reads=["cst"], writes=["omm"])
                      for i in range(2):
                          P.op("pool", lambda e, i=i: e.memset(stg[i][:, 0:1], 0.0), writes=["stg%d" % i])
                      groups = []
                      j = 0
                      while j < NCH:
                          if j + 1 < NCH and CH[j][1] == 128 and CH[j + 1][1] == 128 and CH[j + 1][0] == CH[j][0] + 128:
                              groups.append([j, j + 1])
                              j += 2
                          else:
                              groups.append([j])
                              j += 1
                      bi = 0
                      for gi, grp in enumerate(groups):
                          wbuf = wb[gi % 2]
                          wk = "winb%d" % (gi % 2)
                          c0 = CH[grp[0]][0]
                          ncols = sum(CH[j][1] for j in grp)
                          wload("win%d" % (gi % 2), wbuf, w_in[:, c0:c0 + ncols], 16, ncols, [wk])
                          for jj, j in enumerate(grp):
                              wd = CH[j][1]
                              so = stg[j % 2]
                              sk = "stg%d" % (j % 2)
                              sks = [sk + "_%d" % q for q in range(len(TG))]
                              for q, (t0, n) in enumerate(TG):
                                  bank = bi % 4
                                  bi += 1
                                  for k in range(16):
                                      P.op("pe", lambda e, wbuf=wbuf, k=k, jj=jj, wd=wd, t0=t0, n=n, bank=bank: e.matmul(
                                          ps[bank][0:wd, 0:n], lhsT=wbuf[:, k, jj * 128:jj * 128 + wd], rhs=h0T[:, k, t0:t0 + n],
                                          start=(k == 0), stop=(k == 15)), reads=[wk, "h0T%d" % k], writes=[PSK[bank]])
                                  copy_ps(evq(), so[0:wd, 1 + t0:1 + t0 + n], ps[bank][0:wd, 0:n], [PSK[bank]], [sks[q]])
                              if j < 27:
                                  sm = stm[j % 2]
                                  mk = "stm%d" % (j % 2)
                                  P.op("act", lambda e, so=so, sm=sm, wd=wd, j=j: e.activation(
                                      out=sm[0:wd, :], in_=so[0:wd, 1:T + 1], func=AF.Copy, scale=omm[0:wd, j:j + 1]),
                                      reads=sks + [sk, "omm"], writes=[mk])
                                  P.op("dve", lambda e, so=so, sm=sm, wd=wd, j=j: e.scalar_tensor_tensor(
                                      out=sm[0:wd, :], in0=so[0:wd, 0:T], scalar=cc("mu", j, wd), in1=sm[0:wd, :], op0=ALU.mult, op1=ALU.add),
                                      reads=sks + [sk, mk, "cst"], writes=[mk])
                                  P.dma("sp", "ptst%d" % (j % 2), lambda e, sm=sm, wd=wd, j=j: e.dma_start(out=PT[j * 128:j * 128 + wd, :], in_=sm[0:wd, :]),
                                        reads=[mk], writes=["PT%d" % j])
                              else:
                                  P.dma("sp", "ptsu%d" % (j % 2), lambda e, so=so, wd=wd, j=j: e.dma_start(out=PT[j * 128:j * 128 + wd, :], in_=so[0:wd, 1:T + 1]),
                                        reads=sks + [sk], writes=["PT%d" % j])
            P.barrier(exclude=("cv",))

        if "B" in phases:
            with ExitStack() as sB:
                MU = sb("MU", [128, 128], F32, sB)
                MUI = sb("MUI", [128, 128], F32, sB)
                ML = sb("ML", [128, 128], F32, sB)
                BO = sb("BO", [128, 128], F32, sB)
                ones = sb("onesT", [128, 128], F32, sB)
                for m_, nm, cm, pat, cmp_ in [(MU, "MU", -1, 1, ALU.is_gt), (MUI, "MUI", -1, 1, ALU.is_ge), (ML, "ML", 1, -1, ALU.is_gt)]:
                    P.op("pool", lambda e, m_=m_: e.memset(m_[:], 1.0), writes=[nm])
                    P.op("pool", lambda e, m_=m_, cm=cm, pat=pat, cmp_=cmp_: e.affine_select(
                        out=m_[:], in_=m_[:], pattern=[[pat, 128]], compare_op=cmp_, fill=0.0, base=0, channel_multiplier=cm),
                        reads=[nm], writes=[nm])
                P.op("pool", lambda e: e.memset(BO[:], 0.0), writes=["BO"])
                P.op("pool", lambda e: e.memset(BO[0:64, 0:64], 1.0), reads=["BO"], writes=["BO"])
                P.op("pool", lambda e: e.memset(BO[64:128, 64:128], 1.0), reads=["BO"], writes=["BO"])
                P.op("pool", lambda e: e.memset(ones[:], 1.0), writes=["ones"])
                omka = sb("omka", [128, 8], F32, sB)
                P.op("dve", lambda e: e.tensor_scalar(out=omka[:], in0=cst[:, CST["k_a"]:CST["k_a"] + 8], scalar1=-1.0, scalar2=1.0,
                                                      op0=ALU.mult, op1=ALU.add), reads=["cst"], writes=["omka"])
                W2A2 = sb("W2A2", [128, 1024], BF16, sB)
                G2a = sb("G2a", [128, 1024], BF16, sB)
                G2b = sb("G2b", [32, 1024], BF16, sB)
                P.dma("pool", "lw0", lambda e: e.dma_start(out=W2A2[:], in_=w2a2_d), writes=["W2A2"])
                P.dma("pool", "lw1", lambda e: e.dma_start(out=G2a[:], in_=g2_d[0:128, :]), writes=["G2a"])
                P.dma("pool", "lw2", lambda e: e.dma_start(out=G2b[:], in_=g2_d[128:160, :]), writes=["G2b"])
                LA = sb("LA", [128, T], BF16, sB)
                G0 = sb("G0", [128, T], BF16, sB)
                G1 = sb("G1", [32, T], BF16, sB)
                with ExitStack() as sl:
                    lst = sb("lstage", [128, T], F32, sl)
                    P.dma("sp", "lst", lambda e: e.dma_start(out=lst[:], in_=PT[24 * 128:25 * 128, :]), reads=["PT24"], writes=["lst"])
                    P.op("act", lambda e: e.activation(out=LA[0:64, :], in_=lst[0:64, :], func=AF.Tanh), reads=["lst"], writes=["LA"])
                    P.op("act", lambda e: e.activation(out=LA[64:128, :], in_=lst[64:128, :], func=AF.Copy), reads=["lst"], writes=["LA"])
                    P.dma("sp", "lst", lambda e: e.dma_start(out=lst[:], in_=PT[25 * 128:26 * 128, :]), reads=["PT25"], writes=["lst"])
                    P.op("act", lambda e: e.activation(out=G0[:], in_=lst[:], func=AF.Sigmoid), reads=["lst"], writes=["G0"])
                    P.dma("sp", "lst", lambda e: e.dma_start(out=lst[0:32, :], in_=PT[26 * 128:26 * 128 + 32, :]), reads=["PT26"], writes=["lst"])
                    P.op("act", lambda e: e.activation(out=G1[:], in_=lst[0:32, :], func=AF.Sigmoid), reads=["lst"], writes=["G1"])
                    P.barrier(exclude=("cv",))
                RKV = [[sb("rkv%d_%d" % (i, q), [128, T], F32, sB) for q in range(3)] for i in range(2)]
                LD = sb("LDt", [128, T], F32, sB)
                At = sb("At", [128, T], F32, sB)
                GT = sb("GTt", [128, T], F32, sB)
                YP = [sb("YP%d" % i, [128, T], BF16, sB) for i in range(2)]
                Mst = sb("Mst", [128, 64], F32, sB)
                Mb = sb("Mb", [128, 64], BF16, sB)

                def mk(name, shape, dt, nbuf=2):
                    return [sb("%s_%d" % (name, i), shape, dt, sB) for i in range(nbuf)]

                cum_ = mk("cum", [128, 128], F32); cumx_ = mk("cumx", [128, 128], F32)
                Ep_ = mk("Ep", [128, 128], F32); Em_ = mk("Em", [128, 128], F32); Ex_ = mk("Ex", [128, 128], F32)
                kk_ = mk("kk", [128, 128], F32); sqrk_ = mk("sqrk", [128, 2, 128], F32); rn_ = mk("rn", [128, 128], F32)
                kkn_ = mk("kkn", [128, 128], F32); tf_ = mk("tf", [128, 128], F32); k2_ = mk("k2", [128, 128], F32)
                b_ = mk("bb", [128, 128], F32); ART_ = mk("ART", [128, 2, 128], BF16); KT_ = mk("KT", [128, 128], BF16)
                BT_ = mk("BT", [128, 128], BF16); KH_ = mk("KH", [128, 128], F32); BH_ = mk("BH", [128, 128], F32)
                TOK_ = mk("TOK", [128, 3, 128], BF16); sbc_ = mk("sbc", [128, 128], F32); bon_ = mk("bon", [128, 128], F32, 3)
                Us_ = mk("Us", [128, 128], BF16); ys_ = mk("ys", [128, 128], F32); yT_ = mk("yT", [128, 128], F32)
                gst_ = mk("gst", [128, 2, 6], F32); gmv_ = mk("gmv", [128, 2, 2], F32); grs_ = mk("grs", [128, 2], F32)
                Bm_ = mk("Bm", [128, 128], F32, 4); Am_ = mk("Am", [128, 128], F32, 4)
                ArbT_ = mk("ArbT", [128, 128], BF16); AakT_ = mk("AakT", [128, 128], BF16); ArkT_ = mk("ArkT", [128, 128], BF16)
                Pm_ = mk("Pm", [128, 128], F32); W0s_ = mk("W0s", [128, 64], F32)

                pcnt = 0
                hcnt = 0
                for pr in range(dbg.get("npair", 8)):
                    cs = pr * 128
                    R, Kt, V = RKV[pr % 2]
                    rkk = ["rkv%d_%d" % (pr % 2, q) for q in range(3)]
                    def load_rkv(p2):
                        for q, chn in enumerate([p2, 8 + p2, 16 + p2]):
                            P.dma("sp", "rkv%d_%d" % (p2 % 2, q), lambda e, q=q, chn=chn, tl=RKV[p2 % 2][q]: e.dma_start(out=tl[:], in_=PT[chn * 128:(chn + 1) * 128, :]),
                                  reads=["PT%d" % chn], writes=["rkv%d_%d" % (p2 % 2, q)])
                    if pr == 0:
                        load_rkv(0)
                    if pr + 1 < dbg.get("npair", 8):
                        load_rkv(pr + 1)
                    yp = YP[pr % 2]
                    ypk = "YP%d" % (pr % 2)
                    if "nchunk" in dbg:
                        P.op("pool", lambda e, yp=yp: e.memset(yp[:], 0.0), writes=[ypk])
                    for (t0, n) in TG:
                        P.op("pe", lambda e, t0=t0, n=n, cs=cs: e.matmul(ps[0][:, 0:n], lhsT=W2A2[0:64, cs:cs + 128], rhs=LA[0:64, t0:t0 + n], start=True, stop=True),
                             reads=["W2A2", "LA"], writes=[PSK[0]])
                        P.op("act", lambda e, t0=t0, n=n, pr=pr: e.activation(out=LD[:, t0:t0 + n], in_=ps[0][:, 0:n], func=AF.Sigmoid, bias=cc("w0", pr), scale=1.0),
                             reads=[PSK[0], "cst"], writes=["LD"])
                        P.op("pe", lambda e, t0=t0, n=n, cs=cs: e.matmul(ps[1][:, 0:n], lhsT=W2A2[64:128, cs:cs + 128], rhs=LA[64:128, t0:t0 + n], start=True, stop=True),
                             reads=["W2A2", "LA"], writes=[PSK[1]])
                        P.op("act", lambda e, t0=t0, n=n, pr=pr: e.activation(out=At[:, t0:t0 + n], in_=ps[1][:, 0:n], func=AF.Sigmoid, bias=cc("a0", pr), scale=1.0),
                             reads=[PSK[1], "cst"], writes=["At"])
                        P.op("pe", lambda e, t0=t0, n=n, cs=cs: e.matmul(ps[2][:, 0:n], lhsT=G2a[:, cs:cs + 128], rhs=G0[:, t0:t0 + n], start=True, stop=False),
                             reads=["G2a", "G0"], writes=[PSK[2]])
                        P.op("pe", lambda e, t0=t0, n=n, cs=cs: e.matmul(ps[2][:, 0:n], lhsT=G2b[0:32, cs:cs + 128], rhs=G1[0:32, t0:t0 + n], start=False, stop=True),
                             reads=["G2b", "G1"], writes=[PSK[2]])
                        P.op("dve", lambda e, t0=t0, n=n: e.tensor_copy(out=GT[:, t0:t0 + n], in_=ps[2][:, 0:n]), reads=[PSK[2]], writes=["GT"])
                    P.op("dve", lambda e: e.tensor_scalar(out=LD[:], in0=LD[:], scalar1=-0.6065306597126334, scalar2=None, op0=ALU.mult),
                         reads=["LD"], writes=["LD"])
                    P.op("pool", lambda e: e.memset(Mst[:], 0.0), writes=["M"])
                    P.op("pool", lambda e: e.memset(Mb[:], 0.0), writes=["Mb"])
                    chunks = TT[:dbg.get("nchunk", 17)]

                    def pre_gen(ci, pr=pr, R=R, Kt=Kt, V=V, rkk=rkk):
                        t0, C = chunks[ci]
                        z = ci % 2
                        sl_ = slice(t0, t0 + C)
                        cum, cumx, Ep, Em, Ex = cum_[z], cumx_[z], Ep_[z], Em_[z], Ex_[z]
                        kk, sqrk, rn, kkn, tf, k2, bb = kk_[z], sqrk_[z], rn_[z], kkn_[z], tf_[z], k2_[z], b_[z]
                        ART, KT, BT, KH, BH, TOK = ART_[z], KT_[z], BT_[z], KH_[z], BH_[z], TOK_[z]
                        sbc, bon = sbc_[z], bon_[ci % 3]
                        K_ = lambda nm: ("bon_%d" % (ci % 3)) if nm == "bon" else "%s_%d" % (nm, z)
                        P.op("dve", lambda e: e.tensor_tensor_scan(out=cum[:, 0:C], data0=ones[:, 0:C], data1=LD[:, sl_], initial=0.0, op0=ALU.mult, op1=ALU.add),
                             reads=["LD", "ones"], writes=[K_("cum")])
                        P.op("pool", lambda e: e.tensor_scalar(out=kk[:, 0:C], in0=Kt[:, sl_], scalar1=cc("k_k", pr), scalar2=None, op0=ALU.mult),
                             reads=[rkk[1], "cst"], writes=[K_("kk")])
                        P.op("dve", lambda e: e.tensor_scalar(out=tf[:, 0:C], in0=At[:, sl_], scalar1=cc("k_a", pr), scalar2=omka[:, pr:pr + 1], op0=ALU.mult, op1=ALU.add),
                             reads=["At", "cst", "omka"], writes=[K_("tf")])
                        yield
                        P.op("pool", lambda e: e.tensor_tensor(out=cumx[:, 0:C], in0=cum[:, 0:C], in1=LD[:, sl_], op=ALU.subtract),
                             reads=[K_("cum"), "LD"], writes=[K_("cumx")])
                        P.op("act", lambda e: e.activation(out=Ep[:, 0:C], in_=cum[:, 0:C], func=AF.Exp), reads=[K_("cum")], writes=[K_("Ep")])
                        P.op("act", lambda e: e.activation(out=Em[:, 0:C], in_=cum[:, 0:C], func=AF.Exp, scale=-1.0), reads=[K_("cum")], writes=[K_("Em")])
                        P.op("pool", lambda e: e.tensor_tensor(out=sqrk[:, 0, 0:C], in0=kk[:, 0:C], in1=kk[:, 0:C], op=ALU.mult),
                             reads=[K_("kk")], writes=[K_("sq")])
                        P.op("dve", lambda e: e.tensor_tensor(out=k2[:, 0:C], in0=Kt[:, sl_], in1=tf[:, 0:C], op=ALU.mult),
                             reads=[rkk[1], K_("tf")], writes=[K_("k2")])
                        yield
                        P.op("act", lambda e: e.activation(out=Ex[:, 0:C], in_=cumx[:, 0:C], func=AF.Exp), reads=[K_("cumx")], writes=[K_("Ex")])
                        P.op("dve", lambda e: e.scalar_tensor_tensor(out=sqrk[:, 1, 0:C], in0=R[:, sl_], scalar=cc("r_k", pr), in1=k2[:, 0:C], op0=ALU.mult, op1=ALU.mult),
                             reads=[rkk[0], K_("k2"), "cst"], writes=[K_("rk")])
                        yield
                        for hf in range(2):
                            P.op("pe", lambda e, hf=hf: e.matmul(ps[0][:, hf * 128:hf * 128 + C], lhsT=BO[:, :], rhs=sqrk[:, hf, 0:C], start=True, stop=True),
                                 reads=["BO", K_("sq") if hf == 0 else K_("rk")], writes=[PSK[0]])
                        P.op("dve", lambda e: e.tensor_tensor(out=ART[:, 1, 0:C], in0=R[:, sl_], in1=Ep[:, 0:C], op=ALU.mult),
                             reads=[rkk[0], K_("Ep")], writes=[K_("RT")])
                        P.op("pool", lambda e: e.tensor_tensor(out=KT[:, 0:C], in0=k2[:, 0:C], in1=Em[:, 0:C], op=ALU.mult),
                             reads=[K_("k2"), K_("Em")], writes=[K_("KT")])
                        yield
                        P.op("act", lambda e: e.activation(out=rn[:, 0:C], in_=ps[0][:, 0:C], func=AF.Ln), reads=[PSK[0]], writes=[K_("rn")])
                        P.op("act", lambda e: e.activation(out=sbc[:, 0:C], in_=ps[0][:, 128:128 + C], func=AF.Copy), reads=[PSK[0]], writes=[K_("sbc")])
                        P.op("act", lambda e: e.activation(out=rn[:, 0:C], in_=rn[:, 0:C], func=AF.Exp, scale=-0.5), reads=[K_("rn")], writes=[K_("rn")])
                        EC = Ep[:, C - 1:C]
                        P.op("dve", lambda e: e.scalar_tensor_tensor(out=KH[:, 0:C], in0=k2[:, 0:C], scalar=EC, in1=Em[:, 0:C], op0=ALU.mult, op1=ALU.mult),
                             reads=[K_("k2"), K_("Em"), K_("Ep")], writes=[K_("KH")])
                        yield
                        P.op("pool", lambda e: e.tensor_tensor(out=bon[:, 0:C], in0=sbc[:, 0:C], in1=V[:, sl_], op=ALU.mult),
                             reads=[K_("sbc"), rkk[2]], writes=[K_("bon")])
                        P.op("dve", lambda e: e.tensor_tensor(out=kkn[:, 0:C], in0=kk[:, 0:C], in1=rn[:, 0:C], op=ALU.mult),
                             reads=[K_("kk"), K_("rn")], writes=[K_("kkn")])
                        yield
                        P.op("pool", lambda e: e.tensor_tensor(out=bb[:, 0:C], in0=kkn[:, 0:C], in1=At[:, sl_], op=ALU.mult),
                             reads=[K_("kkn"), "At"], writes=[K_("bb")])
                        P.op("dve", lambda e: e.scalar_tensor_tensor(out=ART[:, 0, 0:C], in0=kkn[:, 0:C], scalar=-1.0, in1=Ex[:, 0:C], op0=ALU.mult, op1=ALU.mult),
                             reads=[K_("kkn"), K_("Ex")], writes=[K_("AT")])
                        yield
                        P.op("pool", lambda e: e.tensor_tensor(out=BT[:, 0:C], in0=bb[:, 0:C], in1=Em[:, 0:C], op=ALU.mult),
                             reads=[K_("bb"), K_("Em")], writes=[K_("BT")])
                        P.op("dve", lambda e: e.scalar_tensor_tensor(out=BH[:, 0:C], in0=bb[:, 0:C], scalar=EC, in1=Em[:, 0:C], op0=ALU.mult, op1=ALU.mult),
                             reads=[K_("bb"), K_("Em"), K_("Ep")], writes=[K_("BH")])
                        yield
                        for q, (src, rk_) in enumerate([(KH[:, 0:C], K_("KH")), (BH[:, 0:C], K_("BH")), (V[:, sl_], rkk[2])]):
                            P.op("pe", lambda e, q=q, src=src: e.transpose(out=ps[1][0:C, q * 128:(q + 1) * 128], in_=src, identity=ident[:, :]),
                                 reads=[rk_, "ident"], writes=[PSK[1]])
                        yield
                        P.op("act", lambda e: e.activation(out=TOK[0:C].rearrange("p a b -> p (a b)"), in_=ps[1][0:C, 0:384], func=AF.Copy),
                             reads=[PSK[1]], writes=[K_("TOK")])

                    def head_gen(ci, hh):
                        t0, C = chunks[ci]
                        z = ci % 2
                        ART, KT, BT, TOK, Us = ART_[z], KT_[z], BT_[z], TOK_[z], Us_[z]
                        ys = ys_[z]
                        K_ = lambda nm: "%s_%d" % (nm, z)
                        y_ = hh
                        hs_ = slice(hh * 64, hh * 64 + 64)
                        XB, YB, ZB = 2 + hh, 4 + hh, 6 + hh
                        y4 = 2 * (ci % 2) + hh
                        Bm, Am = Bm_[y4], Am_[y4]
                        y4p = (y4 + 2) % 4
                        zz = ci % 2
                        ArbT, AakT, ArkT, Pm, W0s = ArbT_[hh], AakT_[hh], ArkT_[hh], Pm_[hh], W0s_[hh]
                        H_ = lambda nm: "%s_h%d" % (nm, hh)
                        P.op("pe", lambda e: e.matmul(ps[XB][0:C, 0:2 * C].rearrange("p (a c) -> p a c", a=2), lhsT=BT[hs_, 0:C], rhs=ART[hs_, :, 0:C], start=True, stop=True),
                             reads=[K_("BT"), K_("AT"), K_("RT")], writes=[PSK[XB]])
                        P.op("pe", lambda e: e.matmul(ps[XB][0:C, 256:256 + C], lhsT=ART[hs_, 0, 0:C], rhs=BT[hs_, 0:C], start=True, stop=True),
                             reads=[K_("BT"), K_("AT")], writes=[PSK[XB]])
                        P.op("pe", lambda e: e.matmul(ps[YB][0:C, 0:2 * C].rearrange("p (a c) -> p a c", a=2), lhsT=KT[hs_, 0:C], rhs=ART[hs_, :, 0:C], start=True, stop=True),
                             reads=[K_("KT"), K_("AT"), K_("RT")], writes=[PSK[YB]])
                        yield
                        Bk, Ak = "Bm%d" % y4, "Am%d" % y4
                        P.op("dve", lambda e: e.tensor_tensor(out=Bm[0:C, 0:C], in0=ps[XB][0:C, 0:C], in1=MU[0:C, 0:C], op=ALU.mult), reads=[PSK[XB], "MU"], writes=[Bk])
                        P.op("dve", lambda e: e.tensor_tensor(out=Am[0:C, 0:C], in0=ps[XB][0:C, 256:256 + C], in1=ML[0:C, 0:C], op=ALU.mult), reads=[PSK[XB], "ML"], writes=[Ak])
                        yield
                        P.op("pool", lambda e: e.tensor_tensor(out=Pm[0:C, 0:C], in0=Bm[0:C, 0:C], in1=ident[0:C, 0:C], op=ALU.add), reads=[Bk, "ident"], writes=[H_("Pm")])
                        P.op("dve", lambda e: e.tensor_tensor(out=ArbT[0:C, 0:C], in0=ps[XB][0:C, C:2 * C], in1=MUI[0:C, 0:C], op=ALU.mult), reads=[PSK[XB], "MUI"], writes=[H_("ArbT")])
                        L = 7 if C == 128 else 4
                        Acur, Bcur, Akc, Bkc = Am, Bm, Ak, Bk
                        for l in range(1, L):
                            odd = (l % 2) == 1
                            An = Am_[y4p] if odd else Am_[y4]
                            Bn = Bm_[y4p] if odd else Bm_[y4]
                            Ank = "Am%d" % (y4p if odd else y4)
                            Bnk = "Bm%d" % (y4p if odd else y4)
                            P.op("pe", lambda e, Acur=Acur, Bcur=Bcur: e.matmul(ps[ZB][0:C, 0:C], lhsT=Bcur[0:C, 0:C], rhs=Acur[0:C, 0:C], start=True, stop=True),
                                 reads=[Akc, Bkc], writes=[PSK[ZB]])
                            if l < L - 1:
                                P.op("pe", lambda e, Acur=Acur, Bcur=Bcur: e.matmul(ps[ZB][0:C, 128:128 + C], lhsT=Acur[0:C, 0:C], rhs=Bcur[0:C, 0:C], start=True, stop=True),
                                     reads=[Akc, Bkc], writes=[PSK[ZB]])
                            yield
                            if l == 1:
                                P.op("dve", lambda e: e.tensor_tensor(out=AakT[0:C, 0:C], in0=ps[YB][0:C, 0:C], in1=MU[0:C, 0:C], op=ALU.mult), reads=[PSK[YB], "MU"], writes=[H_("AakT")])
                                P.op("dve", lambda e: e.tensor_tensor(out=ArkT[0:C, 0:C], in0=ps[YB][0:C, C:2 * C], in1=MUI[0:C, 0:C], op=ALU.mult), reads=[PSK[YB], "MUI"], writes=[H_("ArkT")])
                            if l < L - 1:
                                P.op("act", lambda e, An=An, Bn=Bn: e.activation(out=An[0:C, 0:C], in_=ps[ZB][0:C, 0:C], func=AF.Copy), reads=[PSK[ZB]], writes=[Ank])
                                P.op("act", lambda e, An=An, Bn=Bn: e.activation(out=Bn[0:C, 0:C], in_=ps[ZB][0:C, 128:128 + C], func=AF.Copy), reads=[PSK[ZB]], writes=[Bnk])
                            else:
                                P.op("act", lambda e, An=An: e.activation(out=An[0:C, 0:C], in_=ps[ZB][0:C, 0:C], func=AF.Copy), reads=[PSK[ZB]], writes=[Ank])
                            yield
                            P.op("pe", lambda e, An=An: e.matmul(ps[YB][0:C, 256:256 + C], lhsT=An[0:C, 0:C], rhs=Pm[0:C, 0:C], start=True, stop=True),
                                 reads=[Ank, H_("Pm")], writes=[PSK[YB]])
                            yield
                            P.op("dve", lambda e: e.tensor_tensor(out=Pm[0:C, 0:C], in0=Pm[0:C, 0:C], in1=ps[YB][0:C, 256:256 + C], op=ALU.add), reads=[H_("Pm"), PSK[YB]], writes=[H_("Pm")])
                            Acur, Bcur, Akc, Bkc = An, Bn, Ank, Bnk
                        yield
                        Vt_h = TOK[0:C, 2, hh * 64:hh * 64 + 64]
                        P.op("pe", lambda e: e.matmul(ps[ZB][0:C, 256:320], lhsT=ART[hs_, 0, 0:C], rhs=Mb[hs_, :], start=True, stop=False),
                             reads=[K_("AT"), "Mb"], writes=[PSK[ZB]])
                        P.op("pe", lambda e: e.matmul(ps[ZB][0:C, 256:320], lhsT=AakT[0:C, 0:C], rhs=Vt_h, start=False, stop=True),
                             reads=[H_("AakT"), K_("TOK")], writes=[PSK[ZB]])
                        yield
                        P.op("act", lambda e: e.activation(out=W0s[0:C, :], in_=ps[ZB][0:C, 256:320], func=AF.Copy), reads=[PSK[ZB]], writes=[H_("W0s")])
                        yield
                        P.op("pe", lambda e: e.matmul(ps[ZB][0:C, 320:384], lhsT=Pm[0:C, 0:C], rhs=W0s[0:C, :], start=True, stop=True),
                             reads=[H_("Pm"), H_("W0s")], writes=[PSK[ZB]])
                        yield
                        P.op("act", lambda e: e.activation(out=Us[0:C, hh * 64:hh * 64 + 64], in_=ps[ZB][0:C, 320:384], func=AF.Copy), reads=[PSK[ZB]], writes=[K_("Us%d" % hh)])
                        yield
                        P.op("pe", lambda e: e.matmul(ps[ZB][0:C, 384:448], lhsT=ART[hs_, 1, 0:C], rhs=Mb[hs_, :], start=True, stop=False),
                             reads=[K_("RT"), "Mb"], writes=[PSK[ZB]])
                        P.op("pe", lambda e: e.matmul(ps[ZB][0:C, 384:448], lhsT=ArbT[0:C, 0:C], rhs=Us[0:C, hh * 64:hh * 64 + 64], start=False, stop=False),
                             reads=[H_("ArbT"), K_("Us%d" % hh)], writes=[PSK[ZB]])
                        P.op("pe", lambda e: e.matmul(ps[ZB][0:C, 384:448], lhsT=ArkT[0:C, 0:C], rhs=Vt_h, start=False, stop=True),
                             reads=[H_("ArkT"), K_("TOK")], writes=[PSK[ZB]])
                        yield
                        P.op("act", lambda e: e.activation(out=ys[0:C, hh * 64:hh * 64 + 64], in_=ps[ZB][0:C, 384:448], func=AF.Copy), reads=[PSK[ZB]], writes=[K_("ys%d" % hh)])

                    def state_step(ci):
                        t0, C = chunks[ci]
                        z = ci % 2
                        TOK, Us, Ep = TOK_[z], Us_[z], Ep_[z]
                        K_ = lambda nm: "%s_%d" % (nm, z)
                        P.op("pe", lambda e: e.matmul(ps[2][:, 384:512], lhsT=TOK[0:C, 1, :], rhs=Us[0:C, :], start=True, stop=False),
                             reads=[K_("TOK"), K_("Us0"), K_("Us1")], writes=[PSK[2]])
                        P.op("pe", lambda e: e.matmul(ps[2][:, 384:512], lhsT=TOK[0:C, 0, :], rhs=TOK[0:C, 2, :], start=False, stop=True),
                             reads=[K_("TOK")], writes=[PSK[2]])
                        for hh in range(2):
                            hs_ = slice(hh * 64, hh * 64 + 64)
                            P.op("dve", lambda e, hs_=hs_, hh=hh: e.scalar_tensor_tensor(out=Mst[hs_, :], in0=Mst[hs_, :], scalar=Ep[hs_, C - 1:C], in1=ps[2][hs_, 384 + hh * 64:384 + hh * 64 + 64], op0=ALU.mult, op1=ALU.add),
                                 reads=["M", K_("Ep"), PSK[2]], writes=["M"])
                        P.op("pool", lambda e: e.tensor_copy(out=Mb[:], in_=Mst[:]), reads=["M"], writes=["Mb"])

                    def post_gen(ci, pr=pr, yp=yp, ypk=ypk):
                        t0, C = chunks[ci]
                        z = ci % 2
                        sl_ = slice(t0, t0 + C)
                        ys, yT, bon = ys_[z], yT_[z], bon_[ci % 3]
                        gst, gmv, grs = gst_[z], gmv_[z], grs_[z]
                        K_ = lambda nm: ("bon_%d" % (ci % 3)) if nm == "bon" else "%s_%d" % (nm, z)
                        for hh in range(2):
                            P.op("dve", lambda e, hh=hh: e.bn_stats(out=gst[0:C, hh, :], in_=ys[0:C, hh * 64:hh * 64 + 64]), reads=[K_("ys%d" % hh)], writes=[K_("gst%d" % hh)])
                        yield
                        for hh in range(2):
                            P.op("dve", lambda e, hh=hh: e.bn_aggr(out=gmv[0:C, hh, :], in_=gst[0:C, hh, :]), reads=[K_("gst%d" % hh)], writes=[K_("gmv%d" % hh)])
                        yield
                        P.op("act", lambda e: e.activation(out=grs[0:C, :], in_=gmv[0:C, :, 1], func=AF.Ln, bias=epsb[0:C, 1:2], scale=1.0),
                             reads=[K_("gmv0"), K_("gmv1"), "epsb"], writes=[K_("grs")])
                        P.op("act", lambda e: e.activation(out=grs[0:C, :], in_=grs[0:C, :], func=AF.Exp, scale=-0.5), reads=[K_("grs")], writes=[K_("grs")])
                        yield
                        for hh in range(2):
                            P.op("dve", lambda e, hh=hh: e.tensor_scalar(out=ys[0:C, hh * 64:hh * 64 + 64], in0=ys[0:C, hh * 64:hh * 64 + 64],
                                                                  scalar1=gmv[0:C, hh, 0:1], scalar2=grs[0:C, hh:hh + 1], op0=ALU.subtract, op1=ALU.mult),
                                 reads=[K_("ys%d" % hh), K_("gmv%d" % hh), K_("grs")], writes=[K_("ys%d" % hh)])
                        yield
                        P.op("pe", lambda e: e.transpose(out=ps[1][:, 384:384 + C], in_=ys[0:C, :], identity=ident[0:C, 0:C]), reads=[K_("ys0"), K_("ys1"), "ident"], writes=[PSK[1]])
                        yield
                        P.op("act", lambda e: e.activation(out=yT[:, 0:C], in_=ps[1][:, 384:384 + C], func=AF.Identity, scale=cc("gn_g", pr), bias=cc("gn_b", pr)),
                             reads=[PSK[1], "cst"], writes=[K_("yT")])
                        yield
                        P.op("pool", lambda e: e.tensor_tensor(out=yT[:, 0:C], in0=yT[:, 0:C], in1=bon[:, 0:C], op=ALU.add), reads=[K_("yT"), K_("bon")], writes=[K_("yT")])
                        P.op("pool", lambda e: e.tensor_tensor(out=yp[:, sl_], in0=yT[:, 0:C], in1=GT[:, sl_], op=ALU.mult), reads=[K_("yT"), "GT"], writes=[ypk])

                    def run_il(gens):
                        gens = list(gens)
                        while gens:
                            for g in list(gens):
                                try:
                                    next(g)
                                except StopIteration:
                                    gens.remove(g)

                    nchk = len(chunks)
                    run_il([pre_gen(0)])
                    for ci in range(nchk):
                        gens = [head_gen(ci, 0), head_gen(ci, 1)]
                        if ci + 1 < nchk:
                            gens.append(pre_gen(ci + 1))
                        if ci >= 1:
                            gens.append(post_gen(ci - 1))
                        run_il(gens)
                        state_step(ci)
                    run_il([post_gen(nchk - 1)])
                    P.dma("sp", "ypst%d" % (pr % 2), lambda e, yp=yp, pr=pr: e.dma_start(out=YTD[pr * 128:(pr + 1) * 128, :], in_=yp[:]), reads=[ypk], writes=["YTD%d" % pr])
            P.barrier(exclude=("cv",))

        if "C" in phases:
            with ExitStack() as sC:
                MUc = sb("MUc", [128, 128], F32, sC)
                MUb = sb("MUb", [128, 128], BF16, sC)
                MLc = sb("MLc", [128, 128], F32, sC)
                onesc = sb("onesc", [128, 128], F32, sC)
                P.op("pool", lambda e: e.memset(MUc[:], 1.0), writes=["MUc"])
                P.op("pool", lambda e: e.affine_select(out=MUc[:], in_=MUc[:], pattern=[[1, 128]], compare_op=ALU.is_gt, fill=0.0, base=0, channel_multiplier=-1),
                     reads=["MUc"], writes=["MUc"])
                P.op("pool", lambda e: e.tensor_copy(out=MUb[:], in_=MUc[:]), reads=["MUc"], writes=["MUb"])
                P.op("pool", lambda e: e.memset(MLc[:], 1.0), writes=["MLc"])
                P.op("pool", lambda e: e.affine_select(out=MLc[:], in_=MLc[:], pattern=[[-1, 128]], compare_op=ALU.is_gt, fill=0.0, base=0, channel_multiplier=1),
                     reads=["MLc"], writes=["MLc"])
                P.op("pool", lambda e: e.memset(onesc[:], 1.0), writes=["onesc"])
                QKV = [sb("cqkv%d" % q, [128, T], F32, sC) for q in range(3)]
                qb = sb("cqb", [128, T], BF16, sC)
                kb = sb("ckb", [128, T], BF16, sC)
                vt = sb("cvt", [128, 17, 128], BF16, sC)
                YS = [sb("cYS%d" % i, [128, T], BF16, sC) for i in range(2)]
                NB = 4
                SP_ = [sb("cSP%d" % i, [128, 2048], F32, sC) for i in range(NB)]
                E_ = SP_
                SU_ = [sb("cSU%d" % i, [128, 2048 + 128], F32, sC) for i in range(NB)]
                LB_ = [sb("cLB%d" % i, [128, 2048], F32, sC) for i in range(NB)]
                AT_ = [sb("cAT%d" % i, [128, 2048], BF16, sC) for i in range(NB)]
                m0_ = [sb("cm0_%d" % i, [16, 4, 128], F32, sC) for i in range(NB)]
                a0_ = [sb("ca0_%d" % i, [16, 128], BF16, sC) for i in range(NB)]
                os_ = [sb("cos%d" % i, [128, 128], F32, sC) for i in range(4)]
                junk_ = [sb("cjunk%d" % i, [128, 2, 64], F32, sC) for i in range(4)]
                ssq_ = [sb("cssq%d" % i, [128, 2], F32, sC) for i in range(4)]
                for i in range(NB):
                    P.op("pool", lambda e, i=i: e.memset(SU_[i][:, 0:128], 0.0), writes=["cSU%d" % i])
                hc = 0
                bc_ = 0
                for pr in range(dbg.get("npair", 8)):
                    for q, chn in enumerate([27 + pr, 35 + pr, 43 + pr]):
                        P.dma("sp", "cqkv%d" % q, lambda e, q=q, chn=chn: e.dma_start(out=QKV[q][:], in_=PT[chn * 128:(chn + 1) * 128, :]),
                              reads=["PT%d" % chn], writes=["cqkv%d" % q])
                    P.op("act", lambda e: e.activation(out=qb[:], in_=QKV[0][:], func=AF.Copy, scale=0.125), reads=["cqkv0"], writes=["cqb"])
                    P.op("pool", lambda e: e.tensor_copy(out=kb[:], in_=QKV[1][:]), reads=["cqkv1"], writes=["ckb"])
                    for ti, (t0, n) in enumerate(TT):
                        P.op("pe", lambda e, t0=t0, n=n: e.transpose(out=ps[6][0:n, 0:128], in_=QKV[2][:, t0:t0 + n], identity=ident[:, :]),
                             reads=["cqkv2", "ident"], writes=[PSK[6]])
                        P.op("act", lambda e, ti=ti, n=n: e.activation(out=vt[0:n, ti, :], in_=ps[6][0:n, 0:128], func=AF.Copy), reads=[PSK[6]], writes=["cvt%d" % ti])
                    ys_t = YS[pr % 2]
                    ysk = "cYS%d" % (pr % 2)
                    if "nblk" in dbg:
                        P.op("pool", lambda e, ys_t=ys_t: e.memset(ys_t[:], 0.0), writes=[ysk])
                    blocks = TT[:dbg.get("nblk", 17)]

                    def chead_gen(bi, hh, g):
                        tq, Cq = blocks[bi]
                        z_ = g
                        hs_ = slice(hh * 64, hh * 64 + 64)
                        E, SPm, SU, LBt, ATT, m0, a0 = E_[z_], SP_[z_], SU_[z_], LB_[z_], AT_[z_], m0_[z_], a0_[z_]
                        Z_ = lambda nm: "%s%d" % (nm, z_)
                        osb = os_[bi % 4]; osk = "cos%d" % (bi % 4)
                        Zb = [2 * g, 2 * g]
                        eb = 2 * g + 1
                        mb = 2 * g + 1
                        nk = bi
                        ncol = nk * Cq
                        PW = 3
                        nbank = (nk + PW - 1) // PW
                        qsl = slice(tq, tq + Cq)
                        P.op("pe", lambda e: e.matmul(ps[mb][0:16, 0:Cq], lhsT=kb[hs_, 0:16], rhs=qb[hs_, qsl], start=True, stop=True),
                             reads=["ckb", "cqb"], writes=[PSK[mb]])
                        yield
                        P.op("act", lambda e: e.activation(out=m0[:, 0, 0:Cq], in_=ps[mb][0:16, 0:Cq], func=AF.Exp), reads=[PSK[mb]], writes=[Z_("cm0a")])
                        P.op("act", lambda e: e.activation(out=m0[:, 1, 0:Cq], in_=m0[:, 0, 0:Cq], func=AF.Ln, bias=epsb[0:16, 3:4], scale=1.0), reads=[Z_("cm0a"), "epsb"], writes=[Z_("cm0b")])
                        yield
                        P.op("dve", lambda e: e.tensor_tensor(out=m0[:, 2, 0:Cq], in0=ps[mb][0:16, 0:Cq], in1=m0[:, 1, 0:Cq], op=ALU.subtract), reads=[PSK[mb], Z_("cm0b")], writes=[Z_("cm0c")])
                        if bi == 0:
                            P.op("pool", lambda e: e.tensor_tensor(out=m0[:, 1, 0:Cq], in0=m0[:, 1, 0:Cq], in1=MUc[0:16, 0:Cq], op=ALU.mult), reads=[Z_("cm0b"), "MUc"], writes=[Z_("cm0b")])
                        yield
                        for b in range(nbank):
                            zb = Zb[b % 2]
                            c0, c1 = b * PW * Cq, min(ncol, (b + 1) * PW * Cq)
                            for j in range(PW * b, min(nk, PW * b + PW)):
                                kc = bi - j
                                s0 = TT[kc][0]
                                P.op("pe", lambda e, j=j, s0=s0, zb=zb: e.matmul(ps[zb][:, (j % PW) * Cq:(j % PW + 1) * Cq], lhsT=kb[hs_, s0:s0 + 128], rhs=qb[hs_, qsl], start=True, stop=True),
                                     reads=["ckb", "cqb"], writes=[PSK[zb]])
                            yield
                            P.op("act", lambda e, zb=zb, c0=c0, c1=c1: e.activation(out=E[:, c0:c1], in_=ps[zb][:, 0:c1 - c0], func=AF.Exp), reads=[PSK[zb]], writes=[Z_("cE") + "_%d" % b])
                            yield
                            P.op("act", lambda e, c0=c0, c1=c1: e.activation(out=SPm[:, c0:c1], in_=E[:, c0:c1], func=AF.Ln, bias=epsb[:, 3:4], scale=1.0),
                                 reads=[Z_("cE") + "_%d" % b, "epsb"], writes=[Z_("cSP") + "_%d" % b])
                            yield
                            P.op("dve", lambda e, zb=zb, c0=c0, c1=c1: e.tensor_tensor(out=LBt[:, c0:c1], in0=ps[zb][:, 0:c1 - c0], in1=SPm[:, c0:c1], op=ALU.subtract),
                                 reads=[PSK[zb], Z_("cSP") + "_%d" % b], writes=[Z_("cLB") + "_%d" % b])
                            if b == 0:
                                P.op("pool", lambda e: e.tensor_tensor(out=SPm[:, 0:Cq], in0=SPm[:, 0:Cq], in1=MUc[:, 0:Cq], op=ALU.mult), reads=[Z_("cSP") + "_0", "MUc"], writes=[Z_("cSP") + "_0"])
                            yield
                            for j in range(PW * b, min(nk, PW * b + PW)):
                                P.op("pool", lambda e, j=j: e.tensor_tensor(out=SU[:, (j + 1) * Cq:(j + 2) * Cq], in0=SU[:, j * Cq:(j + 1) * Cq], in1=SPm[:, j * Cq:(j + 1) * Cq], op=ALU.add),
                                     reads=[Z_("cSP") + "_%d" % b, Z_("cSU")], writes=[Z_("cSU")])
                                yield
                        for b in range(nbank):
                            c0, c1 = b * PW * Cq, min(ncol, (b + 1) * PW * Cq)
                            P.op("pe", lambda e, c0=c0, c1=c1: e.matmul(ps[eb][:, 0:c1 - c0], lhsT=MLc[:, :], rhs=SPm[:, c0:c1], start=True, stop=False),
                                 reads=["MLc", Z_("cSP") + "_%d" % b], writes=[PSK[eb]])
                            P.op("pe", lambda e, c0=c0, c1=c1: e.matmul(ps[eb][:, 0:c1 - c0], lhsT=onesc[:, :], rhs=SU[:, c0:c1], start=False, stop=True),
                                 reads=["onesc", Z_("cSU")], writes=[PSK[eb]])
                            yield
                            P.op("dve", lambda e, c0=c0, c1=c1: e.tensor_tensor(out=LBt[:, c0:c1], in0=LBt[:, c0:c1], in1=ps[eb][:, 0:c1 - c0], op=ALU.subtract),
                                 reads=[PSK[eb], Z_("cLB") + "_%d" % b], writes=[Z_("cLB") + "_%d" % b])
                            yield
                        if nk > 0:
                            P.op("act", lambda e: e.activation(out=ATT[:, 0:ncol], in_=LBt[:, 0:ncol], func=AF.Exp),
                                 reads=[Z_("cLB") + "_%d" % b for b in range(nbank)], writes=[Z_("cAT")])
                        P.op("pe", lambda e: e.matmul(ps[mb][0:16, 128:128 + Cq], lhsT=MLc[0:16, 0:16], rhs=m0[:, 1, 0:Cq], start=True, stop=(nk == 0)),
                             reads=["MLc", Z_("cm0b")], writes=[PSK[mb]])
                        if nk > 0:
                            P.op("pe", lambda e: e.matmul(ps[mb][0:16, 128:128 + Cq], lhsT=onesc[:, 0:16], rhs=SU[:, nk * Cq:(nk + 1) * Cq], start=False, stop=True),
                                 reads=["onesc", Z_("cSU")], writes=[PSK[mb]])
                        yield
                        if nk > 0:
                            P.op("pool", lambda e: e.tensor_tensor(out=ATT[:, 0:Cq], in0=ATT[:, 0:Cq], in1=MUb[:, 0:Cq], op=ALU.mult), reads=[Z_("cAT"), "MUb"], writes=[Z_("cAT")])
                        P.op("dve", lambda e: e.tensor_tensor(out=m0[:, 3, 0:Cq], in0=m0[:, 2, 0:Cq], in1=ps[mb][0:16, 128:128 + Cq], op=ALU.subtract), reads=[PSK[mb], Z_("cm0c")], writes=[Z_("cm0d")])
                        yield
                        P.op("act", lambda e: e.activation(out=a0[:, 0:Cq], in_=m0[:, 3, 0:Cq], func=AF.Exp), reads=[Z_("cm0d")], writes=[Z_("ca0")])
                        if bi == 0:
                            P.op("pool", lambda e: e.tensor_tensor(out=a0[:, 0:Cq], in0=a0[:, 0:Cq], in1=MUb[0:16, 0:Cq], op=ALU.mult), reads=[Z_("ca0"), "MUb"], writes=[Z_("ca0")])
                        yield
                        oc = slice(256, 320)
                        for j in range(nk):
                            kc = bi - j
                            P.op("pe", lambda e, j=j, kc=kc: e.matmul(ps[mb][0:Cq, oc], lhsT=ATT[:, j * Cq:(j + 1) * Cq], rhs=vt[:, kc, hh * 64:hh * 64 + 64], start=(j == 0), stop=False),
                                 reads=[Z_("cAT"), "cvt%d" % kc], writes=[PSK[mb]])
                        P.op("pe", lambda e: e.matmul(ps[mb][0:Cq, oc], lhsT=a0[:, 0:Cq], rhs=vt[0:16, 0, hh * 64:hh * 64 + 64], start=(nk == 0), stop=True),
                             reads=[Z_("ca0"), "cvt0"], writes=[PSK[mb]])
                        yield
                        P.op("act", lambda e: e.activation(out=osb[0:Cq, hh * 64:hh * 64 + 64], in_=ps[mb][0:Cq, oc], func=AF.Copy), reads=[PSK[mb]], writes=[osk + "_%d" % hh])

                    def cpost_gen(bi, pr=pr, ys_t=ys_t, ysk=ysk):
                        tq, Cq = blocks[bi]
                        osb = os_[bi % 4]; osk = "cos%d" % (bi % 4)
                        ssq = ssq_[bi % 4]; ssk = "cssq%d" % (bi % 4)
                        jk = junk_[bi % 4]
                        for hh in range(2):
                            P.op("act", lambda e, hh=hh: e.activation(out=jk[0:Cq, hh, :], in_=osb[0:Cq, hh * 64:hh * 64 + 64], func=AF.Square, accum_out=ssq[0:Cq, hh:hh + 1]),
                                 reads=[osk + "_%d" % hh], writes=[ssk + "_%d" % hh, "cjunk%d_%d" % (bi % 4, hh)])
                        yield
                        P.op("act", lambda e: e.activation(out=ssq[0:Cq, :], in_=ssq[0:Cq, :], func=AF.Ln, bias=epsb[0:Cq, 2:3], scale=1.0 / 64.0),
                             reads=[ssk + "_0", ssk + "_1", "epsb"], writes=[ssk + "_0", ssk + "_1"])
                        P.op("act", lambda e: e.activation(out=ssq[0:Cq, :], in_=ssq[0:Cq, :], func=AF.Exp, scale=-0.5), reads=[ssk + "_0", ssk + "_1"], writes=[ssk + "_0", ssk + "_1"])
                        yield
                        for hh in range(2):
                            P.op("dve", lambda e, hh=hh: e.tensor_scalar(out=osb[0:Cq, hh * 64:hh * 64 + 64], in0=osb[0:Cq, hh * 64:hh * 64 + 64], scalar1=ssq[0:Cq, hh:hh + 1], scalar2=None, op0=ALU.mult),
                                 reads=[osk + "_%d" % hh, ssk + "_%d" % hh], writes=[osk + "_%d" % hh])
                        yield
                        pbk = 0 if bi % 2 == 0 else 2
                        P.op("pe", lambda e: e.transpose(out=ps[pbk][:, 384:384 + Cq], in_=osb[0:Cq, :], identity=ident[0:Cq, 0:Cq]), reads=[osk + "_0", osk + "_1", "ident"], writes=[PSK[pbk]])
                        yield
                        P.op("act", lambda e: e.activation(out=ys_t[:, tq:tq + Cq], in_=ps[pbk][:, 384:384 + Cq], func=AF.Identity, scale=cc("sb_g", pr), bias=0.0),
                             reads=[PSK[pbk], "cst"], writes=[ysk])

                    def run_ilc(gens):
                        gens = list(gens)
                        while gens:
                            for g in list(gens):
                                try:
                                    next(g)
                                except StopIteration:
                                    gens.remove(g)

                    nblk_ = len(blocks)
                    for bi in range(0, nblk_, 2):
                        gens = [chead_gen(bi, 0, 0), chead_gen(bi, 1, 1)]
                        if bi + 1 < nblk_:
                            gens += [chead_gen(bi + 1, 0, 2), chead_gen(bi + 1, 1, 3)]
                        for pb in (bi - 2, bi - 1):
                            if pb >= 0:
                                gens.append(cpost_gen(pb))
                        run_ilc(gens)
                    last0 = ((nblk_ - 1) // 2) * 2
                    run_ilc([cpost_gen(pb) for pb in range(last0, nblk_)])
                    P.dma("sp", "ysst%d" % (pr % 2), lambda e, ys_t=ys_t, pr=pr: e.dma_start(out=YTD[1024 + pr * 128:1024 + (pr + 1) * 128, :], in_=ys_t[:]), reads=[ysk], writes=["YTD%d" % (8 + pr)])
            P.barrier(exclude=("cv",))

        if "D" in phases:
            with ExitStack() as sd:
                wo = sb("wo", [128, 16, D], BF16, sd)
                yt = sb("ytres", [128, 16, T], BF16, sd)
                g1B = sb("g1B", [128, D], F32, sd)
                b1B = sb("b1B", [128, D], F32, sd)
                P.dma("sp", "bc0", lambda e: e.dma_start(out=g1B[:], in_=bc_d[2]), writes=["g1B"])
                P.dma("sp", "bc1", lambda e: e.dma_start(out=b1B[:], in_=bc_d[3]), writes=["b1B"])
                for k in range(16):
                    P.dma("sp", "ytl%d" % k, lambda e, k=k: e.dma_start(out=yt[:, k, :], in_=YTD[k * 128:(k + 1) * 128, :]),
                          reads=["YTD%d" % k], writes=["yt%d" % k])
                for k0 in range(0, 16, 4):
                    src = w_out[k0 * 128:(k0 + 4) * 128, :].rearrange("(k p) n -> p k n", p=128)
                    P.dma("pool", "wo%d" % (k0 // 4), lambda e, k0=k0, src=src: e.dma_start(out=wo[:, k0:k0 + 4, :], in_=src),
                          writes=["wo%d" % k for k in range(k0, k0 + 4)])
                xts = [sb("dxt%d" % i, [128, D], F32, sd) for i in range(2)]
                hts = [sb("dht%d" % i, [128, 16, 128], BF16, sd) for i in range(2)]
                st = sb("dst", [128, 4, 6], F32, sd)
                mv = sb("dmv", [128, 2], F32, sd)
                rstd = sb("drstd", [128, 1], F32, sd)
                for ti, (t0, n) in enumerate(TT):
                    xt = xts[ti % 2]
                    xk = "dxt%d" % (ti % 2)
                    ht = hts[ti % 2]
                    hk = "dht%d" % (ti % 2)
                    P.dma("sp", "dx%d" % (ti % 2), lambda e, xt=xt, t0=t0, n=n: e.dma_start(out=xt[0:n, :], in_=H0[t0:t0 + n, :]),
                          reads=["H0_%d" % ti], writes=[xk])
                    for g in range(4):
                        for k in range(16):
                            P.op("pe", lambda e, g=g, k=k, t0=t0, n=n: e.matmul(
                                ps[g][0:n, :], lhsT=yt[:, k, t0:t0 + n], rhs=wo[:, k, g * 512:(g + 1) * 512],
                                start=(k == 0), stop=(k == 15)), reads=["yt%d" % k, "wo%d" % k], writes=[PSK[g]])
                        P.op("dve", lambda e, g=g, xt=xt, n=n: e.scalar_tensor_tensor(
                            out=xt[0:n, g * 512:(g + 1) * 512], in0=xt[0:n, g * 512:(g + 1) * 512], scalar=ALPHA,
                            in1=ps[g][0:n, :], op0=ALU.mult, op1=ALU.add), reads=[xk, PSK[g]], writes=[xk])
                    layer_norm_tile("pd", xt, n, st, mv, rstd, [xk])
                    P.op("pool", lambda e, xt=xt, n=n: e.tensor_tensor(out=xt[0:n, :], in0=xt[0:n, :], in1=g1B[0:n, :], op=ALU.mult),
                         reads=[xk, "g1B"], writes=[xk])
                    P.op("dve", lambda e, xt=xt, n=n: e.tensor_tensor(out=xt[0:n, :], in0=xt[0:n, :], in1=b1B[0:n, :], op=ALU.add),
                         reads=[xk, "b1B"], writes=[xk])
                    P.dma("sp", "h1st%d" % (ti % 2), lambda e, xt=xt, t0=t0, n=n: e.dma_start(out=H1[t0:t0 + n, :], in_=xt[0:n, :]),
                          reads=[xk], writes=["H1_%d" % ti])
                    for gi in range(4):
                        bank = 4 + gi % 2
                        for j in range(4):
                            c = gi * 4 + j
                            P.op("pe", lambda e, xt=xt, n=n, c=c, j=j, bank=bank: e.transpose(
                                out=ps[bank][:, j * 128:j * 128 + n], in_=xt[0:n, c * 128:(c + 1) * 128], identity=ident[0:n, 0:n]),
                                reads=[xk, "ident"], writes=[PSK[bank]])
                        copy_ps(evq(), ht[:, gi * 4:gi * 4 + 4, 0:n],
                                ps[bank][:, :].rearrange("p (j m) -> p j m", m=128)[:, :, 0:n], [PSK[bank]], [hk + "_%d" % gi])
                    for k0 in range(0, 16, 4):
                        dst = H1T[k0 * 128:(k0 + 4) * 128, t0:t0 + n].rearrange("(k p) t -> p k t", p=128)
                        P.dma("sp", "h1t%d_%d" % (ti % 2, k0 // 4), lambda e, ht=ht, k0=k0, n=n, dst=dst: e.dma_start(out=dst, in_=ht[:, k0:k0 + 4, 0:n]),
                              reads=[hk + "_%d" % (k0 // 4)], writes=["H1T_%d_%d" % (ti, k0)])
            P.barrier(exclude=("cv",))

        if "E" in phases:
            P.barrier()
            with ExitStack() as se:
                g2B = sb("g2B", [128, D], F32, se)
                b2B = sb("b2B", [128, D], F32, se)
                P.dma("sp", "bc0", lambda e: e.dma_start(out=g2B[:], in_=bc_d[4]), writes=["g2B"])
                P.dma("sp", "bc1", lambda e: e.dma_start(out=b2B[:], in_=bc_d[5]), writes=["b2B"])
                NTOK = 528
                hs = sb("ehs", [128, 16, NTOK], BF16, se)
                acc = sb("eacc", [128, 5, D], F32, se)
                carry = sb("ecarry", [128, 44, 2], F32, se)
                P.op("pool", lambda e: e.memset(carry[:].rearrange("p a b -> p (a b)"), 0.0), writes=["carry%d" % i for i in range(44)])
                wus = [sb("ewu%d" % i, [128, 16, 1024], BF16, se) for i in range(2)]
                wds = [sb("ewd%d" % i, [128, 4, D], BF16, se) for i in range(2)]
                gs = [sb("egs%d" % i, [128, NTOK + 2], F32, se) for i in range(2)]
                vs = [sb("evs%d" % i, [128, NTOK], F32, se) for i in range(2)]
                tb = [sb("etb%d" % i, [128, NTOK], F32, se) for i in range(2)]
                ab = [sb("eab%d" % i, [128, 4, NTOK], BF16, se) for i in range(2)]
                st = sb("est", [128, 4, 6], F32, se)
                mv = sb("emv", [128, 2], F32, se)
                rstd = sb("erstd", [128, 1], F32, se)
                STS = [(0, [(0, 16), (16, 512)], TT[0:5])] + [(16 + 512 * i, [(16 + 512 * i, 512)], TT[1 + 4 * i:5 + 4 * i]) for i in range(1, 4)]
                blk = 0
                cidx = 0
                dbank = 0
                for sti, (ts, groups, tiles) in enumerate(STS[:dbg.get("nst", 4)]):
                    ntok = sum(n for _, n in groups)
                    for k0 in range(0, 16, 4):
                        src = H1T[k0 * 128:(k0 + 4) * 128, ts:ts + ntok].rearrange("(k p) t -> p k t", p=128)
                        P.dma("sp", "ehs", lambda e, k0=k0, src=src, ntok=ntok: e.dma_start(out=hs[:, k0:k0 + 4, 0:ntok], in_=src),
                              reads=["H1T_%d_%d" % (TT.index(tl), k0) for tl in tiles], writes=["hs"])
                    for li, (t0, n) in enumerate(tiles):
                        P.dma("sp", "eacc%d" % li, lambda e, li=li, t0=t0, n=n: e.dma_start(out=acc[0:n, li, :], in_=H1[t0:t0 + n, :]),
                              reads=["H1_%d" % (TT.index((t0, n)))], writes=["acc%d" % li])
                        P.op("pool", lambda e, li=li, n=n: e.tensor_scalar(out=acc[0:n, li, :], in0=acc[0:n, li, :], scalar1=ALPHA, scalar2=None, op0=ALU.mult),
                             reads=["acc%d" % li], writes=["acc%d" % li])
                    for fb in range(dbg.get("nfb", 11)):
                        wu = wus[blk % 2]; wuk = "ewu%d" % (blk % 2)
                        wd = wds[blk % 2]; wdk = "ewd%d" % (blk % 2)
                        abt = ab[blk % 2]; abk = "eab%d" % (blk % 2)
                        P.dma("sp", "wu%d" % (blk % 2), lambda e, wu=wu, fb=fb: e.dma_start(out=wu[:, :, :], in_=WUB[fb]), reads=["WUB%d" % fb], writes=[wuk])
                        P.dma("sp", "wd%d" % (blk % 2), lambda e, wd=wd, fb=fb: e.dma_start(out=wd[:, :, :], in_=WDB[fb]), reads=["WDB%d" % fb], writes=[wdk])
                        blk += 1
                        for c in range(4):
                            fc = fb * 4 + c
                            g_ = gs[cidx % 2]; gk = "egs%d" % (cidx % 2)
                            v_ = vs[cidx % 2]; vk = "evs%d" % (cidx % 2)
                            t_ = tb[cidx % 2]; tk = "etb%d" % (cidx % 2)
                            cidx += 1
                            P.op("pool", lambda e, g_=g_, fc=fc: e.tensor_copy(out=g_[:, 0:2], in_=carry[:, fc, :]), reads=["carry%d" % fc], writes=[gk + "c"])
                            off = 0
                            gkeys = []
                            vkeys = []
                            for qi, (t0, n) in enumerate(groups):
                                lo = t0 - ts
                                for k in range(16):
                                    P.op("pe", lambda e, wu=wu, k=k, c=c, lo=lo, n=n: e.matmul(
                                        ps[0][:, 0:n], lhsT=wu[:, k, c * 128:(c + 1) * 128], rhs=hs[:, k, lo:lo + n],
                                        start=(k == 0), stop=(k == 15)), reads=[wuk, "hs"], writes=[PSK[0]])
                                P.op("act", lambda e, g_=g_, lo=lo, n=n: e.activation(out=g_[:, 2 + lo:2 + lo + n], in_=ps[0][:, 0:n], func=AF.Copy),
                                     reads=[PSK[0]], writes=[gk + "_%d" % qi])
                                gkeys.append(gk + "_%d" % qi)
                                for k in range(16):
                                    P.op("pe", lambda e, wu=wu, k=k, c=c, lo=lo, n=n: e.matmul(
                                        ps[1][:, 0:n], lhsT=wu[:, k, 512 + c * 128:512 + (c + 1) * 128], rhs=hs[:, k, lo:lo + n],
                                        start=(k == 0), stop=(k == 15)), reads=[wuk, "hs"], writes=[PSK[1]])
                                P.op("dve", lambda e, v_=v_, lo=lo, n=n: e.tensor_copy(out=v_[:, lo:lo + n], in_=ps[1][:, 0:n]),
                                     reads=[PSK[1]], writes=[vk + "_%d" % qi])
                                vkeys.append(vk + "_%d" % qi)
                            gall = gkeys + [gk + "c"]
                            P.op("act", lambda e, g_=g_, t_=t_, fc=fc, ntok=ntok: e.activation(
                                out=t_[:, 0:ntok], in_=g_[:, 2:2 + ntok], func=AF.Identity, scale=cc("cw2", fc), bias=cc("cb", fc)),
                                reads=gall + ["cst"], writes=[tk])
                            P.op("dve", lambda e, g_=g_, t_=t_, fc=fc, ntok=ntok: e.scalar_tensor_tensor(
                                out=t_[:, 0:ntok], in0=g_[:, 1:1 + ntok], scalar=cc("cw1", fc), in1=t_[:, 0:ntok], op0=ALU.mult, op1=ALU.add),
                                reads=gall + [tk, "cst"], writes=[tk])
                            P.op("dve", lambda e, g_=g_, t_=t_, fc=fc, ntok=ntok: e.scalar_tensor_tensor(
                                out=t_[:, 0:ntok], in0=g_[:, 0:ntok], scalar=cc("cw0", fc), in1=t_[:, 0:ntok], op0=ALU.mult, op1=ALU.add),
                                reads=gall + [tk, "cst"], writes=[tk])
                            P.op("pool", lambda e, g_=g_, fc=fc, ntok=ntok: e.tensor_copy(out=carry[:, fc, :], in_=g_[:, ntok:ntok + 2]),
                                 reads=gall, writes=["carry%d" % fc])
                            P.op("act", lambda e, t_=t_, ntok=ntok: e.activation(out=t_[:, 0:ntok], in_=t_[:, 0:ntok], func=AF.Silu),
                                 reads=[tk], writes=[tk])
                            P.op("pool", lambda e, t_=t_, v_=v_, abt=abt, c=c, ntok=ntok: e.tensor_tensor(
                                out=abt[:, c, 0:ntok], in0=t_[:, 0:ntok], in1=v_[:, 0:ntok], op=ALU.mult),
                                reads=[tk] + vkeys, writes=[abk + "_%d" % c])
                        for li, (t0, n) in enumerate(tiles):
                            lo = t0 - ts
                            for g in range(4):
                                bank = 4 + dbank % 4
                                dbank += 1
                                for c in range(4):
                                    P.op("pe", lambda e, abt=abt, wd=wd, c=c, lo=lo, n=n, g=g, bank=bank: e.matmul(
                                        ps[bank][0:n, :], lhsT=abt[:, c, lo:lo + n], rhs=wd[:, c, g * 512:(g + 1) * 512],
                                        start=(c == 0), stop=(c == 3)), reads=[abk + "_%d" % c, wdk], writes=[PSK[bank]])
                                P.op("dve", lambda e, li=li, n=n, g=g, bank=bank: e.tensor_tensor(
                                    out=acc[0:n, li, g * 512:(g + 1) * 512], in0=acc[0:n, li, g * 512:(g + 1) * 512], in1=ps[bank][0:n, :], op=ALU.add),
                                    reads=["acc%d" % li, PSK[bank]], writes=["acc%d" % li])
                    for li, (t0, n) in enumerate(tiles):
                        if t0 < NMETA:
                            continue
                        at = acc[:, li, :]
                        layer_norm_tile("pe", at, n, st, mv, rstd, ["acc%d" % li])
                        P.op("pool", lambda e, at=at, n=n: e.tensor_tensor(out=at[0:n, :], in0=at[0:n, :], in1=g2B[0:n, :], op=ALU.mult),
                             reads=["acc%d" % li, "g2B"], writes=["acc%d" % li])
                        P.op("dve", lambda e, at=at, n=n: e.tensor_tensor(out=at[0:n, :], in0=at[0:n, :], in1=b2B[0:n, :], op=ALU.add),
                             reads=["acc%d" % li, "b2B"], writes=["acc%d" % li])
                        P.dma("sp", "ost%d" % li, lambda e, at=at, t0=t0, n=n: e.dma_start(out=out_d[t0 - NMETA:t0 - NMETA + n, :], in_=at[0:n, :]),
                              reads=["acc%d" % li], writes=["out%d" % t0])
            P.barrier(exclude=("cv",))

        P.emit(final_wait_chans=[c for c in P.chan_order])
    nc.used_inputs = used_inputs
    nc.prog_stats = {e: len(v) for e, v in P.ops.items()}
    return nc


def _pc(v, n):
    return np.ascontiguousarray(np.asarray(v, np.float32).reshape(n, 128).T)


def host_consts(inp):
    cst = np.zeros((128, NCST), np.float32)

    def put(name, arr):
        cst[:, CST[name]:CST[name] + arr.shape[1]] = arr

    put("emb_g", _pc(inp["emb_ln_g"], 16)); put("emb_b", _pc(inp["emb_ln_b"], 16))
    put("ln1_g", _pc(inp["ln1_g"][0], 16)); put("ln1_b", _pc(inp["ln1_b"][0], 16))
    put("ln2_g", _pc(inp["ln2_g"][0], 16)); put("ln2_b", _pc(inp["ln2_b"][0], 16))
    mu = np.zeros(27 * 128, np.float32)
    mu[:3360] = inp["rwkv_mu"][0]
    put("mu", _pc(mu, 27))
    for nm, key in [("w0", "rwkv_w0"), ("a0", "rwkv_a0"), ("k_k", "rwkv_k_k"), ("k_a", "rwkv_k_a"),
                    ("gn_g", "rwkv_gn_g"), ("gn_b", "rwkv_gn_b"), ("sb_g", "sb_norm_g")]:
        put(nm, _pc(inp[key][0], 8))
    put("r_k", _pc(inp["rwkv_r_k"][0].reshape(-1), 8))
    cw = inp["ffn_conv_w"][0]
    put("cw0", _pc(cw[0], 44)); put("cw1", _pc(cw[1], 44)); put("cw2", _pc(cw[2], 44))
    put("cb", _pc(inp["ffn_conv_b"][0], 44))
    bc = np.stack([np.broadcast_to(np.asarray(v, np.float32)[None, :], (128, D)) for v in
                   [inp["emb_ln_g"], inp["emb_ln_b"], inp["ln1_g"][0], inp["ln1_b"][0], inp["ln2_g"][0], inp["ln2_b"][0]]])
    return cst, np.ascontiguousarray(bc)


def make_in_maps(inp):
    inp = {k: np.asarray(v) for k, v in inp.items()}
    cst, bc = host_consts(inp)
    shared = dict(cst=cst, bc=bc,
                  w_in=np.ascontiguousarray(inp["w_in"][0], dtype=np.float32),
                  w2a2=np.ascontiguousarray(np.concatenate([inp["rwkv_w2"][0], inp["rwkv_a2"][0]], 0), dtype=np.float32),
                  g2=np.ascontiguousarray(inp["rwkv_g2"][0], dtype=np.float32),
                  w_out=np.ascontiguousarray(inp["w_out"][0], dtype=np.float32),
                  w_up=np.ascontiguousarray(inp["ffn_w_up"][0], dtype=np.float32),
                  w_dn=np.ascontiguousarray(inp["ffn_w_down"][0], dtype=np.float32))
    maps = []
    for b in range(NCORES):
        m = dict(shared)
        m["xcat"] = np.ascontiguousarray(np.concatenate([inp["meta_tokens"], inp["x"][b]], 0), dtype=np.float32)
        maps.append(m)
    return maps


_NC_CACHE = {}


def kernel(**inputs):
    if "nc" not in _NC_CACHE:
        _NC_CACHE["nc"] = build_nc()
    nc = _NC_CACHE["nc"]
    maps = make_in_maps(inputs)
    res = run_bass_kernel_spmd(nc, maps, core_ids=list(range(NCORES)))
    return np.stack([np.asarray(r["out"], np.float32) for r in res.results], 0)
```
